# Optimizing a Trainium2 kernel written in Bass

```python
import jax, jax.numpy as jnp
from jax import lax
import numpy as np

D_MODEL = 1024
BATCH = 2
SEQ = 8192
DEPTH = 2

HEAD_DIM = 128
HEADS_PER_GROUP = 4
DILATION_GROUPS = ((128, 1), (512, 4), (2048, 16))
N_GROUPS = 3
ATT_WIDTH = HEADS_PER_GROUP * HEAD_DIM
QKV_WIDTH = N_GROUPS * ATT_WIDTH
BAND_BLOCK = 128
LRU_WIDTH = D_MODEL
LRU_BLOCKS = 16
LRU_BLOCK_DIM = LRU_WIDTH // LRU_BLOCKS
CONV_WIDTH = 4
LRU_C = 8.0
N_BRANCHES = 2
NORM_EPS = 1e-6
NEG_INF = -1e30
SPLIT_SIZES = (QKV_WIDTH, QKV_WIDTH, QKV_WIDTH, ATT_WIDTH, LRU_WIDTH, LRU_WIDTH, N_BRANCHES * D_MODEL)
IN_WIDTH = 3 * QKV_WIDTH + ATT_WIDTH + 2 * LRU_WIDTH + N_BRANCHES * D_MODEL

kernel_name = "hybrid_dilated_attn_rglru_block"


def rms_norm(x, gain):
    xf = x.astype(jnp.float32)
    y = xf * lax.rsqrt(jnp.mean(xf * xf, axis=-1, keepdims=True) + NORM_EPS)
    return (y * gain.astype(jnp.float32)).astype(x.dtype)


def dilated_window_group(q, k, v, window, dilation):
    B, S, H, Dh = q.shape
    span = window // dilation
    L = S // dilation
    n_blk = -(-L // BAND_BLOCK)
    Lp = n_blk * BAND_BLOCK

    def to_sub(t):
        t = t.astype(jnp.float32).reshape(B, L, dilation, H, Dh).transpose(0, 2, 1, 3, 4)
        return jnp.pad(t, ((0, 0), (0, 0), (0, Lp - L), (0, 0), (0, 0)))

    def band(t):
        t = jnp.pad(to_sub(t), ((0, 0), (0, 0), (BAND_BLOCK, 0), (0, 0), (0, 0)))
        t = t.reshape(B, dilation, n_blk + 1, BAND_BLOCK, H, Dh)
        return jnp.concatenate([t[:, :, :-1], t[:, :, 1:]], axis=3)

    qs = to_sub(q).reshape(B, dilation, n_blk, BAND_BLOCK, H, Dh)
    kb = band(k)
    vb = band(v)
    s = jnp.einsum('brnqhd,brnkhd->brnhqk', qs, kb) * (Dh ** -0.5)
    qi = jnp.arange(BAND_BLOCK)[:, None] + BAND_BLOCK
    ki = jnp.arange(2 * BAND_BLOCK)[None, :]
    dist = qi - ki
    key_pos = jnp.arange(n_blk)[:, None, None] * BAND_BLOCK + ki[None] - BAND_BLOCK
    valid = (dist >= 0) & (dist <= span) & (key_pos >= 0)
    s = jnp.where(valid[None, None, :, None], s, NEG_INF)
    m = jnp.max(s, axis=-1)
    p = jnp.exp(s - m[..., None])
    l = jnp.sum(p, axis=-1)
    m = jnp.swapaxes(m, -1, -2)
    l = jnp.swapaxes(l, -1, -2)
    o = jnp.einsum('brnhqk,brnkhd->brnqhd', p, vb) / l[..., None]

    def from_sub(t):
        t = t.reshape((B, dilation, Lp) + t.shape[4:])[:, :, :L]
        t = jnp.swapaxes(t, 1, 2)
        return t.reshape((B, S) + t.shape[3:])

    return from_sub(o), from_sub(m), from_sub(l)


def dilated_attention(q, k, v):
    B, S, _ = q.shape
    shp = (B, S, N_GROUPS, HEADS_PER_GROUP, HEAD_DIM)
    qg, kg, vg = q.reshape(shp), k.reshape(shp), v.reshape(shp)
    outs, maxes, dens = [], [], []
    for g, (window, dilation) in enumerate(DILATION_GROUPS):
        o, m, l = dilated_window_group(qg[:, :, g], kg[:, :, g], vg[:, :, g], window, dilation)
        outs.append(o)
        maxes.append(m)
        dens.append(l)
    o = jnp.stack(outs, 0)
    m = jnp.stack(maxes, 0)
    l = jnp.stack(dens, 0)
    wgt = l * jnp.exp(m - jnp.max(m, axis=0, keepdims=True))
    o = jnp.sum(wgt[..., None] * o, axis=0) / jnp.sum(wgt, axis=0)[..., None]
    return o.reshape(B, S, ATT_WIDTH).astype(q.dtype)


def causal_depthwise_conv(u, w, b):
    S = u.shape[1]
    up = jnp.pad(u, ((0, 0), (CONV_WIDTH - 1, 0), (0, 0)))
    y = b
    for j in range(CONV_WIDTH):
        y = y + up[:, CONV_WIDTH - 1 - j: CONV_WIDTH - 1 - j + S] * w[j]
    return y


def rg_lru(u, w_rg, b_rg, w_ig, b_ig, lru_lambda):
    B, S, _ = u.shape
    uf = u.astype(jnp.float32)
    ub = uf.reshape(B, S, LRU_BLOCKS, LRU_BLOCK_DIM)
    r = jax.nn.sigmoid(jnp.einsum('bshi,hij->bshj', ub, w_rg.astype(jnp.float32)).reshape(B, S, LRU_WIDTH) + b_rg.astype(jnp.float32))
    i = jax.nn.sigmoid(jnp.einsum('bshi,hij->bshj', ub, w_ig.astype(jnp.float32)).reshape(B, S, LRU_WIDTH) + b_ig.astype(jnp.float32))
    log_a = -LRU_C * r * jax.nn.softplus(-lru_lambda.astype(jnp.float32))
    a = jnp.exp(log_a)
    xin = jnp.sqrt(-jnp.expm1(2.0 * log_a)) * (i * uf)

    def combine(left, right):
        a1, b1 = left
        a2, b2 = right
        return a1 * a2, a2 * b1 + b2

    _, h = lax.associative_scan(combine, (a, xin), axis=1)
    return h.astype(u.dtype)


def hybrid_layer(x, c, w_mod, b_mod, g_pre, w_in, conv_w, conv_b, w_rg, b_rg, w_ig, b_ig,
                 lru_lambda, w_pa, w_pb, w_o, g_post):
    mod = jax.nn.silu(c) @ w_mod + b_mod
    shift, scale, gate = jnp.split(mod[:, None, :], 3, axis=-1)
    h = rms_norm(x, g_pre) * (1 + scale) + shift
    proj = h @ w_in
    split_at = [int(v) for v in np.cumsum(SPLIT_SIZES)[:-1]]
    q, k, v, g_att, u, g_lru, merge = jnp.split(proj, split_at, axis=-1)
    y_a = (dilated_attention(q, k, v) * jax.nn.silu(g_att)) @ w_pa
    u = causal_depthwise_conv(u, conv_w, conv_b)
    y_b = (rg_lru(u, w_rg, b_rg, w_ig, b_ig, lru_lambda) * jax.nn.silu(g_lru)) @ w_pb
    gate_a, gate_b = jnp.split(jax.nn.sigmoid(merge), N_BRANCHES, axis=-1)
    out = (gate_a * y_a + gate_b * y_b) @ w_o
    return x + gate * rms_norm(out, g_post)


def setup_inputs(seed: int = 0) -> dict:
    key = jax.random.key(seed)
    ks = jax.random.split(key, 20)
    f32 = jnp.float32
    nrm = lambda k, shape, s: jax.random.normal(k, shape, f32) * s
    u = jax.random.uniform(ks[13], (DEPTH, LRU_WIDTH), f32, 0.9, 0.999)
    a_base = u ** (1.0 / LRU_C)
    lru_lambda = jnp.log(a_base) - jnp.log1p(-a_base)
    return {
        "x": nrm(ks[0], (BATCH, SEQ, D_MODEL), 1.0),
        "c": nrm(ks[1], (BATCH, D_MODEL), 1.0),
        "w_mod": nrm(ks[2], (DEPTH, D_MODEL, 3 * D_MODEL), 0.5 * D_MODEL ** -0.5),
        "b_mod": nrm(ks[3], (DEPTH, 3 * D_MODEL), 0.01),
        "g_pre": 1.0 + nrm(ks[4], (DEPTH, D_MODEL), 0.05),
        "w_in": nrm(ks[5], (DEPTH, D_MODEL, IN_WIDTH), D_MODEL ** -0.5),
        "conv_w": nrm(ks[6], (DEPTH, CONV_WIDTH, LRU_WIDTH), CONV_WIDTH ** -0.5),
        "conv_b": nrm(ks[7], (DEPTH, LRU_WIDTH), 0.01),
        "w_rg": nrm(ks[8], (DEPTH, LRU_BLOCKS, LRU_BLOCK_DIM, LRU_BLOCK_DIM), LRU_BLOCK_DIM ** -0.5),
        "b_rg": nrm(ks[9], (DEPTH, LRU_WIDTH), 0.01),
        "w_ig": nrm(ks[10], (DEPTH, LRU_BLOCKS, LRU_BLOCK_DIM, LRU_BLOCK_DIM), LRU_BLOCK_DIM ** -0.5),
        "b_ig": nrm(ks[11], (DEPTH, LRU_WIDTH), 0.01),
        "lru_lambda": lru_lambda,
        "w_pa": nrm(ks[14], (DEPTH, ATT_WIDTH, D_MODEL), ATT_WIDTH ** -0.5),
        "w_pb": nrm(ks[15], (DEPTH, LRU_WIDTH, D_MODEL), LRU_WIDTH ** -0.5),
        "w_o": nrm(ks[16], (DEPTH, D_MODEL, D_MODEL), D_MODEL ** -0.5),
        "g_post": 1.0 + nrm(ks[17], (DEPTH, D_MODEL), 0.05),
    }


def reference(x, c, w_mod, b_mod, g_pre, w_in, conv_w, conv_b, w_rg, b_rg, w_ig, b_ig,
              lru_lambda, w_pa, w_pb, w_o, g_post):
    for layer in range(DEPTH):
        x = hybrid_layer(x, c, w_mod[layer], b_mod[layer], g_pre[layer], w_in[layer],
                         conv_w[layer], conv_b[layer], w_rg[layer], b_rg[layer],
                         w_ig[layer], b_ig[layer], lru_lambda[layer], w_pa[layer],
                         w_pb[layer], w_o[layer], g_post[layer])
    return x
```

```python
import contextlib
import os
import numpy as np
import concourse.bass as bass
import concourse.mybir as mybir
from concourse.bass_utils import run_bass_kernel_spmd

F32 = mybir.dt.float32
BF16 = mybir.dt.bfloat16
I32 = mybir.dt.int32
ALU = mybir.AluOpType
AF = mybir.ActivationFunctionType

D_MODEL = 1024
T = 2048
DEPTH = 2
NV = 104
EPS = 1e-6
ARENA_BYTES = 186 * 1024
GROUP_DIL = (1, 4, 16)

V_BMOD, V_GPRE, V_CONVW, V_CONVB, V_BRG, V_BIG, V_LAM, V_GPOST = 0, 24, 32, 64, 72, 80, 88, 96


class Op:
    __slots__ = ("eng", "fn", "deps", "dma", "idx", "signal", "sigval", "sem", "semval", "inc")

    def __init__(self, eng, fn, dma):
        self.eng, self.fn, self.dma = eng, fn, dma
        self.deps = []
        self.signal = False
        self.sigval = None
        self.sem = None
        self.semval = None
        self.inc = 16


ENGS = ("pe", "act", "dve", "pool", "sp")


class Prog:
    def __init__(self):
        self.ops = {e: [] for e in ENGS}
        self.lastw = {}
        self.readers = {}
        self.dmas_since_barrier = []

    def op(self, eng, fn, r=(), w=(), dma=False, inc=16):
        o = Op(eng, fn, dma)
        o.inc = inc
        deps = []
        for k in r:
            lw = self.lastw.get(k)
            if lw is not None:
                deps.append(lw)
        for k in w:
            lw = self.lastw.get(k)
            if lw is not None:
                deps.append(lw)
            deps.extend(self.readers.get(k, ()))
        seen = set()
        for d in deps:
            if id(d) in seen or d is o:
                continue
            seen.add(id(d))
            if (not d.dma) and (not dma) and d.eng == eng and eng == "pe":
                continue
            o.deps.append(d)
            d.signal = True
        for k in r:
            lst = self.readers.setdefault(k, [])
            if not dma:
                lst[:] = [x_ for x_ in lst if x_.dma or x_.eng != eng]
            lst.append(o)
        for k in w:
            self.lastw[k] = o
            self.readers[k] = []
        self.ops[eng].append(o)
        if dma:
            o.signal = True
            self.dmas_since_barrier.append(o)
        return o

    def barrier(self):
        lasts = []
        for e in ENGS:
            for o in reversed(self.ops[e]):
                if o.fn is not None and not o.dma:
                    lasts.append(o)
                    break
        deps = lasts + self.dmas_since_barrier
        self.dmas_since_barrier = []
        for e in ENGS:
            b = Op(e, None, False)
            for d in deps:
                if (not d.dma) and d.eng == e:
                    continue
                b.deps.append(d)
                d.signal = True
            self.ops[e].append(b)

    def emit(self, nc, block, sems, dma_sems):
        tot = {}
        ncc = 0
        for e in ENGS:
            cnt = 0
            nd = 0
            for o in self.ops[e]:
                if o.fn is None:
                    continue
                if o.dma:
                    if o.inc == 16:
                        pool = dma_sems[e]
                        o.sem = pool[nd % len(pool)]
                        nd += 1
                    else:
                        pool = dma_sems["cc"]
                        o.sem = pool[ncc % len(pool)]
                        ncc += 1
                    prev = tot.get(id(o.sem), 0)
                    o.sigval = prev
                    o.semval = prev + o.inc
                    tot[id(o.sem)] = o.semval
                elif o.signal:
                    cnt += 1
                    o.sigval = cnt
        prog = self

        def run_engine(e, engobj):
            known = {x: 0 for x in ENGS}
            known_dma = {}
            nd = 0
            for o in prog.ops[e]:
                for d in o.deps:
                    if d.dma:
                        key = id(d.sem)
                        if known_dma.get(key, 0) >= d.semval:
                            continue
                        engobj.wait_ge(d.sem, d.semval)
                        known_dma[key] = d.semval
                    else:
                        if known[d.eng] >= d.sigval:
                            continue
                        engobj.wait_ge(sems[d.eng], d.sigval)
                        known[d.eng] = d.sigval
                if o.fn is None:
                    continue
                if o.dma:
                    pool = dma_sems[e]
                    prev = o.sigval
                    key = id(o.sem)
                    if prev > 0 and known_dma.get(key, 0) < prev:
                        engobj.wait_ge(o.sem, prev)
                        known_dma[key] = prev
                    nd += 1
                    ins = o.fn(engobj)
                    ins.then_inc(o.sem, o.inc)
                else:
                    ins = o.fn(engobj)
                    if o.signal:
                        ins.then_inc(sems[e], 1)

        @block.tensor
        def _(eng):
            run_engine("pe", eng)

        @block.scalar
        def _(eng):
            run_engine("act", eng)

        @block.vector
        def _(eng):
            run_engine("dve", eng)

        @block.gpsimd
        def _(eng):
            run_engine("pool", eng)

        @block.sync
        def _(eng):
            run_engine("sp", eng)


class Arena:
    def __init__(self, ap_bf16):
        self.ap = ap_bf16
        self.off = 0

    def reset(self, off):
        self.off = off

    def alloc(self, shape, dtype):
        esz = 4 if dtype in (F32, I32) else 2
        n = int(np.prod(shape[1:]))
        nbytes = n * esz
        self.off = (self.off + 63) // 64 * 64
        start = self.off
        self.off += nbytes
        assert self.off <= ARENA_BYTES, f"arena overflow {self.off}"
        v = self.ap[:, start // 2:(start + nbytes) // 2]
        if esz == 4:
            v = v.bitcast(dtype)
        if len(shape) == 3:
            v = v.rearrange("p (a b) -> p a b", a=shape[1])
        return v


class _Stop(Exception):
    pass


def build_nc(stop=None, debug=False):
    nc = bass.Bass("TRN2", target_bir_lowering=False)
    P = Prog()

    def din(name, shape, dt):
        return nc.dram_tensor(name, shape, dt, kind="ExternalInput").ap()

    x_in = din("x", [T, D_MODEL], F32)
    cT_in = din("cT", [128, 8], F32)
    vecs_in = din("vecs", [128, DEPTH * NV], F32)
    wmod_in = din("w_mod", [DEPTH, D_MODEL, 3 * D_MODEL], F32)
    win_in = din("w_in", [DEPTH, D_MODEL, 9216], F32)
    wg_in = din("wg", [128, DEPTH * 16 * 128], F32)
    wpa_in = din("w_pa", [DEPTH, 512, D_MODEL], F32)
    wpb_in = din("w_pb", [DEPTH, D_MODEL, D_MODEL], F32)
    wo_in = din("w_o", [DEPTH, D_MODEL, D_MODEL], F32)
    consts_in = din("consts", [128, 128 + 256 + 256 + 256], F32)
    idx_in = din("idx", [128, 16], I32)
    flags_in = din("flags", [128, 8], F32)
    out = nc.dram_tensor("out", [T, D_MODEL], F32, kind="ExternalOutput").ap()

    def dscr(name, shape, dt):
        if debug and not name.startswith(("pkg", "gath")):
            return nc.dram_tensor(name, shape, dt, kind="ExternalOutput").ap()
        return nc.dram_tensor(name, shape, dt).ap()

    xT_s = dscr("xT_s", [8, 128, T], F32)
    QT_s = dscr("QT_s", [12, 128, T], BF16)
    KT_s = dscr("KT_s", [12, 128, T], BF16)
    Vs = dscr("Vs", [3, 16, 128, 512], BF16)
    GA_s = dscr("GA_s", [4, 128, T], BF16)
    UT_s = dscr("UT_s", [8, 128, T], F32)
    GL_s = dscr("GL_s", [8, 128, T], BF16)
    MG_s = dscr("MG_s", [16, 128, T], BF16)
    A_s = dscr("A_s", [8, 128, T], F32)
    X_s = dscr("X_s", [8, 128, T], F32)
    pkg = dscr("pkg", [12 * 128, 2048], BF16)
    gaths = [dscr(f"gathkv{i}", [4 * 256, 2048], BF16) for i in range(6)]
    pkg2 = dscr("pkg2", [128, 24], F32)
    gath2 = dscr("gath2", [4 * 128, 24], F32)
    pkg3 = dscr("pkg3", [128, 16], F32)
    gath3 = dscr("gath3", [4 * 128, 16], F32)
    RG = [[0, 1, 2, 3], [4, 5, 6, 7]]
    if debug:
        AT_d = dscr("AT_d", [128, 4, T], BF16)
        BT_d = dscr("BT_d", [128, 8, T], BF16)
        hT_d = dscr("hT_d", [128, 8, T], BF16)
        sm_d = dscr("sm_d", [128, 64 + 40 + 24], F32)

    es = contextlib.ExitStack()
    with es:
        def sb(name, shape, dt):
            return es.enter_context(nc.sbuf_tensor(name, shape, dt))

        arena_t = sb("arena", [128, ARENA_BYTES // 2], BF16)
        AR = Arena(arena_t[:, :])
        ident = sb("ident_sb", [128, 128], F32)
        consts_bf = sb("consts_bf", [128, 256 + 512], BF16)
        ones_bf = consts_bf[:, 0:128]
        ident_bf = consts_bf[:, 128:256]
        maskPC = consts_bf[:, 256:512]
        maskF = consts_bf[:, 512:768]
        vecs = sb("vecs_sb", [128, DEPTH * NV], F32)
        cT = sb("cTs", [128, 8], F32)
        siluc = sb("siluc", [128, 8], BF16)
        idx = sb("idxs", [128, 16], I32)
        flags = sb("flagss", [128, 8], F32)
        modv_all = sb("modv", [128, 48], F32)
        derived = sb("derived", [128, 96], F32)
        wg_bf = sb("wg_bf", [128, DEPTH * 16 * 128], BF16)
        lru_sm = sb("lru_sm", [128, 64], F32)
        g3sb = sb("g3sb", [128, 4 * 16], F32)
        onep = sb("onep", [128, 1], F32)
        psum_all = es.enter_context(nc.psum_tensor("psall", [128, 8, 512], F32))
        psums = [psum_all[:, i, :] for i in range(8)]
        sems = {e: es.enter_context(nc.semaphore(f"sem_{e}")) for e in ENGS}
        dma_sems = {e: [es.enter_context(nc.semaphore(f"dsem_{e}{i}")) for i in range(12)]
                    for e in ("sp", "pool", "act")}
        dma_sems["cc"] = [es.enter_context(nc.semaphore(f"ccsem{i}")) for i in range(4)]
        dma_sems["pe"] = dma_sems["sp"]
        dma_sems["dve"] = dma_sems["sp"]
        block = es.enter_context(nc.Block())

        def dma(q, out_ap, in_ap, r, w):
            return P.op(q, lambda e, o=out_ap, i=in_ap: e.dma_start(out=o, in_=i), r=r, w=w, dma=True)

        def mm(out_ap, lhsT, rhs, start, stop, r, w):
            return P.op("pe", lambda e, o=out_ap, l=lhsT, rr=rhs, s=start, t=stop:
                        e.matmul(o, lhsT=l, rhs=rr, start=s, stop=t), r=r, w=w)

        def act(out_ap, in_ap, func, r, w, bias=None, scale=None):
            kw = {}
            if bias is not None:
                kw["bias"] = bias
            if scale is not None:
                kw["scale"] = scale
            return P.op("act", lambda e, o=out_ap, i=in_ap, f=func, kw=kw:
                        e.activation(out=o, in_=i, func=f, **kw), r=r, w=w)

        def tt(eng, out_ap, in0, in1, op, r, w):
            return P.op(eng, lambda e, o=out_ap, a=in0, b=in1, p=op:
                        e.tensor_tensor(out=o, in0=a, in1=b, op=p), r=r, w=w)

        def ts(eng, out_ap, in0, s1, s2, op0, op1, r, w):
            if op1 is None:
                return P.op(eng, lambda e, o=out_ap, a=in0, x=s1, p0=op0:
                            e.tensor_scalar(out=o, in0=a, scalar1=x, scalar2=None, op0=p0), r=r, w=w)
            return P.op(eng, lambda e, o=out_ap, a=in0, x=s1, y=s2, p0=op0, p1=op1:
                        e.tensor_scalar(out=o, in0=a, scalar1=x, scalar2=y, op0=p0, op1=p1), r=r, w=w)

        def stt(out_ap, in0, scalar, in1, op0, op1, r, w):
            return P.op("dve", lambda e, o=out_ap, a=in0, s=scalar, b=in1, p0=op0, p1=op1:
                        e.scalar_tensor_tensor(out=o, in0=a, scalar=s, in1=b, op0=p0, op1=p1), r=r, w=w)

        def cp(eng, out_ap, in_ap, r, w):
            return P.op(eng, lambda e, o=out_ap, i=in_ap: e.tensor_copy(out=o, in_=i), r=r, w=w)

        def gather(out_ap, src, col, r, w):
            return P.op("pool", lambda e, o=out_ap, s=src, c=col: e.indirect_dma_start(
                out=o, out_offset=None, in_=s,
                in_offset=bass.IndirectOffsetOnAxis(ap=idx[:, c:c + 1], axis=0)), r=r, w=w, dma=True)

        def allgather(src, dst, r, w):
            return P.op("pool", lambda e, s=src, d=dst: e.collective_compute(
                "AllGather", ALU.bypass, replica_groups=RG, ins=[s], outs=[d]), r=r, w=w, dma=True, inc=1)

        try:
            dma("sp", ident[:, :], consts_in[:, 0:128], r=[], w=["ident"])
            dma("pool", consts_bf[:, :], consts_in[:, 128:896], r=[], w=["cbf"])
            dma("sp", vecs[:, :], vecs_in, r=[], w=["vecs"])
            dma("sp", cT[:, :], cT_in, r=[], w=["cT"])
            dma("sp", idx[:, :], idx_in, r=[], w=["idx"])
            dma("sp", flags[:, :], flags_in, r=[], w=["flags"])
            dma("pool", wg_bf[:, :], wg_in, r=[], w=["wg"])
            act(siluc[:, :], cT[:, :], AF.Silu, r=["cT"], w=["siluc"])
            P.op("dve", lambda e: e.memset(onep[:, :], 1.0000001), r=[], w=["onep"])

            AR.reset(0)
            xt_b = [AR.alloc([128, 4, 1024], F32) for _ in range(2)]
            xst_b = [AR.alloc([128, 8, 512], F32) for _ in range(2)]
            for s in range(4):
                xt = xt_b[s % 2]
                xst = xst_b[s % 2]
                dma("sp", xt, x_in[s * 512:(s + 1) * 512, :].rearrange("(t p) f -> p t f", p=128),
                    r=[], w=[f"xt{s % 2}"])
                for k in range(8):
                    ps = psums[k % 4]
                    for t in range(4):
                        P.op("pe", lambda e, o=ps[:, t * 128:(t + 1) * 128], i=xt[:, t, k * 128:(k + 1) * 128]:
                             e.transpose(o, i, ident[:, :]),
                             r=[f"xt{s % 2}", "ident"], w=[f"ps{k % 4}"])
                    if k % 2 == 0:
                        cp("dve", xst[:, k, :], ps[:, :], r=[f"ps{k % 4}"], w=[f"xst{s % 2}"])
                    else:
                        act(xst[:, k, :], ps[:, :], AF.Copy, r=[f"ps{k % 4}"], w=[f"xst{s % 2}"])
                dma("sp", xT_s.rearrange("k p t -> p k t")[:, :, s * 512:(s + 1) * 512], xst,
                    r=[f"xst{s % 2}"], w=[f"xT_s{s}"])
            for l in range(DEPTH):
                vo = l * NV
                dv_ = derived[:, l * 48:(l + 1) * 48]
                Apre = dv_[:, 0:8]
                shiftv = dv_[:, 8:16]
                Gp = dv_[:, 16:24]
                Cv = dv_[:, 24:32]
                dtmp = dv_[:, 32:40]
                C2v = dv_[:, 40:48]
                modv = modv_all[:, l * 24:(l + 1) * 24]
                AR.reset(64 * 1024 + l * 32 * 1024)
                wm_b = [AR.alloc([128, 8, 1024], BF16) for _ in range(2)]
                for m in range(3):
                    wm = wm_b[m % 2]
                    dma("pool", wm, wmod_in[l].rearrange("(k p) c -> p k c", p=128)[:, :, m * 1024:(m + 1) * 1024],
                        r=[], w=[f"wm{l}_{m % 2}"])
                    ps = psums[4 + m % 2]
                    for oc in range(8):
                        for k in range(8):
                            mm(ps[:, oc:oc + 1], wm[:, k, oc * 128:(oc + 1) * 128], siluc[:, k:k + 1],
                               k == 0, k == 7, r=[f"wm{l}_{m % 2}", "siluc"], w=[f"ps{4 + m % 2}"])
                    tt("dve", modv[:, m * 8:(m + 1) * 8], ps[:, 0:8], vecs[:, vo + V_BMOD + m * 8: vo + V_BMOD + (m + 1) * 8],
                       ALU.add, r=[f"ps{4 + m % 2}", "vecs"], w=["modv"])
                cp("dve", shiftv, modv[:, 0:8], r=["modv"], w=["derived"])
                ts("dve", dtmp, modv[:, 8:16], 1.0, None, ALU.add, None, r=["modv"], w=["derived"])
                tt("dve", Apre, dtmp, vecs[:, vo + V_GPRE: vo + V_GPRE + 8], ALU.mult, r=["vecs", "derived"], w=["derived"])
                tt("dve", Gp, modv[:, 16:24], vecs[:, vo + V_GPOST: vo + V_GPOST + 8], ALU.mult,
                   r=["vecs", "modv"], w=["derived"])
                zz = lru_sm[:, 56:64]
                act(zz, vecs[:, vo + V_LAM: vo + V_LAM + 8], AF.Exp, r=["vecs"], w=["zz"], scale=-1.0)
                ts("dve", dtmp, zz, -0.25, 1.0 / 3.0, ALU.mult, ALU.add, r=["zz"], w=["derived"])
                tt("dve", dtmp, dtmp, zz, ALU.mult, r=["zz", "derived"], w=["derived"])
                ts("dve", dtmp, dtmp, -1.0, 0.5, ALU.mult, ALU.add, r=["derived"], w=["derived"])
                tt("dve", dtmp, dtmp, zz, ALU.mult, r=["zz", "derived"], w=["derived"])
                ts("dve", dtmp, dtmp, -1.0, 1.0, ALU.mult, ALU.add, r=["derived"], w=["derived"])
                tt("dve", dtmp, dtmp, zz, ALU.mult, r=["zz", "derived"], w=["derived"])
                ts("dve", Cv, dtmp, -8.0, None, ALU.mult, None, r=["derived"], w=["derived"])
                ts("dve", C2v, dtmp, -16.0, None, ALU.mult, None, r=["derived"], w=["derived"])
            P.barrier()
            if stop == 1:
                raise _Stop()

            for l in range(DEPTH):
                vo = l * NV
                last = (l == DEPTH - 1)
                dv_ = derived[:, l * 48:(l + 1) * 48]
                Apre = dv_[:, 0:8]
                shiftv = dv_[:, 8:16]
                Gp = dv_[:, 16:24]
                Cv = dv_[:, 24:32]
                dtmp = dv_[:, 32:40]
                C2v = dv_[:, 40:48]
                modv = modv_all[:, l * 24:(l + 1) * 24]

                AR.reset(0)
                hT = AR.alloc([128, 8, T], BF16)
                offA = AR.off
                xs_b = [AR.alloc([128, 8, 512], F32) for _ in range(2)]
                sq_b = [AR.alloc([128, 8, 512], BF16) for _ in range(2)]
                rstd_b = [AR.alloc([128, 512], F32) for _ in range(2)]
                tmp_b = [AR.alloc([128, 512], F32) for _ in range(2)]
                for s in range(4):
                    b = s % 2
                    xs, sq, rstd = xs_b[b], sq_b[b], rstd_b[b]
                    dma("sp", xs, xT_s.rearrange("k p t -> p k t")[:, :, s * 512:(s + 1) * 512],
                        r=[f"xT_s{s}"], w=[f"xs{b}"])
                    act(sq, xs, AF.Square, r=[f"xs{b}"], w=[f"sq{b}"])
                    ps = psums[b]
                    for k in range(8):
                        mm(ps[:, :], ones_bf, sq[:, k, :], k == 0, k == 7, r=[f"sq{b}", "cbf"], w=[f"ps{b}"])
                    act(rstd, ps[:, :], AF.Sqrt, r=[f"ps{b}"], w=[f"rstd{b}"], bias=EPS, scale=1.0 / D_MODEL)
                    P.op("dve", lambda e, o=rstd: e.reciprocal(out=o, in_=o), r=[f"rstd{b}"], w=[f"rstd{b}"])
                    for k in range(8):
                        tb = k % 2
                        stt(tmp_b[tb], xs[:, k, :], Apre[:, k:k + 1], rstd, ALU.mult, ALU.mult,
                            r=[f"xs{b}", f"rstd{b}", "derived"], w=[f"tmpA{tb}"])
                        act(hT[:, k, s * 512:(s + 1) * 512], tmp_b[tb], AF.Identity,
                            r=[f"tmpA{tb}", "derived"], w=["hT"], bias=shiftv[:, k:k + 1])
                P.barrier()
                if stop == 3:
                    raise _Stop()

                if debug and l == 0:
                    dma("sp", hT_d, hT, r=["hT"], w=["hT_d"])
                AR.reset(offA)
                wb_b = [AR.alloc([128, 8, 512], BF16) for _ in range(3)]
                st_b = [AR.alloc([128, T], F32) for _ in range(3)]
                vst_b = [AR.alloc([128, 4, 512], BF16) for _ in range(2)]
                ubuf = AR.alloc([128, T + 64], F32)
                uc_b = [AR.alloc([128, T], F32) for _ in range(2)]
                ucbf_b = [AR.alloc([128, T], BF16) for _ in range(2)]
                rr = AR.alloc([128, T], F32)
                ii = AR.alloc([128, T], F32)
                aa = AR.alloc([128, T], F32)
                t1 = AR.alloc([128, T], F32)
                t2 = AR.alloc([128, T], F32)
                xin = AR.alloc([128, T], F32)
                hl = AR.alloc([128, T], F32)
                pk3 = lru_sm[:, 0:16]
                hin = lru_sm[:, 16:24]
                rsum = lru_sm[:, 24:32]
                ut = lru_sm[:, 32:56]
                ut3 = ut.rearrange("p (k t) -> p k t", k=8)
                win_v = win_in[l].rearrange("(k p) c -> p k c", p=128)
                groups = [("u", 0, 5120), ("u", 1, 5632)]
                for g in range(3):
                    groups.append(("k", g, 1536 + g * 512))
                for g in range(3):
                    groups.append(("v", g, 3072 + g * 512))
                for g in range(3):
                    groups.append(("q", g, g * 512))
                groups.append(("ga", 0, 4608))
                groups.append(("gl", 0, 6144))
                groups.append(("gl", 1, 6656))
                for i in range(4):
                    groups.append(("mg", i, 7168 + i * 512))

                def utail_exchange():
                    dma("sp", pkg2.rearrange("p (k t) -> p k t", k=8), UT_s.rearrange("k p t -> p k t")[:, :, T - 3:T],
                        r=[f"UT{k}" for k in range(8)], w=["pkg2"])
                    allgather(pkg2, gath2, r=["pkg2"], w=["gath2"])
                    gather(ut, gath2, 12, r=["gath2", "idx"], w=["ut"])
                    ts("dve", ut, ut, flags[:, 0:1], None, ALU.mult, None, r=["ut", "flags"], w=["ut"])

                def halo_pack(ci):
                    if ci in (0, 1):
                        for h in (2 * ci, 2 * ci + 1):
                            dma("pool", pkg[h * 128:(h + 1) * 128, :], KT_s[8 + h], r=[f"KT{8 + h}"], w=[f"pkg{ci}"])
                    elif ci in (2, 3):
                        for h in (2 * (ci - 2), 2 * (ci - 2) + 1):
                            dma("pool", pkg[(4 + h) * 128:(5 + h) * 128, :].rearrange("p (t c) -> p t c", t=4),
                                Vs[2, 4 * h:4 * h + 4].rearrange("t p c -> p t c"), r=["Vs2"], w=[f"pkg{ci}"])
                    elif ci == 4:
                        for h in range(4):
                            dma("pool", pkg[8 * 128:9 * 128, h * 512:(h + 1) * 512].rearrange("p (r i) -> p r i", r=4),
                                KT_s[4 + h].rearrange("p (r i) -> p r i", r=4)[:, :, 384:512],
                                r=[f"KT{4 + h}"], w=[f"pkg{ci}"])
                            dma("pool", pkg[9 * 128:10 * 128, h * 128:(h + 1) * 128], KT_s[h][:, 1920:2048],
                                r=[f"KT{h}"], w=[f"pkg{ci}"])
                    else:
                        for r_ in range(4):
                            dma("pool", pkg[10 * 128:11 * 128, r_ * 512:(r_ + 1) * 512], Vs[1, r_ * 4 + 3],
                                r=["Vs1"], w=[f"pkg{ci}"])
                        dma("pool", pkg[11 * 128:12 * 128, 0:512], Vs[0, 15], r=["Vs0"], w=[f"pkg{ci}"])
                    allgather(pkg[ci * 256:(ci + 1) * 256, :], gaths[ci], r=[f"pkg{ci}"], w=[f"gath{ci}"])

                def lru_stage1(k):
                    b = k % 2
                    uc, ucbf = uc_b[b], ucbf_b[b]
                    cp("dve", ubuf[:, 61:64], ut3[:, k, :], r=["ut"], w=["ubuf"])
                    dma("pool", ubuf[:, 64:64 + T], UT_s[k], r=[f"UT{k}"], w=["ubuf"])
                    cw = lambda j: vecs[:, vo + V_CONVW + j * 8 + k: vo + V_CONVW + j * 8 + k + 1]
                    ts("dve", uc, ubuf[:, 64:64 + T], cw(0), vecs[:, vo + V_CONVB + k: vo + V_CONVB + k + 1],
                       ALU.mult, ALU.add, r=["ubuf", "vecs"], w=[f"uc{b}"])
                    for j in range(1, 4):
                        stt(uc, ubuf[:, 64 - j:64 - j + T], cw(j), uc, ALU.mult, ALU.add,
                            r=["ubuf", "vecs", f"uc{b}"], w=[f"uc{b}"])
                    cp("dve", ucbf, uc, r=[f"uc{b}"], w=[f"ucbf{b}"])

                def lru_stage2(k):
                    b = k % 2
                    uc, ucbf = uc_b[b], ucbf_b[b]
                    for gate in range(2):
                        wgt = wg_bf[:, ((l * 2 + gate) * 8 + k) * 128:((l * 2 + gate) * 8 + k + 1) * 128]
                        bcol = (V_BRG if gate == 0 else V_BIG) + k
                        dstt = rr if gate == 0 else ii
                        for hf in range(2):
                            for j in range(2):
                                s = hf * 2 + j
                                mm(psums[6 + j], wgt, ucbf[:, s * 512:(s + 1) * 512], True, True,
                                   r=[f"ucbf{b}", "wg"], w=[f"ps{6 + j}"])
                            act(dstt[:, hf * 1024:(hf + 1) * 1024].rearrange("p (a c) -> p a c", a=2),
                                psum_all[:, 6:8, :], AF.Sigmoid, r=["ps6", "ps7", "vecs"],
                                w=["rr" if gate == 0 else "ii"], bias=vecs[:, vo + bcol: vo + bcol + 1])
                    P.op("dve", lambda e, o=rsum[:, k:k + 1], i=rr: e.reduce_sum(out=o, in_=i, axis=mybir.AxisListType.X),
                         r=["rr"], w=["rsum"])
                    act(aa, rr, AF.Exp, r=["rr", "derived"], w=["aa"], scale=Cv[:, k:k + 1])
                    act(t1, rr, AF.Exp, r=["rr", "derived"], w=["t1"], scale=C2v[:, k:k + 1])
                    act(t1, t1, AF.Sqrt, r=["t1"], w=["t1"], scale=-1.0, bias=onep[:, 0:1])
                    tt("dve", t2, ii, uc, ALU.mult, r=["ii", f"uc{b}"], w=["t2"])
                    tt("dve", xin, t1, t2, ALU.mult, r=["t1", "t2"], w=["xin"])
                    P.op("dve", lambda e, o=hl, a=aa, x=xin: e.tensor_tensor_scan(
                        out=o, data0=a, data1=x, initial=0.0, op0=ALU.mult, op1=ALU.add), r=["aa", "xin"], w=["hl"])
                    cp("dve", pk3[:, k:k + 1], hl[:, T - 1:T], r=["hl"], w=["pk3"])
                    dma("pool", A_s[k], aa, r=["aa"], w=[f"A_s{k}"])
                    dma("pool", X_s[k], xin, r=["xin"], w=[f"X_s{k}"])

                stcount = 0
                pscount = 0
                vcount = 0
                def wload(gx):
                    if gx < len(groups):
                        dma("pool", wb_b[gx % 3], win_v[:, :, groups[gx][2]:groups[gx][2] + 512], r=[], w=[f"wb{gx % 3}"])
                wload(0)
                wload(1)
                for gidx, (kind, gi, c0) in enumerate(groups):
                    wbi = gidx % 3
                    wb = wb_b[wbi]
                    wload(gidx + 2)
                    if kind == "v":
                        D = GROUP_DIL[gi]
                        nb = 16 // D
                        for tq in range(4):
                            vb = vcount % 2
                            vcount += 1
                            vst = vst_b[vb]
                            for ti in range(4):
                                tidx = tq * 4 + ti
                                r_, n_ = tidx // nb, tidx % nb
                                pi = pscount % 6
                                pscount += 1
                                ps = psums[pi]
                                t0 = D * 128 * n_ + r_
                                if os.environ.get("V_NOSTRIDE"):
                                    D = 1
                                    t0 = 128 * tidx
                                for k in range(8):
                                    mm(ps[:, :], hT[:, k, t0:t0 + D * 127 + 1:D], wb[:, k, :], k == 0, k == 7,
                                       r=["hT", f"wb{wbi}"], w=[f"ps{pi}"])
                                act(vst[:, ti, :], ps[:, :], AF.Copy, r=[f"ps{pi}"], w=[f"vst{vb}"])
                            dma("sp", Vs[gi, tq * 4:(tq + 1) * 4].rearrange("t p c -> p t c"), vst,
                                r=[f"vst{vb}"], w=[f"Vs{gi}"])
                    else:
                        for cc in range(4):
                            sbi = stcount % 3
                            stcount += 1
                            if kind in ("q", "k", "ga", "gl", "mg"):
                                st = st_b[sbi][:, 0:T // 2].bitcast(BF16)
                            else:
                                st = st_b[sbi]
                            for half in range(2):
                                banks = ((pscount % 6), ((pscount + 1) % 6))
                                pscount += 2
                                merged = not (kind in ("q", "k") and GROUP_DIL[gi] != 1)
                                for k in range(8):
                                    for j in range(2):
                                        s = half * 2 + j
                                        mm(psums[banks[j]][:, :], wb[:, k, cc * 128:(cc + 1) * 128],
                                           hT[:, k, s * 512:(s + 1) * 512], k == 0, k == 7,
                                           r=["hT", f"wb{wbi}"], w=[f"ps{banks[j]}"])
                                if merged:
                                    fn_ = {"q": AF.Copy, "k": AF.Copy, "u": AF.Copy, "ga": AF.Silu, "gl": AF.Silu,
                                           "mg": AF.Sigmoid}[kind]
                                    act(st[:, half * 1024:(half + 1) * 1024].rearrange("p (a c) -> p a c", a=2),
                                        psum_all[:, banks[0]:banks[0] + 2, :], fn_,
                                        r=[f"ps{banks[0]}", f"ps{banks[1]}"], w=[f"st{sbi}"])
                                for j in range(2):
                                    if merged:
                                        break
                                    s = half * 2 + j
                                    pi = banks[j]
                                    ps = psums[pi]
                                    if kind in ("q", "k"):
                                        D = GROUP_DIL[gi]
                                        if D == 1:
                                            o_ap = st[:, s * 512:(s + 1) * 512]
                                            i_ap = ps[:, :]
                                        else:
                                            w_ = 512 // D
                                            o_ap = st.rearrange("p (r i) -> p r i", r=D)[:, :, s * w_:(s + 1) * w_]
                                            i_ap = ps[:, :].rearrange("p (i r) -> p r i", r=D)
                                        act(o_ap, i_ap, AF.Copy, r=[f"ps{pi}"], w=[f"st{sbi}"])
                                    elif kind == "u":
                                        act(st[:, s * 512:(s + 1) * 512], ps[:, :], AF.Copy, r=[f"ps{pi}"], w=[f"st{sbi}"])
                                    elif kind in ("ga", "gl"):
                                        act(st[:, s * 512:(s + 1) * 512], ps[:, :], AF.Silu, r=[f"ps{pi}"], w=[f"st{sbi}"])
                                    else:
                                        act(st[:, s * 512:(s + 1) * 512], ps[:, :], AF.Sigmoid, r=[f"ps{pi}"],
                                            w=[f"st{sbi}"])
                            if kind == "q":
                                dst, key = QT_s[gi * 4 + cc], f"QT{gi * 4 + cc}"
                            elif kind == "k":
                                dst, key = KT_s[gi * 4 + cc], f"KT{gi * 4 + cc}"
                            elif kind == "ga":
                                dst, key = GA_s[cc], f"GA{cc}"
                            elif kind == "u":
                                dst, key = UT_s[gi * 4 + cc], f"UT{gi * 4 + cc}"
                            elif kind == "gl":
                                dst, key = GL_s[gi * 4 + cc], f"GL{gi * 4 + cc}"
                            else:
                                dst, key = MG_s[gi * 4 + cc], f"MG{gi * 4 + cc}"
                            dma("sp", dst, st, r=[f"st{sbi}"], w=[key])
                    nl_ = os.environ.get("NO_LRU", "")
                    if nl_ == "1":
                        continue
                    if gidx == 1:
                        utail_exchange()
                    if 7 <= gidx < 13 and nl_ != "halo":
                        halo_pack(gidx - 7)
                    if nl_ == "stages":
                        continue
                    if gidx >= 2 and (gidx - 2) % 2 == 0 and (gidx - 2) // 2 < 8:
                        lru_stage1((gidx - 2) // 2)
                    if gidx >= 3 and (gidx - 3) % 2 == 0 and (gidx - 3) // 2 < 8:
                        lru_stage2((gidx - 3) // 2)
                tt("dve", rsum, rsum, Cv, ALU.mult, r=["rsum", "derived"], w=["rsum"])
                act(pk3[:, 8:16], rsum, AF.Exp, r=["rsum"], w=["pk3"])
                dma("sp", pkg3, pk3, r=["pk3"], w=["pkg3"])
                allgather(pkg3, gath3, r=["pkg3"], w=["gath3"])
                g3 = g3sb[:, :].rearrange("p (s c) -> p s c", s=4)
                dma("sp", g3, gath3.rearrange("(s p) c -> p s c", p=128), r=["gath3"], w=["g3"])
                Fa = lru_sm[:, 56:64]
                cp("dve", Fa, g3[:, 0, 0:8], r=["g3"], w=["Fa"])
                ts("dve", hin, Fa, flags[:, 1:2], None, ALU.mult, None, r=["Fa", "flags"], w=["hin"])
                for j in (1, 2):
                    tt("dve", Fa, Fa, g3[:, j, 8:16], ALU.mult, r=["Fa", "g3"], w=["Fa"])
                    tt("dve", Fa, Fa, g3[:, j, 0:8], ALU.add, r=["Fa", "g3"], w=["Fa"])
                    stt(hin, Fa, flags[:, 1 + j:2 + j], hin, ALU.mult, ALU.add, r=["Fa", "flags", "hin"], w=["hin"])
                P.barrier()
                if stop == 6:
                    raise _Stop()

                TOFF = 48 * 1024
                AR.reset(0)
                AT = AR.alloc([128, 4, T], BF16)
                BT = AR.alloc([128, 8, T], BF16)
                assert AR.off <= TOFF
                AR.reset(TOFF)
                hp = [AR.alloc([128, 2048], BF16) for _ in range(4)]
                hv3 = [AR.alloc([128, 2048], BF16) for _ in range(4)]
                k3h = AR.alloc([128, 2048], BF16)
                q_b = [AR.alloc([128, T], BF16) for _ in range(2)]
                k_b = [AR.alloc([128, T], BF16) for _ in range(2)]
                v_b = [AR.alloc([128, 16, 128], BF16) for _ in range(2)]
                acc = AR.alloc([128, 2, T], F32)
                pT_b = [AR.alloc([128, 256], BF16) for _ in range(2)]
                pm_b = [AR.alloc([128, 256], BF16) for _ in range(3)]
                gab = AR.alloc([128, T], BF16)
                rden = AR.alloc([128, T], F32)
                assert AR.off <= ARENA_BYTES - 40 * 1024, AR.off
                AR.reset(ARENA_BYTES - 40 * 1024)
                wpa = AR.alloc([128, 4, 1024], BF16)
                wpb = AR.alloc([128, 8, 1024], BF16)
                wo = AR.alloc([128, 8, 1024], BF16)
                dma("pool", wpa, wpa_in[l].rearrange("(k p) c -> p k c", p=128), r=[], w=["wpa"])
                dma("pool", wpb, wpb_in[l].rearrange("(k p) c -> p k c", p=128), r=[], w=["wpb"])
                dma("pool", wo, wo_in[l].rearrange("(k p) c -> p k c", p=128), r=[], w=["wo"])
                for i in range(4):
                    gather(hp[i], gaths[(8 + i) // 2], 8 + i, r=[f"gath{(8 + i) // 2}", "idx"], w=[f"hp{i}"])
                    gather(hv3[i], gaths[(4 + i) // 2], 4 + i, r=[f"gath{(4 + i) // 2}", "idx"], w=[f"hv3{i}"])
                blk = 0
                hg = 0
                SC = 128.0 ** -0.5
                for h in range(4):
                    gather(k3h, gaths[h // 2], h, r=[f"gath{h // 2}", "idx"], w=["k3h"])
                    dma("sp", gab, GA_s[h], r=[f"GA{h}"], w=["gab"])
                    for g in range(3):
                        D = GROUP_DIL[g]
                        nb = 16 // D
                        Lr = T // D
                        b = hg % 2
                        hg += 1
                        qT, kT, vv = q_b[b], k_b[b], v_b[b]
                        dma("sp", qT, QT_s[g * 4 + h], r=[f"QT{g * 4 + h}"], w=[f"qT{b}"])
                        dma("sp", kT, KT_s[g * 4 + h], r=[f"KT{g * 4 + h}"], w=[f"kT{b}"])
                        dma("sp", vv, Vs[g].rearrange("t p c -> p t c")[:, :, h * 128:(h + 1) * 128],
                            r=[f"Vs{g}"], w=[f"vv{b}"])
                        blocks = []
                        for r_ in range(D):
                            for n_ in range(nb):
                                pos = r_ * Lr + n_ * 128
                                tidx = r_ * nb + n_
                                if n_ > 0:
                                    kprev, vprev, msk = kT[:, pos - 128:pos], vv[:, tidx - 1, :], maskPC
                                    rk, rv = [f"kT{b}"], [f"vv{b}"]
                                else:
                                    msk = maskF
                                    if g == 0:
                                        kprev, vprev = hp[1][:, h * 128:(h + 1) * 128], hp[3][:, h * 128:(h + 1) * 128]
                                        rk, rv = ["hp1"], ["hp3"]
                                    elif g == 1:
                                        kprev = hp[0][:, h * 512 + r_ * 128: h * 512 + (r_ + 1) * 128]
                                        vprev = hp[2][:, r_ * 512 + h * 128: r_ * 512 + (h + 1) * 128]
                                        rk, rv = ["hp0"], ["hp2"]
                                    else:
                                        kprev = k3h[:, r_ * 128:(r_ + 1) * 128]
                                        vprev = hv3[r_ // 4][:, (r_ % 4) * 512 + h * 128:(r_ % 4) * 512 + (h + 1) * 128]
                                        rk, rv = ["k3h"], [f"hv3{r_ // 4}"]
                                blocks.append((pos, tidx, kprev, vprev, msk, rk, rv, D * 128 * n_ + r_, blk))
                                blk += 1

                        def emit_S(bd):
                            pos, tidx, kprev, vprev, msk, rk, rv, t0, bn = bd
                            pb = bn % 3
                            pss, pm = psums[pb], pm_b[pb]
                            qblk = qT[:, pos:pos + 128]
                            mm(pss[:, 0:256], ident_bf, msk, True, False, r=["cbf"], w=[f"ps{pb}"])
                            mm(pss[:, 0:128], kprev, qblk, False, True, r=rk + [f"qT{b}"], w=[f"ps{pb}"])
                            mm(pss[:, 128:256], kT[:, pos:pos + 128], qblk, False, True,
                               r=[f"kT{b}", f"qT{b}"], w=[f"ps{pb}"])
                            act(pm, pss[:, 0:256], AF.Exp, r=[f"ps{pb}"], w=[f"pm{pb}"], scale=SC)

                        def emit_PV(bd):
                            pos, tidx, kprev, vprev, msk, rk, rv, t0, bn = bd
                            pb = bn % 3
                            po = 3 + bn % 2
                            pso, pm = psums[po], pm_b[pb]
                            mm(pso[:, 0:128], vprev, pm[:, 0:128], True, False, r=rv + [f"pm{pb}"], w=[f"ps{po}"])
                            mm(pso[:, 0:128], vv[:, tidx, :], pm[:, 128:256], False, True,
                               r=[f"vv{b}", f"pm{pb}"], w=[f"ps{po}"])
                            mm(pso[:, 128:256], ones_bf, pm[:, 0:128], True, False, r=["cbf", f"pm{pb}"], w=[f"ps{po}"])
                            mm(pso[:, 128:256], ones_bf, pm[:, 128:256], False, True,
                               r=["cbf", f"pm{pb}"], w=[f"ps{po}"])
                            av = acc[:, :, t0:t0 + D * 127 + 1:D]
                            pv = pso[:, 0:256].rearrange("p (a q) -> p a q", a=2)
                            if g == 0:
                                cp("dve", av, pv, r=[f"ps{po}"], w=["acc"])
                            else:
                                tt("dve", av, av, pv, ALU.add, r=[f"ps{po}", "acc"], w=["acc"])

                        emit_S(blocks[0])
                        emit_S(blocks[1])
                        for bi_ in range(len(blocks)):
                            if bi_ + 2 < len(blocks):
                                emit_S(blocks[bi_ + 2])
                            emit_PV(blocks[bi_])
                    P.op("dve", lambda e, o=rden, i=acc[:, 1, :]: e.reciprocal(out=o, in_=i), r=["acc"], w=["rden"])
                    tt("dve", rden, rden, acc[:, 0, :], ALU.mult, r=["acc", "rden"], w=["rden"])
                    tt("pool", AT[:, h, :], rden, gab, ALU.mult, r=["rden", "gab"], w=["AT"])
                P.barrier()
                if stop == 7:
                    raise _Stop()

                AR.reset(TOFF)
                a_b = [AR.alloc([128, T], F32) for _ in range(2)]
                x_b = [AR.alloc([128, T], F32) for _ in range(2)]
                gl_b = [AR.alloc([128, T], BF16) for _ in range(2)]
                h_b = [AR.alloc([128, T], F32) for _ in range(2)]
                for k in range(8):
                    b = k % 2
                    dma("sp", a_b[b], A_s[k], r=[f"A_s{k}"], w=[f"a2{b}"])
                    dma("sp", x_b[b], X_s[k], r=[f"X_s{k}"], w=[f"x2{b}"])
                    dma("sp", gl_b[b], GL_s[k], r=[f"GL{k}"], w=[f"gl{b}"])
                    P.op("dve", lambda e, o=h_b[b], a=a_b[b], x=x_b[b], hi=hin[:, k:k + 1]: e.tensor_tensor_scan(
                        out=o, data0=a, data1=x, initial=hi, op0=ALU.mult, op1=ALU.add),
                        r=[f"a2{b}", f"x2{b}", "hin"], w=[f"h2{b}"])
                    tt("pool", BT[:, k, :], h_b[b], gl_b[b], ALU.mult, r=[f"h2{b}", f"gl{b}"], w=["BT"])
                P.barrier()
                if stop == 8:
                    raise _Stop()

                if debug and l == 0:
                    dma("sp", AT_d, AT, r=["AT"], w=["AT_d"])
                    dma("sp", BT_d, BT, r=["BT"], w=["BT_d"])
                    dma("sp", sm_d[:, 0:64], lru_sm[:, :], r=["hin", "pk3", "ut"], w=["sm_d"])
                    dma("sp", sm_d[:, 64:104], derived[:, 0:40], r=["derived"], w=["sm_d"])
                    dma("sp", sm_d[:, 104:128], modv_all[:, 0:24], r=["modv"], w=["sm_d"])
                    P.barrier()
                AR.reset(TOFF)
                mga = AR.alloc([128, 16, 512], BF16)
                xs = AR.alloc([128, 8, 512], F32)
                ta = AR.alloc([128, 512], F32)
                tb_ = AR.alloc([128, 512], F32)
                zT = AR.alloc([128, 8, 512], BF16)
                ysq = AR.alloc([128, 8, 512], BF16)
                yT = AR.alloc([128, 8, 512], F32)
                rstd2 = AR.alloc([128, 512], F32)
                tmpf = AR.alloc([128, 512], F32)
                xn = AR.alloc([128, 8, 512], F32)
                otile = AR.alloc([128, 1024], F32)
                assert AR.off <= ARENA_BYTES - 40 * 1024, AR.off
                if stop == 8.1:
                    P.barrier()
                    raise _Stop()
                for s in range(4):
                    sl = slice(s * 512, (s + 1) * 512)
                    dma("sp", mga, MG_s.rearrange("o p t -> p o t")[:, :, sl], r=[f"MG{o}" for o in range(16)], w=["mga"])
                    dma("sp", xs, xT_s.rearrange("k p t -> p k t")[:, :, sl], r=[f"xT_s{s}"], w=["xsF"])
                    for o in range(8):
                        pa, pb_ = psums[(2 * o) % 4], psums[(2 * o + 1) % 4]
                        ka, kb = f"ps{(2 * o) % 4}", f"ps{(2 * o + 1) % 4}"
                        for h in range(4):
                            mm(pa[:, :], wpa[:, h, o * 128:(o + 1) * 128], AT[:, h, sl], h == 0, h == 3,
                               r=["wpa", "AT"], w=[ka])
                        for k in range(8):
                            mm(pb_[:, :], wpb[:, k, o * 128:(o + 1) * 128], BT[:, k, sl], k == 0, k == 7,
                               r=["wpb", "BT"], w=[kb])
                        tt("dve", ta, pa[:, :], mga[:, o, :], ALU.mult, r=[ka, "mga"], w=["ta"])
                        tt("dve", tb_, pb_[:, :], mga[:, 8 + o, :], ALU.mult, r=[kb, "mga"], w=["tb"])
                        tt("dve", zT[:, o, :], ta, tb_, ALU.add, r=["ta", "tb"], w=["zT"])
                    if stop == 8.2:
                        P.barrier()
                        raise _Stop()
                    if stop == 8.25:
                        P.barrier()
                        raise _Stop()
                    for o2 in range(8):
                        pi = 4 + o2 % 2
                        py = psums[pi]
                        for o in range(8):
                            mm(py[:, :], wo[:, o, o2 * 128:(o2 + 1) * 128], zT[:, o, :], o == 0, o == 7,
                               r=["wo", "zT"], w=[f"ps{pi}"])
                        if stop == 8.26:
                            P.barrier()
                            raise _Stop()
                        act(yT[:, o2, :], py[:, :], AF.Copy, r=[f"ps{pi}"], w=["yT"])
                        if stop == 8.27:
                            P.barrier()
                            raise _Stop()
                        act(ysq[:, o2, :], yT[:, o2, :], AF.Square, r=["yT"], w=["ysq"])
                    if stop == 8.3:
                        P.barrier()
                        raise _Stop()
                    pss_ = psums[6]
                    for o2 in range(8):
                        mm(pss_[:, :], ones_bf, ysq[:, o2, :], o2 == 0, o2 == 7, r=["ysq", "cbf"], w=["ps6"])
                    act(rstd2, pss_[:, :], AF.Sqrt, r=["ps6"], w=["rstd2"], bias=EPS, scale=1.0 / D_MODEL)
                    P.op("dve", lambda e, o=rstd2: e.reciprocal(out=o, in_=o), r=["rstd2"], w=["rstd2"])
                    for o2 in range(8):
                        stt(tmpf, yT[:, o2, :], Gp[:, o2:o2 + 1], rstd2, ALU.mult, ALU.mult,
                            r=["yT", "rstd2", "derived"], w=["tmpf"])
                        tt("dve", xn[:, o2, :], tmpf, xs[:, o2, :], ALU.add, r=["tmpf", "xsF"], w=["xn"])
                    if stop == 8.4:
                        P.barrier()
                        raise _Stop()
                    if not last:
                        dma("sp", xT_s.rearrange("k p t -> p k t")[:, :, sl], xn, r=["xn"], w=[f"xT_s{s}"])
                    else:
                        for t in range(4):
                            for half in range(2):
                                pi = 6 + half
                                pt = psums[pi]
                                for q in range(4):
                                    o2 = half * 4 + q
                                    P.op("pe", lambda e, o=pt[:, q * 128:(q + 1) * 128], i=xn[:, o2, t * 128:(t + 1) * 128]:
                                         e.transpose(o, i, ident[:, :]), r=["xn", "ident"], w=[f"ps{pi}"])
                                if half == 0:
                                    cp("dve", otile[:, 0:512], pt[:, :], r=["ps6"], w=["otile"])
                                else:
                                    act(otile[:, 512:1024], pt[:, :], AF.Copy, r=["ps7"], w=["otile"])
                            tok0 = s * 512 + t * 128
                            dma("sp", out[tok0:tok0 + 128, :], otile, r=["otile"], w=[f"out{tok0}"])
                P.barrier()
                if stop == 9:
                    raise _Stop()

        except _Stop:
            if debug:
                dma("sp", sm_d[:, 0:64], lru_sm[:, :], r=["hin", "pk3", "ut"], w=["sm_d"])
                dma("sp", sm_d[:, 64:104], derived[:, 0:40], r=["derived"], w=["sm_d"])
                dma("sp", sm_d[:, 104:128], modv_all[:, 0:24], r=["modv"], w=["sm_d"])
                P.barrier()
        P.emit(nc, block, sems, dma_sems)
    return nc


_NC_CACHE = {}


def _host_layouts(inputs):
    f = np.float32
    x = np.asarray(inputs["x"], f)
    c = np.asarray(inputs["c"], f)

    def colvec(v):
        v = np.asarray(v, f)
        return np.ascontiguousarray(v.reshape(-1, 128).T)

    vecs = np.zeros((128, DEPTH * NV), f)
    for l in range(DEPTH):
        o = l * NV
        vecs[:, o + V_BMOD:o + V_BMOD + 24] = colvec(inputs["b_mod"][l])
        vecs[:, o + V_GPRE:o + V_GPRE + 8] = colvec(inputs["g_pre"][l])
        for j in range(4):
            vecs[:, o + V_CONVW + j * 8:o + V_CONVW + (j + 1) * 8] = colvec(inputs["conv_w"][l][j])
        vecs[:, o + V_CONVB:o + V_CONVB + 8] = colvec(inputs["conv_b"][l])
        vecs[:, o + V_BRG:o + V_BRG + 8] = colvec(inputs["b_rg"][l])
        vecs[:, o + V_BIG:o + V_BIG + 8] = colvec(inputs["b_ig"][l])
        vecs[:, o + V_LAM:o + V_LAM + 8] = colvec(inputs["lru_lambda"][l])
        vecs[:, o + V_GPOST:o + V_GPOST + 8] = colvec(inputs["g_post"][l])
    wg = np.zeros((128, DEPTH, 2, 8, 128), f)
    for l in range(DEPTH):
        for gi, name in enumerate(("w_rg", "w_ig")):
            w = np.asarray(inputs[name][l], f)
            for k in range(8):
                wg[0:64, l, gi, k, 0:64] = w[2 * k]
                wg[64:128, l, gi, k, 64:128] = w[2 * k + 1]
    wg = np.ascontiguousarray(wg.reshape(128, -1))
    kk = np.arange(128)[:, None]
    qq = np.arange(128)[None, :]
    mprev = (kk >= qq).astype(f)
    mcur = (kk <= qq).astype(f)
    common = {
        "vecs": vecs, "wg": wg,
        "w_mod": np.ascontiguousarray(inputs["w_mod"], f), "w_in": np.ascontiguousarray(inputs["w_in"], f),
        "w_pa": np.ascontiguousarray(inputs["w_pa"], f), "w_pb": np.ascontiguousarray(inputs["w_pb"], f),
        "w_o": np.ascontiguousarray(inputs["w_o"], f),
    }
    in_maps = []
    for core in range(8):
        b, j = core // 4, core % 4
        hp = 1.0 if j > 0 else 0.0
        NEG = np.float32(-30000.0)
        consts = np.concatenate([np.eye(128, dtype=f), np.ones((128, 128), f), np.eye(128, dtype=f),
                                 (1 - mprev) * NEG, (1 - mcur) * NEG, (1 - mprev * hp) * NEG, (1 - mcur) * NEG], axis=1)
        slot = max(j - 1, 0)
        idx = np.zeros((128, 16), np.int32)
        for pce in range(12):
            idx[:, pce] = slot * 256 + (pce % 2) * 128 + np.arange(128)
        idx[:, 12] = slot * 128 + np.arange(128)
        flags = np.zeros((128, 8), f)
        flags[:, 0] = hp
        if j > 0:
            flags[:, j] = 1.0
        m = dict(common)
        m["x"] = np.ascontiguousarray(x[b, j * T:(j + 1) * T, :])
        m["cT"] = np.ascontiguousarray(c[b].reshape(8, 128).T)
        m["consts"] = np.ascontiguousarray(consts)
        m["idx"] = idx
        m["flags"] = flags
        in_maps.append(m)
    return in_maps


def kernel(**inputs):
    if "nc" not in _NC_CACHE:
        _NC_CACHE["nc"] = build_nc()
    nc = _NC_CACHE["nc"]
    in_maps = _host_layouts(inputs)
    res = run_bass_kernel_spmd(nc, in_maps, core_ids=list(range(8)))
    outf = np.zeros((2, 4 * T, D_MODEL), np.float32)
    for core in range(8):
        b, j = core // 4, core % 4
        outf[b, j * T:(j + 1) * T, :] = res.results[core]["out"]
    return outf
```

```python
import contextlib
import os
import numpy as np
import concourse.bass as bass
import concourse.mybir as mybir
from concourse.bass_utils import run_bass_kernel_spmd

F32 = mybir.dt.float32
BF16 = mybir.dt.bfloat16
I32 = mybir.dt.int32
ALU = mybir.AluOpType
AF = mybir.ActivationFunctionType

D_MODEL = 1024
T = 2048
DEPTH = 2
NV = 104
EPS = 1e-6
ARENA_BYTES = 186 * 1024
GROUP_DIL = (1, 4, 16)

V_BMOD, V_GPRE, V_CONVW, V_CONVB, V_BRG, V_BIG, V_LAM, V_GPOST = 0, 24, 32, 64, 72, 80, 88, 96


class Op:
    __slots__ = ("eng", "fn", "deps", "dma", "idx", "signal", "sigval", "sem", "semval", "inc")

    def __init__(self, eng, fn, dma):
        self.eng, self.fn, self.dma = eng, fn, dma
        self.deps = []
        self.signal = False
        self.sigval = None
        self.sem = None
        self.semval = None
        self.inc = 16


ENGS = ("pe", "act", "dve", "pool", "sp")


class Prog:
    def __init__(self):
        self.ops = {e: [] for e in ENGS}
        self.lastw = {}
        self.readers = {}
        self.dmas_since_barrier = []

    def op(self, eng, fn, r=(), w=(), dma=False, inc=16):
        o = Op(eng, fn, dma)
        o.inc = inc
        deps = []
        for k in r:
            lw = self.lastw.get(k)
            if lw is not None:
                deps.append(lw)
        for k in w:
            lw = self.lastw.get(k)
            if lw is not None:
                deps.append(lw)
            deps.extend(self.readers.get(k, ()))
        seen = set()
        for d in deps:
            if id(d) in seen or d is o:
                continue
            seen.add(id(d))
            if (not d.dma) and (not dma) and d.eng == eng and eng == "pe":
                continue
            o.deps.append(d)
            d.signal = True
        for k in r:
            lst = self.readers.setdefault(k, [])
            if not dma:
                lst[:] = [x_ for x_ in lst if x_.dma or x_.eng != eng]
            lst.append(o)
        for k in w:
            self.lastw[k] = o
            self.readers[k] = []
        self.ops[eng].append(o)
        if dma:
            o.signal = True
            self.dmas_since_barrier.append(o)
        return o

    def barrier(self):
        lasts = []
        for e in ENGS:
            for o in reversed(self.ops[e]):
                if o.fn is not None and not o.dma:
                    lasts.append(o)
                    break
        deps = lasts + self.dmas_since_barrier
        self.dmas_since_barrier = []
        for e in ENGS:
            b = Op(e, None, False)
            for d in deps:
                if (not d.dma) and d.eng == e:
                    continue
                b.deps.append(d)
                d.signal = True
            self.ops[e].append(b)

    def emit(self, nc, block, sems, dma_sems):
        tot = {}
        ncc = 0
        for e in ENGS:
            cnt = 0
            nd = 0
            for o in self.ops[e]:
                if o.fn is None:
                    continue
                if o.dma:
                    if o.inc == 16:
                        pool = dma_sems[e]
                        o.sem = pool[nd % len(pool)]
                        nd += 1
                    else:
                        pool = dma_sems["cc"]
                        o.sem = pool[ncc % len(pool)]
                        ncc += 1
                    prev = tot.get(id(o.sem), 0)
                    o.sigval = prev
                    o.semval = prev + o.inc
                    tot[id(o.sem)] = o.semval
                elif o.signal:
                    cnt += 1
                    o.sigval = cnt
        prog = self

        def run_engine(e, engobj):
            known = {x: 0 for x in ENGS}
            known_dma = {}
            nd = 0
            for o in prog.ops[e]:
                for d in o.deps:
                    if d.dma:
                        key = id(d.sem)
                        if known_dma.get(key, 0) >= d.semval:
                            continue
                        engobj.wait_ge(d.sem, d.semval)
                        known_dma[key] = d.semval
                    else:
                        if known[d.eng] >= d.sigval:
                            continue
                        engobj.wait_ge(sems[d.eng], d.sigval)
                        known[d.eng] = d.sigval
                if o.fn is None:
                    continue
                if o.dma:
                    pool = dma_sems[e]
                    prev = o.sigval
                    key = id(o.sem)
                    if prev > 0 and known_dma.get(key, 0) < prev:
                        engobj.wait_ge(o.sem, prev)
                        known_dma[key] = prev
                    nd += 1
                    ins = o.fn(engobj)
                    ins.then_inc(o.sem, o.inc)
                else:
                    ins = o.fn(engobj)
                    if o.signal:
                        ins.then_inc(sems[e], 1)

        @block.tensor
        def _(eng):
            run_engine("pe", eng)

        @block.scalar
        def _(eng):
            run_engine("act", eng)

        @block.vector
        def _(eng):
            run_engine("dve", eng)

        @block.gpsimd
        def _(eng):
            run_engine("pool", eng)

        @block.sync
        def _(eng):
            run_engine("sp", eng)


class Arena:
    def __init__(self, ap_bf16):
        self.ap = ap_bf16
        self.off = 0

    def reset(self, off):
        self.off = off

    def alloc(self, shape, dtype):
        esz = 4 if dtype in (F32, I32) else 2
        n = int(np.prod(shape[1:]))
        nbytes = n * esz
        self.off = (self.off + 63) // 64 * 64
        start = self.off
        self.off += nbytes
        assert self.off <= ARENA_BYTES, f"arena overflow {self.off}"
        v = self.ap[:, start // 2:(start + nbytes) // 2]
        if esz == 4:
            v = v.bitcast(dtype)
        if len(shape) == 3:
            v = v.rearrange("p (a b) -> p a b", a=shape[1])
        return v


class _Stop(Exception):
    pass


def build_nc(stop=None, debug=False):
    nc = bass.Bass("TRN2", target_bir_lowering=False)
    P = Prog()

    def din(name, shape, dt):
        return nc.dram_tensor(name, shape, dt, kind="ExternalInput").ap()

    x_in = din("x", [T, D_MODEL], F32)
    cT_in = din("cT", [128, 8], F32)
    vecs_in = din("vecs", [128, DEPTH * NV], F32)
    wmod_in = din("w_mod", [DEPTH, D_MODEL, 3 * D_MODEL], F32)
    win_in = din("w_in", [DEPTH, D_MODEL, 9216], F32)
    wg_in = din("wg", [128, DEPTH * 16 * 128], F32)
    wpa_in = din("w_pa", [DEPTH, 512, D_MODEL], F32)
    wpb_in = din("w_pb", [DEPTH, D_MODEL, D_MODEL], F32)
    wo_in = din("w_o", [DEPTH, D_MODEL, D_MODEL], F32)
    consts_in = din("consts", [128, 128 + 256 + 256 + 256], F32)
    idx_in = din("idx", [128, 16], I32)
    flags_in = din("flags", [128, 8], F32)
    out = nc.dram_tensor("out", [T, D_MODEL], F32, kind="ExternalOutput").ap()

    def dscr(name, shape, dt):
        if debug and not name.startswith(("pkg", "gath")):
            return nc.dram_tensor(name, shape, dt, kind="ExternalOutput").ap()
        return nc.dram_tensor(name, shape, dt).ap()

    xT_s = dscr("xT_s", [8, 128, T], F32)
    QT_s = dscr("QT_s", [12, 128, T], BF16)
    KT_s = dscr("KT_s", [12, 128, T], BF16)
    Vs = dscr("Vs", [3, 16, 128, 512], BF16)
    GA_s = dscr("GA_s", [4, 128, T], BF16)
    UT_s = dscr("UT_s", [8, 128, T], F32)
    GL_s = dscr("GL_s", [8, 128, T], BF16)
    MG_s = dscr("MG_s", [16, 128, T], BF16)
    A_s = dscr("A_s", [8, 128, T], F32)
    X_s = dscr("X_s", [8, 128, T], F32)
    pkg = dscr("pkg", [12 * 128, 2048], BF16)
    gaths = [dscr(f"gathkv{i}", [4 * 256, 2048], BF16) for i in range(6)]
    pkg2 = dscr("pkg2", [128, 24], F32)
    gath2 = dscr("gath2", [4 * 128, 24], F32)
    pkg3 = dscr("pkg3", [128, 16], F32)
    gath3 = dscr("gath3", [4 * 128, 16], F32)
    RG = [[0, 1, 2, 3], [4, 5, 6, 7]]
    if debug:
        AT_d = dscr("AT_d", [128, 4, T], BF16)
        BT_d = dscr("BT_d", [128, 8, T], BF16)
        hT_d = dscr("hT_d", [128, 8, T], BF16)
        sm_d = dscr("sm_d", [128, 64 + 40 + 24], F32)

    es = contextlib.ExitStack()
    with es:
        def sb(name, shape, dt):
            return es.enter_context(nc.sbuf_tensor(name, shape, dt))

        arena_t = sb("arena", [128, ARENA_BYTES // 2], BF16)
        AR = Arena(arena_t[:, :])
        ident = sb("ident_sb", [128, 128], F32)
        consts_bf = sb("consts_bf", [128, 256 + 512], BF16)
        ones_bf = consts_bf[:, 0:128]
        ident_bf = consts_bf[:, 128:256]
        maskPC = consts_bf[:, 256:512]
        maskF = consts_bf[:, 512:768]
        vecs = sb("vecs_sb", [128, DEPTH * NV], F32)
        cT = sb("cTs", [128, 8], F32)
        siluc = sb("siluc", [128, 8], BF16)
        idx = sb("idxs", [128, 16], I32)
        flags = sb("flagss", [128, 8], F32)
        modv_all = sb("modv", [128, 48], F32)
        derived = sb("derived", [128, 96], F32)
        wg_bf = sb("wg_bf", [128, DEPTH * 16 * 128], BF16)
        lru_sm = sb("lru_sm", [128, 64], F32)
        g3sb = sb("g3sb", [128, 4 * 16], F32)
        onep = sb("onep", [128, 1], F32)
        psum_all = es.enter_context(nc.psum_tensor("psall", [128, 8, 512], F32))
        psums = [psum_all[:, i, :] for i in range(8)]
        sems = {e: es.enter_context(nc.semaphore(f"sem_{e}")) for e in ENGS}
        dma_sems = {e: [es.enter_context(nc.semaphore(f"dsem_{e}{i}")) for i in range(12)]
                    for e in ("sp", "pool", "act")}
        dma_sems["cc"] = [es.enter_context(nc.semaphore(f"ccsem{i}")) for i in range(4)]
        dma_sems["pe"] = dma_sems["sp"]
        dma_sems["dve"] = dma_sems["sp"]
        block = es.enter_context(nc.Block())

        def dma(q, out_ap, in_ap, r, w):
            return P.op(q, lambda e, o=out_ap, i=in_ap: e.dma_start(out=o, in_=i), r=r, w=w, dma=True)

        def mm(out_ap, lhsT, rhs, start, stop, r, w):
            return P.op("pe", lambda e, o=out_ap, l=lhsT, rr=rhs, s=start, t=stop:
                        e.matmul(o, lhsT=l, rhs=rr, start=s, stop=t), r=r, w=w)

        def act(out_ap, in_ap, func, r, w, bias=None, scale=None):
            kw = {}
            if bias is not None:
                kw["bias"] = bias
            if scale is not None:
                kw["scale"] = scale
            return P.op("act", lambda e, o=out_ap, i=in_ap, f=func, kw=kw:
                        e.activation(out=o, in_=i, func=f, **kw), r=r, w=w)

        def tt(eng, out_ap, in0, in1, op, r, w):
            return P.op(eng, lambda e, o=out_ap, a=in0, b=in1, p=op:
                        e.tensor_tensor(out=o, in0=a, in1=b, op=p), r=r, w=w)

        def ts(eng, out_ap, in0, s1, s2, op0, op1, r, w):
            if op1 is None:
                return P.op(eng, lambda e, o=out_ap, a=in0, x=s1, p0=op0:
                            e.tensor_scalar(out=o, in0=a, scalar1=x, scalar2=None, op0=p0), r=r, w=w)
            return P.op(eng, lambda e, o=out_ap, a=in0, x=s1, y=s2, p0=op0, p1=op1:
                        e.tensor_scalar(out=o, in0=a, scalar1=x, scalar2=y, op0=p0, op1=p1), r=r, w=w)

        def stt(out_ap, in0, scalar, in1, op0, op1, r, w):
            return P.op("dve", lambda e, o=out_ap, a=in0, s=scalar, b=in1, p0=op0, p1=op1:
                        e.scalar_tensor_tensor(out=o, in0=a, scalar=s, in1=b, op0=p0, op1=p1), r=r, w=w)

        def cp(eng, out_ap, in_ap, r, w):
            return P.op(eng, lambda e, o=out_ap, i=in_ap: e.tensor_copy(out=o, in_=i), r=r, w=w)

        def gather(out_ap, src, col, r, w):
            return P.op("pool", lambda e, o=out_ap, s=src, c=col: e.indirect_dma_start(
                out=o, out_offset=None, in_=s,
                in_offset=bass.IndirectOffsetOnAxis(ap=idx[:, c:c + 1], axis=0)), r=r, w=w, dma=True)

        def allgather(src, dst, r, w):
            return P.op("pool", lambda e, s=src, d=dst: e.collective_compute(
                "AllGather", ALU.bypass, replica_groups=RG, ins=[s], outs=[d]), r=r, w=w, dma=True, inc=1)

        try:
            dma("sp", ident[:, :], consts_in[:, 0:128], r=[], w=["ident"])
            dma("pool", consts_bf[:, :], consts_in[:, 128:896], r=[], w=["cbf"])
            dma("sp", vecs[:, :], vecs_in, r=[], w=["vecs"])
            dma("sp", cT[:, :], cT_in, r=[], w=["cT"])
            dma("sp", idx[:, :], idx_in, r=[], w=["idx"])
            dma("sp", flags[:, :], flags_in, r=[], w=["flags"])
            dma("pool", wg_bf[:, :], wg_in, r=[], w=["wg"])
            act(siluc[:, :], cT[:, :], AF.Silu, r=["cT"], w=["siluc"])
            P.op("dve", lambda e: e.memset(onep[:, :], 1.0000001), r=[], w=["onep"])

            AR.reset(0)
            xt_b = [AR.alloc([128, 4, 1024], F32) for _ in range(2)]
            xst_b = [AR.alloc([128, 8, 512], F32) for _ in range(2)]
            for s in range(4):
                xt = xt_b[s % 2]
                xst = xst_b[s % 2]
                dma("sp", xt, x_in[s * 512:(s + 1) * 512, :].rearrange("(t p) f -> p t f", p=128),
                    r=[], w=[f"xt{s % 2}"])
                for k in range(8):
                    ps = psums[k % 4]
                    for t in range(4):
                        P.op("pe", lambda e, o=ps[:, t * 128:(t + 1) * 128], i=xt[:, t, k * 128:(k + 1) * 128]:
                             e.transpose(o, i, ident[:, :]),
                             r=[f"xt{s % 2}", "ident"], w=[f"ps{k % 4}"])
                    if k % 2 == 0:
                        cp("dve", xst[:, k, :], ps[:, :], r=[f"ps{k % 4}"], w=[f"xst{s % 2}"])
                    else:
                        act(xst[:, k, :], ps[:, :], AF.Copy, r=[f"ps{k % 4}"], w=[f"xst{s % 2}"])
                dma("sp", xT_s.rearrange("k p t -> p k t")[:, :, s * 512:(s + 1) * 512], xst,
                    r=[f"xst{s % 2}"], w=[f"xT_s{s}"])
            for l in range(DEPTH):
                vo = l * NV
                dv_ = derived[:, l * 48:(l + 1) * 48]
                Apre = dv_[:, 0:8]
                shiftv = dv_[:, 8:16]
                Gp = dv_[:, 16:24]
                Cv = dv_[:, 24:32]
                dtmp = dv_[:, 32:40]
                C2v = dv_[:, 40:48]
                modv = modv_all[:, l * 24:(l + 1) * 24]
                AR.reset(64 * 1024 + l * 32 * 1024)
                wm_b = [AR.alloc([128, 8, 1024], BF16) for _ in range(2)]
                for m in range(3):
                    wm = wm_b[m % 2]
                    dma("pool", wm, wmod_in[l].rearrange("(k p) c -> p k c", p=128)[:, :, m * 1024:(m + 1) * 1024],
                        r=[], w=[f"wm{l}_{m % 2}"])
                    ps = psums[4 + m % 2]
                    for oc in range(8):
                        for k in range(8):
                            mm(ps[:, oc:oc + 1], wm[:, k, oc * 128:(oc + 1) * 128], siluc[:, k:k + 1],
                               k == 0, k == 7, r=[f"wm{l}_{m % 2}", "siluc"], w=[f"ps{4 + m % 2}"])
                    tt("dve", modv[:, m * 8:(m + 1) * 8], ps[:, 0:8], vecs[:, vo + V_BMOD + m * 8: vo + V_BMOD + (m + 1) * 8],
                       ALU.add, r=[f"ps{4 + m % 2}", "vecs"], w=["modv"])
                cp("dve", shiftv, modv[:, 0:8], r=["modv"], w=["derived"])
                ts("dve", dtmp, modv[:, 8:16], 1.0, None, ALU.add, None, r=["modv"], w=["derived"])
                tt("dve", Apre, dtmp, vecs[:, vo + V_GPRE: vo + V_GPRE + 8], ALU.mult, r=["vecs", "derived"], w=["derived"])
                tt("dve", Gp, modv[:, 16:24], vecs[:, vo + V_GPOST: vo + V_GPOST + 8], ALU.mult,
                   r=["vecs", "modv"], w=["derived"])
                zz = lru_sm[:, 56:64]
                act(zz, vecs[:, vo + V_LAM: vo + V_LAM + 8], AF.Exp, r=["vecs"], w=["zz"], scale=-1.0)
                ts("dve", dtmp, zz, -0.25, 1.0 / 3.0, ALU.mult, ALU.add, r=["zz"], w=["derived"])
                tt("dve", dtmp, dtmp, zz, ALU.mult, r=["zz", "derived"], w=["derived"])
                ts("dve", dtmp, dtmp, -1.0, 0.5, ALU.mult, ALU.add, r=["derived"], w=["derived"])
                tt("dve", dtmp, dtmp, zz, ALU.mult, r=["zz", "derived"], w=["derived"])
                ts("dve", dtmp, dtmp, -1.0, 1.0, ALU.mult, ALU.add, r=["derived"], w=["derived"])
                tt("dve", dtmp, dtmp, zz, ALU.mult, r=["zz", "derived"], w=["derived"])
                ts("dve", Cv, dtmp, -8.0, None, ALU.mult, None, r=["derived"], w=["derived"])
                ts("dve", C2v, dtmp, -16.0, None, ALU.mult, None, r=["derived"], w=["derived"])
            P.barrier()
            if stop == 1:
                raise _Stop()

            for l in range(DEPTH):
                vo = l * NV
                last = (l == DEPTH - 1)
                dv_ = derived[:, l * 48:(l + 1) * 48]
                Apre = dv_[:, 0:8]
                shiftv = dv_[:, 8:16]
                Gp = dv_[:, 16:24]
                Cv = dv_[:, 24:32]
                dtmp = dv_[:, 32:40]
                C2v = dv_[:, 40:48]
                modv = modv_all[:, l * 24:(l + 1) * 24]

                AR.reset(0)
                hT = AR.alloc([128, 8, T], BF16)
                offA = AR.off
                xs_b = [AR.alloc([128, 8, 512], F32) for _ in range(2)]
                sq_b = [AR.alloc([128, 8, 512], BF16) for _ in range(2)]
                rstd_b = [AR.alloc([128, 512], F32) for _ in range(2)]
                tmp_b = [AR.alloc([128, 512], F32) for _ in range(2)]
                for s in range(4):
                    b = s % 2
                    xs, sq, rstd = xs_b[b], sq_b[b], rstd_b[b]
                    dma("sp", xs, xT_s.rearrange("k p t -> p k t")[:, :, s * 512:(s + 1) * 512],
                        r=[f"xT_s{s}"], w=[f"xs{b}"])
                    act(sq, xs, AF.Square, r=[f"xs{b}"], w=[f"sq{b}"])
                    ps = psums[b]
                    for k in range(8):
                        mm(ps[:, :], ones_bf, sq[:, k, :], k == 0, k == 7, r=[f"sq{b}", "cbf"], w=[f"ps{b}"])
                    act(rstd, ps[:, :], AF.Sqrt, r=[f"ps{b}"], w=[f"rstd{b}"], bias=EPS, scale=1.0 / D_MODEL)
                    P.op("dve", lambda e, o=rstd: e.reciprocal(out=o, in_=o), r=[f"rstd{b}"], w=[f"rstd{b}"])
                    for k in range(8):
                        tb = k % 2
                        stt(tmp_b[tb], xs[:, k, :], Apre[:, k:k + 1], rstd, ALU.mult, ALU.mult,
                            r=[f"xs{b}", f"rstd{b}", "derived"], w=[f"tmpA{tb}"])
                        act(hT[:, k, s * 512:(s + 1) * 512], tmp_b[tb], AF.Identity,
                            r=[f"tmpA{tb}", "derived"], w=["hT"], bias=shiftv[:, k:k + 1])
                P.barrier()
                if stop == 3:
                    raise _Stop()

                if debug and l == 0:
                    dma("sp", hT_d, hT, r=["hT"], w=["hT_d"])
                AR.reset(offA)
                wb_b = [AR.alloc([128, 8, 512], BF16) for _ in range(3)]
                st_b = [AR.alloc([128, T], F32) for _ in range(3)]
                vst_b = [AR.alloc([128, 4, 512], BF16) for _ in range(2)]
                ubuf = AR.alloc([128, T + 64], F32)
                uc_b = [AR.alloc([128, T], F32) for _ in range(2)]
                ucbf_b = [AR.alloc([128, T], BF16) for _ in range(2)]
                rr = AR.alloc([128, T], F32)
                ii = AR.alloc([128, T], F32)
                aa = AR.alloc([128, T], F32)
                t1 = AR.alloc([128, T], F32)
                t2 = AR.alloc([128, T], F32)
                xin = AR.alloc([128, T], F32)
                hl = AR.alloc([128, T], F32)
                pk3 = lru_sm[:, 0:16]
                hin = lru_sm[:, 16:24]
                rsum = lru_sm[:, 24:32]
                ut = lru_sm[:, 32:56]
                ut3 = ut.rearrange("p (k t) -> p k t", k=8)
                win_v = win_in[l].rearrange("(k p) c -> p k c", p=128)
                groups = [("u", 0, 5120), ("u", 1, 5632)]
                for g in range(3):
                    groups.append(("k", g, 1536 + g * 512))
                for g in range(3):
                    groups.append(("v", g, 3072 + g * 512))
                for g in range(3):
                    groups.append(("q", g, g * 512))
                groups.append(("ga", 0, 4608))
                groups.append(("gl", 0, 6144))
                groups.append(("gl", 1, 6656))
                for i in range(4):
                    groups.append(("mg", i, 7168 + i * 512))

                def utail_exchange():
                    dma("sp", pkg2.rearrange("p (k t) -> p k t", k=8), UT_s.rearrange("k p t -> p k t")[:, :, T - 3:T],
                        r=[f"UT{k}" for k in range(8)], w=["pkg2"])
                    allgather(pkg2, gath2, r=["pkg2"], w=["gath2"])
                    gather(ut, gath2, 12, r=["gath2", "idx"], w=["ut"])
                    ts("dve", ut, ut, flags[:, 0:1], None, ALU.mult, None, r=["ut", "flags"], w=["ut"])

                def halo_pack(ci):
                    if ci in (0, 1):
                        for h in (2 * ci, 2 * ci + 1):
                            dma("pool", pkg[h * 128:(h + 1) * 128, :], KT_s[8 + h], r=[f"KT{8 + h}"], w=[f"pkg{ci}"])
                    elif ci in (2, 3):
                        for h in (2 * (ci - 2), 2 * (ci - 2) + 1):
                            dma("pool", pkg[(4 + h) * 128:(5 + h) * 128, :].rearrange("p (t c) -> p t c", t=4),
                                Vs[2, 4 * h:4 * h + 4].rearrange("t p c -> p t c"), r=["Vs2"], w=[f"pkg{ci}"])
                    elif ci == 4:
                        for h in range(4):
                            dma("pool", pkg[8 * 128:9 * 128, h * 512:(h + 1) * 512].rearrange("p (r i) -> p r i", r=4),
                                KT_s[4 + h].rearrange("p (r i) -> p r i", r=4)[:, :, 384:512],
                                r=[f"KT{4 + h}"], w=[f"pkg{ci}"])
                            dma("pool", pkg[9 * 128:10 * 128, h * 128:(h + 1) * 128], KT_s[h][:, 1920:2048],
                                r=[f"KT{h}"], w=[f"pkg{ci}"])
                    else:
                        for r_ in range(4):
                            dma("pool", pkg[10 * 128:11 * 128, r_ * 512:(r_ + 1) * 512], Vs[1, r_ * 4 + 3],
                                r=["Vs1"], w=[f"pkg{ci}"])
                        dma("pool", pkg[11 * 128:12 * 128, 0:512], Vs[0, 15], r=["Vs0"], w=[f"pkg{ci}"])
                    allgather(pkg[ci * 256:(ci + 1) * 256, :], gaths[ci], r=[f"pkg{ci}"], w=[f"gath{ci}"])

                def lru_stage1(k):
                    b = k % 2
                    uc, ucbf = uc_b[b], ucbf_b[b]
                    cp("dve", ubuf[:, 61:64], ut3[:, k, :], r=["ut"], w=["ubuf"])
                    dma("pool", ubuf[:, 64:64 + T], UT_s[k], r=[f"UT{k}"], w=["ubuf"])
                    cw = lambda j: vecs[:, vo + V_CONVW + j * 8 + k: vo + V_CONVW + j * 8 + k + 1]
                    ts("dve", uc, ubuf[:, 64:64 + T], cw(0), vecs[:, vo + V_CONVB + k: vo + V_CONVB + k + 1],
                       ALU.mult, ALU.add, r=["ubuf", "vecs"], w=[f"uc{b}"])
                    for j in range(1, 4):
                        stt(uc, ubuf[:, 64 - j:64 - j + T], cw(j), uc, ALU.mult, ALU.add,
                            r=["ubuf", "vecs", f"uc{b}"], w=[f"uc{b}"])
                    cp("dve", ucbf, uc, r=[f"uc{b}"], w=[f"ucbf{b}"])

                def lru_stage2(k):
                    b = k % 2
                    uc, ucbf = uc_b[b], ucbf_b[b]
                    for gate in range(2):
                        wgt = wg_bf[:, ((l * 2 + gate) * 8 + k) * 128:((l * 2 + gate) * 8 + k + 1) * 128]
                        bcol = (V_BRG if gate == 0 else V_BIG) + k
                        dstt = rr if gate == 0 else ii
                        for hf in range(2):
                            for j in range(2):
                                s = hf * 2 + j
                                mm(psums[6 + j], wgt, ucbf[:, s * 512:(s + 1) * 512], True, True,
                                   r=[f"ucbf{b}", "wg"], w=[f"ps{6 + j}"])
                            act(dstt[:, hf * 1024:(hf + 1) * 1024].rearrange("p (a c) -> p a c", a=2),
                                psum_all[:, 6:8, :], AF.Sigmoid, r=["ps6", "ps7", "vecs"],
                                w=["rr" if gate == 0 else "ii"], bias=vecs[:, vo + bcol: vo + bcol + 1])
                    P.op("dve", lambda e, o=rsum[:, k:k + 1], i=rr: e.reduce_sum(out=o, in_=i, axis=mybir.AxisListType.X),
                         r=["rr"], w=["rsum"])
                    act(aa, rr, AF.Exp, r=["rr", "derived"], w=["aa"], scale=Cv[:, k:k + 1])
                    act(t1, rr, AF.Exp, r=["rr", "derived"], w=["t1"], scale=C2v[:, k:k + 1])
                    act(t1, t1, AF.Sqrt, r=["t1"], w=["t1"], scale=-1.0, bias=onep[:, 0:1])
                    tt("dve", t2, ii, uc, ALU.mult, r=["ii", f"uc{b}"], w=["t2"])
                    tt("dve", xin, t1, t2, ALU.mult, r=["t1", "t2"], w=["xin"])
                    P.op("dve", lambda e, o=hl, a=aa, x=xin: e.tensor_tensor_scan(
                        out=o, data0=a, data1=x, initial=0.0, op0=ALU.mult, op1=ALU.add), r=["aa", "xin"], w=["hl"])
                    cp("dve", pk3[:, k:k + 1], hl[:, T - 1:T], r=["hl"], w=["pk3"])
                    dma("pool", A_s[k], aa, r=["aa"], w=[f"A_s{k}"])
                    dma("pool", X_s[k], xin, r=["xin"], w=[f"X_s{k}"])

                S1_AT = {2 + (3 * c_) // 2: c_ for c_ in range(8)}
                S2_AT = {3 + (3 * c_) // 2: c_ for c_ in range(8)}

                def lru_exchange():
                    tt("dve", rsum, rsum, Cv, ALU.mult, r=["rsum", "derived"], w=["rsum"])
                    act(pk3[:, 8:16], rsum, AF.Exp, r=["rsum"], w=["pk3"])
                    dma("pool", pkg3, pk3, r=["pk3"], w=["pkg3"])
                    allgather(pkg3, gath3, r=["pkg3"], w=["gath3"])
                    g3 = g3sb[:, :].rearrange("p (s c) -> p s c", s=4)
                    dma("pool", g3, gath3.rearrange("(s p) c -> p s c", p=128), r=["gath3"], w=["g3"])
                    Fa = lru_sm[:, 56:64]
                    cp("dve", Fa, g3[:, 0, 0:8], r=["g3"], w=["Fa"])
                    ts("dve", hin, Fa, flags[:, 1:2], None, ALU.mult, None, r=["Fa", "flags"], w=["hin"])
                    for j in (1, 2):
                        tt("dve", Fa, Fa, g3[:, j, 8:16], ALU.mult, r=["Fa", "g3"], w=["Fa"])
                        tt("dve", Fa, Fa, g3[:, j, 0:8], ALU.add, r=["Fa", "g3"], w=["Fa"])
                        stt(hin, Fa, flags[:, 1 + j:2 + j], hin, ALU.mult, ALU.add, r=["Fa", "flags", "hin"], w=["hin"])

                stcount = 0
                pscount = 0
                vcount = 0
                def wload(gx):
                    if gx < len(groups):
                        dma("pool", wb_b[gx % 3], win_v[:, :, groups[gx][2]:groups[gx][2] + 512], r=[], w=[f"wb{gx % 3}"])
                wload(0)
                wload(1)
                for gidx, (kind, gi, c0) in enumerate(groups):
                    wbi = gidx % 3
                    wb = wb_b[wbi]
                    wload(gidx + 2)
                    if kind == "v":
                        D = GROUP_DIL[gi]
                        nb = 16 // D
                        for tq in range(4):
                            vb = vcount % 2
                            vcount += 1
                            vst = vst_b[vb]
                            for ti in range(4):
                                tidx = tq * 4 + ti
                                r_, n_ = tidx // nb, tidx % nb
                                pi = pscount % 6
                                pscount += 1
                                ps = psums[pi]
                                t0 = D * 128 * n_ + r_
                                if os.environ.get("V_NOSTRIDE"):
                                    D = 1
                                    t0 = 128 * tidx
                                for k in range(8):
                                    mm(ps[:, :], hT[:, k, t0:t0 + D * 127 + 1:D], wb[:, k, :], k == 0, k == 7,
                                       r=["hT", f"wb{wbi}"], w=[f"ps{pi}"])
                                act(vst[:, ti, :], ps[:, :], AF.Copy, r=[f"ps{pi}"], w=[f"vst{vb}"])
                            dma("sp", Vs[gi, tq * 4:(tq + 1) * 4].rearrange("t p c -> p t c"), vst,
                                r=[f"vst{vb}"], w=[f"Vs{gi}"])
                    else:
                        for cc in range(4):
                            sbi = stcount % 3
                            stcount += 1
                            if kind in ("q", "k", "ga", "gl", "mg"):
                                st = st_b[sbi][:, 0:T // 2].bitcast(BF16)
                            else:
                                st = st_b[sbi]
                            for half in range(2):
                                banks = ((pscount % 6), ((pscount + 1) % 6))
                                pscount += 2
                                merged = not (kind in ("q", "k") and GROUP_DIL[gi] != 1)
                                for k in range(8):
                                    for j in range(2):
                                        s = half * 2 + j
                                        mm(psums[banks[j]][:, :], wb[:, k, cc * 128:(cc + 1) * 128],
                                           hT[:, k, s * 512:(s + 1) * 512], k == 0, k == 7,
                                           r=["hT", f"wb{wbi}"], w=[f"ps{banks[j]}"])
                                if merged:
                                    fn_ = {"q": AF.Copy, "k": AF.Copy, "u": AF.Copy, "ga": AF.Silu, "gl": AF.Silu,
                                           "mg": AF.Sigmoid}[kind]
                                    act(st[:, half * 1024:(half + 1) * 1024].rearrange("p (a c) -> p a c", a=2),
                                        psum_all[:, banks[0]:banks[0] + 2, :], fn_,
                                        r=[f"ps{banks[0]}", f"ps{banks[1]}"], w=[f"st{sbi}"])
                                for j in range(2):
                                    if merged:
                                        break
                                    s = half * 2 + j
                                    pi = banks[j]
                                    ps = psums[pi]
                                    if kind in ("q", "k"):
                                        D = GROUP_DIL[gi]
                                        if D == 1:
                                            o_ap = st[:, s * 512:(s + 1) * 512]
                                            i_ap = ps[:, :]
                                        else:
                                            w_ = 512 // D
                                            o_ap = st.rearrange("p (r i) -> p r i", r=D)[:, :, s * w_:(s + 1) * w_]
                                            i_ap = ps[:, :].rearrange("p (i r) -> p r i", r=D)
                                        act(o_ap, i_ap, AF.Copy, r=[f"ps{pi}"], w=[f"st{sbi}"])
                                    elif kind == "u":
                                        act(st[:, s * 512:(s + 1) * 512], ps[:, :], AF.Copy, r=[f"ps{pi}"], w=[f"st{sbi}"])
                                    elif kind in ("ga", "gl"):
                                        act(st[:, s * 512:(s + 1) * 512], ps[:, :], AF.Silu, r=[f"ps{pi}"], w=[f"st{sbi}"])
                                    else:
                                        act(st[:, s * 512:(s + 1) * 512], ps[:, :], AF.Sigmoid, r=[f"ps{pi}"],
                                            w=[f"st{sbi}"])
                            if kind == "q":
                                dst, key = QT_s[gi * 4 + cc], f"QT{gi * 4 + cc}"
                            elif kind == "k":
                                dst, key = KT_s[gi * 4 + cc], f"KT{gi * 4 + cc}"
                            elif kind == "ga":
                                dst, key = GA_s[cc], f"GA{cc}"
                            elif kind == "u":
                                dst, key = UT_s[gi * 4 + cc], f"UT{gi * 4 + cc}"
                            elif kind == "gl":
                                dst, key = GL_s[gi * 4 + cc], f"GL{gi * 4 + cc}"
                            else:
                                dst, key = MG_s[gi * 4 + cc], f"MG{gi * 4 + cc}"
                            dma("sp", dst, st, r=[f"st{sbi}"], w=[key])
                    nl_ = os.environ.get("NO_LRU", "")
                    if nl_ == "1":
                        continue
                    if gidx == 1:
                        utail_exchange()
                    if 7 <= gidx < 13 and nl_ != "halo":
                        halo_pack(gidx - 7)
                    if nl_ == "stages":
                        continue
                    if gidx in S1_AT:
                        lru_stage1(S1_AT[gidx])
                    if gidx in S2_AT:
                        lru_stage2(S2_AT[gidx])
                    if gidx == 14:
                        lru_exchange()
                P.barrier()
                if stop == 6:
                    raise _Stop()

                TOFF = 48 * 1024
                AR.reset(0)
                AT = AR.alloc([128, 4, T], BF16)
                BT = AR.alloc([128, 8, T], BF16)
                assert AR.off <= TOFF
                AR.reset(TOFF)
                hp = [AR.alloc([128, 2048], BF16) for _ in range(4)]
                hv3 = [AR.alloc([128, 2048], BF16) for _ in range(4)]
                k3h = AR.alloc([128, 2048], BF16)
                q_b = [AR.alloc([128, T], BF16) for _ in range(2)]
                k_b = [AR.alloc([128, T], BF16) for _ in range(2)]
                v_b = [AR.alloc([128, 16, 128], BF16) for _ in range(2)]
                acc = AR.alloc([128, 2, T], F32)
                pT_b = [AR.alloc([128, 256], BF16) for _ in range(2)]
                pm_b = [AR.alloc([128, 256], BF16) for _ in range(3)]
                gab = AR.alloc([128, T], BF16)
                rden = AR.alloc([128, T], F32)
                assert AR.off <= ARENA_BYTES - 40 * 1024, AR.off
                AR.reset(ARENA_BYTES - 40 * 1024)
                wpa = AR.alloc([128, 4, 1024], BF16)
                wpb = AR.alloc([128, 8, 1024], BF16)
                wo = AR.alloc([128, 8, 1024], BF16)
                dma("pool", wpa, wpa_in[l].rearrange("(k p) c -> p k c", p=128), r=[], w=["wpa"])
                dma("pool", wpb, wpb_in[l].rearrange("(k p) c -> p k c", p=128), r=[], w=["wpb"])
                dma("pool", wo, wo_in[l].rearrange("(k p) c -> p k c", p=128), r=[], w=["wo"])
                for i in range(4):
                    gather(hp[i], gaths[(8 + i) // 2], 8 + i, r=[f"gath{(8 + i) // 2}", "idx"], w=[f"hp{i}"])
                    gather(hv3[i], gaths[(4 + i) // 2], 4 + i, r=[f"gath{(4 + i) // 2}", "idx"], w=[f"hv3{i}"])
                blk = 0
                hg = 0
                SC = 128.0 ** -0.5
                for h in range(4):
                    gather(k3h, gaths[h // 2], h, r=[f"gath{h // 2}", "idx"], w=["k3h"])
                    dma("sp", gab, GA_s[h], r=[f"GA{h}"], w=["gab"])
                    for g in range(3):
                        D = GROUP_DIL[g]
                        nb = 16 // D
                        Lr = T // D
                        b = hg % 2
                        hg += 1
                        qT, kT, vv = q_b[b], k_b[b], v_b[b]
                        dma("sp", qT, QT_s[g * 4 + h], r=[f"QT{g * 4 + h}"], w=[f"qT{b}"])
                        dma("sp", kT, KT_s[g * 4 + h], r=[f"KT{g * 4 + h}"], w=[f"kT{b}"])
                        dma("sp", vv, Vs[g].rearrange("t p c -> p t c")[:, :, h * 128:(h + 1) * 128],
                            r=[f"Vs{g}"], w=[f"vv{b}"])
                        blocks = []
                        for r_ in range(D):
                            for n_ in range(nb):
                                pos = r_ * Lr + n_ * 128
                                tidx = r_ * nb + n_
                                if n_ > 0:
                                    kprev, vprev, msk = kT[:, pos - 128:pos], vv[:, tidx - 1, :], maskPC
                                    rk, rv = [f"kT{b}"], [f"vv{b}"]
                                else:
                                    msk = maskF
                                    if g == 0:
                                        kprev, vprev = hp[1][:, h * 128:(h + 1) * 128], hp[3][:, h * 128:(h + 1) * 128]
                                        rk, rv = ["hp1"], ["hp3"]
                                    elif g == 1:
                                        kprev = hp[0][:, h * 512 + r_ * 128: h * 512 + (r_ + 1) * 128]
                                        vprev = hp[2][:, r_ * 512 + h * 128: r_ * 512 + (h + 1) * 128]
                                        rk, rv = ["hp0"], ["hp2"]
                                    else:
                                        kprev = k3h[:, r_ * 128:(r_ + 1) * 128]
                                        vprev = hv3[r_ // 4][:, (r_ % 4) * 512 + h * 128:(r_ % 4) * 512 + (h + 1) * 128]
                                        rk, rv = ["k3h"], [f"hv3{r_ // 4}"]
                                blocks.append((pos, tidx, kprev, vprev, msk, rk, rv, D * 128 * n_ + r_, blk))
                                blk += 1

                        def emit_S(bd):
                            pos, tidx, kprev, vprev, msk, rk, rv, t0, bn = bd
                            pb = bn % 3
                            pss, pm = psums[pb], pm_b[pb]
                            qblk = qT[:, pos:pos + 128]
                            mm(pss[:, 0:256], ident_bf, msk, True, False, r=["cbf"], w=[f"ps{pb}"])
                            mm(pss[:, 0:128], kprev, qblk, False, True, r=rk + [f"qT{b}"], w=[f"ps{pb}"])
                            mm(pss[:, 128:256], kT[:, pos:pos + 128], qblk, False, True,
                               r=[f"kT{b}", f"qT{b}"], w=[f"ps{pb}"])
                            act(pm, pss[:, 0:256], AF.Exp, r=[f"ps{pb}"], w=[f"pm{pb}"], scale=SC)

                        def emit_PV(bd):
                            pos, tidx, kprev, vprev, msk, rk, rv, t0, bn = bd
                            pb = bn % 3
                            po = 3 + bn % 2
                            pso, pm = psums[po], pm_b[pb]
                            mm(pso[:, 0:128], vprev, pm[:, 0:128], True, False, r=rv + [f"pm{pb}"], w=[f"ps{po}"])
                            mm(pso[:, 0:128], vv[:, tidx, :], pm[:, 128:256], False, True,
                               r=[f"vv{b}", f"pm{pb}"], w=[f"ps{po}"])
                            mm(pso[:, 128:256], ones_bf, pm[:, 0:128], True, False, r=["cbf", f"pm{pb}"], w=[f"ps{po}"])
                            mm(pso[:, 128:256], ones_bf, pm[:, 128:256], False, True,
                               r=["cbf", f"pm{pb}"], w=[f"ps{po}"])
                            av = acc[:, :, t0:t0 + D * 127 + 1:D]
                            pv = pso[:, 0:256].rearrange("p (a q) -> p a q", a=2)
                            if g == 0:
                                cp("dve", av, pv, r=[f"ps{po}"], w=["acc"])
                            else:
                                tt("dve", av, av, pv, ALU.add, r=[f"ps{po}", "acc"], w=["acc"])

                        emit_S(blocks[0])
                        emit_S(blocks[1])
                        for bi_ in range(len(blocks)):
                            if bi_ + 2 < len(blocks):
                                emit_S(blocks[bi_ + 2])
                            emit_PV(blocks[bi_])
                    P.op("dve", lambda e, o=rden, i=acc[:, 1, :]: e.reciprocal(out=o, in_=i), r=["acc"], w=["rden"])
                    tt("dve", rden, rden, acc[:, 0, :], ALU.mult, r=["acc", "rden"], w=["rden"])
                    tt("pool", AT[:, h, :], rden, gab, ALU.mult, r=["rden", "gab"], w=["AT"])
                P.barrier()
                if stop == 7:
                    raise _Stop()

                AR.reset(TOFF)
                a_b = [AR.alloc([128, T], F32) for _ in range(2)]
                x_b = [AR.alloc([128, T], F32) for _ in range(2)]
                gl_b = [AR.alloc([128, T], BF16) for _ in range(2)]
                h_b = [AR.alloc([128, T], F32) for _ in range(2)]
                for k in range(8):
                    b = k % 2
                    dma("sp", a_b[b], A_s[k], r=[f"A_s{k}"], w=[f"a2{b}"])
                    dma("sp", x_b[b], X_s[k], r=[f"X_s{k}"], w=[f"x2{b}"])
                    dma("sp", gl_b[b], GL_s[k], r=[f"GL{k}"], w=[f"gl{b}"])
                    P.op("dve", lambda e, o=h_b[b], a=a_b[b], x=x_b[b], hi=hin[:, k:k + 1]: e.tensor_tensor_scan(
                        out=o, data0=a, data1=x, initial=hi, op0=ALU.mult, op1=ALU.add),
                        r=[f"a2{b}", f"x2{b}", "hin"], w=[f"h2{b}"])
                    tt("pool", BT[:, k, :], h_b[b], gl_b[b], ALU.mult, r=[f"h2{b}", f"gl{b}"], w=["BT"])
                P.barrier()
                if stop == 8:
                    raise _Stop()

                if debug and l == 0:
                    dma("sp", AT_d, AT, r=["AT"], w=["AT_d"])
                    dma("sp", BT_d, BT, r=["BT"], w=["BT_d"])
                    dma("sp", sm_d[:, 0:64], lru_sm[:, :], r=["hin", "pk3", "ut"], w=["sm_d"])
                    dma("sp", sm_d[:, 64:104], derived[:, 0:40], r=["derived"], w=["sm_d"])
                    dma("sp", sm_d[:, 104:128], modv_all[:, 0:24], r=["modv"], w=["sm_d"])
                    P.barrier()
                AR.reset(TOFF)
                mga = AR.alloc([128, 16, 512], BF16)
                xs = AR.alloc([128, 8, 512], F32)
                ta = AR.alloc([128, 512], F32)
                tb_ = AR.alloc([128, 512], F32)
                zT = AR.alloc([128, 8, 512], BF16)
                ysq = AR.alloc([128, 8, 512], BF16)
                yT = AR.alloc([128, 8, 512], F32)
                rstd2 = AR.alloc([128, 512], F32)
                tmpf = AR.alloc([128, 512], F32)
                xn = AR.alloc([128, 8, 512], F32)
                otile = AR.alloc([128, 1024], F32)
                assert AR.off <= ARENA_BYTES - 40 * 1024, AR.off
                if stop == 8.1:
                    P.barrier()
                    raise _Stop()
                for s in range(4):
                    sl = slice(s * 512, (s + 1) * 512)
                    dma("sp", mga, MG_s.rearrange("o p t -> p o t")[:, :, sl], r=[f"MG{o}" for o in range(16)], w=["mga"])
                    dma("sp", xs, xT_s.rearrange("k p t -> p k t")[:, :, sl], r=[f"xT_s{s}"], w=["xsF"])
                    for o in range(8):
                        pa, pb_ = psums[(2 * o) % 4], psums[(2 * o + 1) % 4]
                        ka, kb = f"ps{(2 * o) % 4}", f"ps{(2 * o + 1) % 4}"
                        for h in range(4):
                            mm(pa[:, :], wpa[:, h, o * 128:(o + 1) * 128], AT[:, h, sl], h == 0, h == 3,
                               r=["wpa", "AT"], w=[ka])
                        for k in range(8):
                            mm(pb_[:, :], wpb[:, k, o * 128:(o + 1) * 128], BT[:, k, sl], k == 0, k == 7,
                               r=["wpb", "BT"], w=[kb])
                        tt("dve", ta, pa[:, :], mga[:, o, :], ALU.mult, r=[ka, "mga"], w=["ta"])
                        tt("dve", tb_, pb_[:, :], mga[:, 8 + o, :], ALU.mult, r=[kb, "mga"], w=["tb"])
                        tt("dve", zT[:, o, :], ta, tb_, ALU.add, r=["ta", "tb"], w=["zT"])
                    if stop == 8.2:
                        P.barrier()
                        raise _Stop()
                    if stop == 8.25:
                        P.barrier()
                        raise _Stop()
                    for o2 in range(8):
                        pi = 4 + o2 % 2
                        py = psums[pi]
                        for o in range(8):
                            mm(py[:, :], wo[:, o, o2 * 128:(o2 + 1) * 128], zT[:, o, :], o == 0, o == 7,
                               r=["wo", "zT"], w=[f"ps{pi}"])
                        if stop == 8.26:
                            P.barrier()
                            raise _Stop()
                        act(yT[:, o2, :], py[:, :], AF.Copy, r=[f"ps{pi}"], w=["yT"])
                        if stop == 8.27:
                            P.barrier()
                            raise _Stop()
                        act(ysq[:, o2, :], yT[:, o2, :], AF.Square, r=["yT"], w=["ysq"])
                    if stop == 8.3:
                        P.barrier()
                        raise _Stop()
                    pss_ = psums[6]
                    for o2 in range(8):
                        mm(pss_[:, :], ones_bf, ysq[:, o2, :], o2 == 0, o2 == 7, r=["ysq", "cbf"], w=["ps6"])
                    act(rstd2, pss_[:, :], AF.Sqrt, r=["ps6"], w=["rstd2"], bias=EPS, scale=1.0 / D_MODEL)
                    P.op("dve", lambda e, o=rstd2: e.reciprocal(out=o, in_=o), r=["rstd2"], w=["rstd2"])
                    for o2 in range(8):
                        stt(tmpf, yT[:, o2, :], Gp[:, o2:o2 + 1], rstd2, ALU.mult, ALU.mult,
                            r=["yT", "rstd2", "derived"], w=["tmpf"])
                        tt("dve", xn[:, o2, :], tmpf, xs[:, o2, :], ALU.add, r=["tmpf", "xsF"], w=["xn"])
                    if stop == 8.4:
                        P.barrier()
                        raise _Stop()
                    if not last:
                        dma("sp", xT_s.rearrange("k p t -> p k t")[:, :, sl], xn, r=["xn"], w=[f"xT_s{s}"])
                    else:
                        for t in range(4):
                            for half in range(2):
                                pi = 6 + half
                                pt = psums[pi]
                                for q in range(4):
                                    o2 = half * 4 + q
                                    P.op("pe", lambda e, o=pt[:, q * 128:(q + 1) * 128], i=xn[:, o2, t * 128:(t + 1) * 128]:
                                         e.transpose(o, i, ident[:, :]), r=["xn", "ident"], w=[f"ps{pi}"])
                                if half == 0:
                                    cp("dve", otile[:, 0:512], pt[:, :], r=["ps6"], w=["otile"])
                                else:
                                    act(otile[:, 512:1024], pt[:, :], AF.Copy, r=["ps7"], w=["otile"])
                            tok0 = s * 512 + t * 128
                            dma("sp", out[tok0:tok0 + 128, :], otile, r=["otile"], w=[f"out{tok0}"])
                P.barrier()
                if stop == 9:
                    raise _Stop()

        except _Stop:
            if debug:
                dma("sp", sm_d[:, 0:64], lru_sm[:, :], r=["hin", "pk3", "ut"], w=["sm_d"])
                dma("sp", sm_d[:, 64:104], derived[:, 0:40], r=["derived"], w=["sm_d"])
                dma("sp", sm_d[:, 104:128], modv_all[:, 0:24], r=["modv"], w=["sm_d"])
                P.barrier()
        P.emit(nc, block, sems, dma_sems)
    return nc


_NC_CACHE = {}


def _host_layouts(inputs):
    f = np.float32
    x = np.asarray(inputs["x"], f)
    c = np.asarray(inputs["c"], f)

    def colvec(v):
        v = np.asarray(v, f)
        return np.ascontiguousarray(v.reshape(-1, 128).T)

    vecs = np.zeros((128, DEPTH * NV), f)
    for l in range(DEPTH):
        o = l * NV
        vecs[:, o + V_BMOD:o + V_BMOD + 24] = colvec(inputs["b_mod"][l])
        vecs[:, o + V_GPRE:o + V_GPRE + 8] = colvec(inputs["g_pre"][l])
        for j in range(4):
            vecs[:, o + V_CONVW + j * 8:o + V_CONVW + (j + 1) * 8] = colvec(inputs["conv_w"][l][j])
        vecs[:, o + V_CONVB:o + V_CONVB + 8] = colvec(inputs["conv_b"][l])
        vecs[:, o + V_BRG:o + V_BRG + 8] = colvec(inputs["b_rg"][l])
        vecs[:, o + V_BIG:o + V_BIG + 8] = colvec(inputs["b_ig"][l])
        vecs[:, o + V_LAM:o + V_LAM + 8] = colvec(inputs["lru_lambda"][l])
        vecs[:, o + V_GPOST:o + V_GPOST + 8] = colvec(inputs["g_post"][l])
    wg = np.zeros((128, DEPTH, 2, 8, 128), f)
    for l in range(DEPTH):
        for gi, name in enumerate(("w_rg", "w_ig")):
            w = np.asarray(inputs[name][l], f)
            for k in range(8):
                wg[0:64, l, gi, k, 0:64] = w[2 * k]
                wg[64:128, l, gi, k, 64:128] = w[2 * k + 1]
    wg = np.ascontiguousarray(wg.reshape(128, -1))
    kk = np.arange(128)[:, None]
    qq = np.arange(128)[None, :]
    mprev = (kk >= qq).astype(f)
    mcur = (kk <= qq).astype(f)
    common = {
        "vecs": vecs, "wg": wg,
        "w_mod": np.ascontiguousarray(inputs["w_mod"], f), "w_in": np.ascontiguousarray(inputs["w_in"], f),
        "w_pa": np.ascontiguousarray(inputs["w_pa"], f), "w_pb": np.ascontiguousarray(inputs["w_pb"], f),
        "w_o": np.ascontiguousarray(inputs["w_o"], f),
    }
    in_maps = []
    for core in range(8):
        b, j = core // 4, core % 4
        hp = 1.0 if j > 0 else 0.0
        NEG = np.float32(-30000.0)
        consts = np.concatenate([np.eye(128, dtype=f), np.ones((128, 128), f), np.eye(128, dtype=f),
                                 (1 - mprev) * NEG, (1 - mcur) * NEG, (1 - mprev * hp) * NEG, (1 - mcur) * NEG], axis=1)
        slot = max(j - 1, 0)
        idx = np.zeros((128, 16), np.int32)
        for pce in range(12):
            idx[:, pce] = slot * 256 + (pce % 2) * 128 + np.arange(128)
        idx[:, 12] = slot * 128 + np.arange(128)
        flags = np.zeros((128, 8), f)
        flags[:, 0] = hp
        if j > 0:
            flags[:, j] = 1.0
        m = dict(common)
        m["x"] = np.ascontiguousarray(x[b, j * T:(j + 1) * T, :])
        m["cT"] = np.ascontiguousarray(c[b].reshape(8, 128).T)
        m["consts"] = np.ascontiguousarray(consts)
        m["idx"] = idx
        m["flags"] = flags
        in_maps.append(m)
    return in_maps


def kernel(**inputs):
    if "nc" not in _NC_CACHE:
        _NC_CACHE["nc"] = build_nc()
    nc = _NC_CACHE["nc"]
    in_maps = _host_layouts(inputs)
    res = run_bass_kernel_spmd(nc, in_maps, core_ids=list(range(8)))
    outf = np.zeros((2, 4 * T, D_MODEL), np.float32)
    for core in range(8):
        b, j = core // 4, core % 4
        outf[b, j * T:(j + 1) * T, :] = res.results[core]["out"]
    return outf
```

```python
import contextlib
import os
import numpy as np
import concourse.bass as bass
import concourse.mybir as mybir
from concourse.bass_utils import run_bass_kernel_spmd

F32 = mybir.dt.float32
BF16 = mybir.dt.bfloat16
I32 = mybir.dt.int32
ALU = mybir.AluOpType
AF = mybir.ActivationFunctionType

D_MODEL = 1024
T = 2048
DEPTH = 2
NV = 104
EPS = 1e-6
ARENA_BYTES = 186 * 1024
GROUP_DIL = (1, 4, 16)

V_BMOD, V_GPRE, V_CONVW, V_CONVB, V_BRG, V_BIG, V_LAM, V_GPOST = 0, 24, 32, 64, 72, 80, 88, 96


class Op:
    __slots__ = ("eng", "fn", "deps", "dma", "idx", "signal", "sigval", "sem", "semval", "inc")

    def __init__(self, eng, fn, dma):
        self.eng, self.fn, self.dma = eng, fn, dma
        self.deps = []
        self.signal = False
        self.sigval = None
        self.sem = None
        self.semval = None
        self.inc = 16


ENGS = ("pe", "act", "dve", "pool", "sp")


class Prog:
    def __init__(self):
        self.ops = {e: [] for e in ENGS}
        self.lastw = {}
        self.readers = {}
        self.dmas_since_barrier = []

    def op(self, eng, fn, r=(), w=(), dma=False, inc=16):
        o = Op(eng, fn, dma)
        o.inc = inc
        deps = []
        for k in r:
            lw = self.lastw.get(k)
            if lw is not None:
                deps.append(lw)
        for k in w:
            lw = self.lastw.get(k)
            if lw is not None:
                deps.append(lw)
            deps.extend(self.readers.get(k, ()))
        seen = set()
        for d in deps:
            if id(d) in seen or d is o:
                continue
            seen.add(id(d))
            if (not d.dma) and (not dma) and d.eng == eng and eng == "pe":
                continue
            o.deps.append(d)
            d.signal = True
        for k in r:
            lst = self.readers.setdefault(k, [])
            if not dma:
                lst[:] = [x_ for x_ in lst if x_.dma or x_.eng != eng]
            lst.append(o)
        for k in w:
            self.lastw[k] = o
            self.readers[k] = []
        self.ops[eng].append(o)
        if dma:
            o.signal = True
            self.dmas_since_barrier.append(o)
        return o

    def barrier(self):
        lasts = []
        for e in ENGS:
            for o in reversed(self.ops[e]):
                if o.fn is not None and not o.dma:
                    lasts.append(o)
                    break
        deps = lasts + self.dmas_since_barrier
        self.dmas_since_barrier = []
        for e in ENGS:
            b = Op(e, None, False)
            for d in deps:
                if (not d.dma) and d.eng == e:
                    continue
                b.deps.append(d)
                d.signal = True
            self.ops[e].append(b)

    def emit(self, nc, block, sems, dma_sems):
        tot = {}
        ncc = 0
        for e in ENGS:
            cnt = 0
            nd = 0
            for o in self.ops[e]:
                if o.fn is None:
                    continue
                if o.dma:
                    if o.inc == 16:
                        pool = dma_sems[e]
                        o.sem = pool[nd % len(pool)]
                        nd += 1
                    else:
                        pool = dma_sems["cc"]
                        o.sem = pool[ncc % len(pool)]
                        ncc += 1
                    prev = tot.get(id(o.sem), 0)
                    o.sigval = prev
                    o.semval = prev + o.inc
                    tot[id(o.sem)] = o.semval
                elif o.signal:
                    cnt += 1
                    o.sigval = cnt
        prog = self

        def run_engine(e, engobj):
            known = {x: 0 for x in ENGS}
            known_dma = {}
            nd = 0
            for o in prog.ops[e]:
                for d in o.deps:
                    if d.dma:
                        key = id(d.sem)
                        if known_dma.get(key, 0) >= d.semval:
                            continue
                        engobj.wait_ge(d.sem, d.semval)
                        known_dma[key] = d.semval
                    else:
                        if known[d.eng] >= d.sigval:
                            continue
                        engobj.wait_ge(sems[d.eng], d.sigval)
                        known[d.eng] = d.sigval
                if o.fn is None:
                    continue
                if o.dma:
                    pool = dma_sems[e]
                    prev = o.sigval
                    key = id(o.sem)
                    if prev > 0 and known_dma.get(key, 0) < prev:
                        engobj.wait_ge(o.sem, prev)
                        known_dma[key] = prev
                    nd += 1
                    ins = o.fn(engobj)
                    ins.then_inc(o.sem, o.inc)
                else:
                    ins = o.fn(engobj)
                    if o.signal:
                        ins.then_inc(sems[e], 1)

        @block.tensor
        def _(eng):
            run_engine("pe", eng)

        @block.scalar
        def _(eng):
            run_engine("act", eng)

        @block.vector
        def _(eng):
            run_engine("dve", eng)

        @block.gpsimd
        def _(eng):
            run_engine("pool", eng)

        @block.sync
        def _(eng):
            run_engine("sp", eng)


class Arena:
    def __init__(self, ap_bf16):
        self.ap = ap_bf16
        self.off = 0

    def reset(self, off):
        self.off = off

    def alloc(self, shape, dtype):
        esz = 4 if dtype in (F32, I32) else 2
        n = int(np.prod(shape[1:]))
        nbytes = n * esz
        self.off = (self.off + 63) // 64 * 64
        start = self.off
        self.off += nbytes
        assert self.off <= ARENA_BYTES, f"arena overflow {self.off}"
        v = self.ap[:, start // 2:(start + nbytes) // 2]
        if esz == 4:
            v = v.bitcast(dtype)
        if len(shape) == 3:
            v = v.rearrange("p (a b) -> p a b", a=shape[1])
        return v


class _Stop(Exception):
    pass


def build_nc(stop=None, debug=False):
    nc = bass.Bass("TRN2", target_bir_lowering=False)
    P = Prog()

    def din(name, shape, dt):
        return nc.dram_tensor(name, shape, dt, kind="ExternalInput").ap()

    x_in = din("x", [T, D_MODEL], F32)
    cT_in = din("cT", [128, 8], F32)
    vecs_in = din("vecs", [128, DEPTH * NV], F32)
    wmod_in = din("w_mod", [DEPTH, D_MODEL, 3 * D_MODEL], F32)
    win_in = din("w_in", [DEPTH, D_MODEL, 9216], F32)
    wg_in = din("wg", [128, DEPTH * 16 * 128], F32)
    wpa_in = din("w_pa", [DEPTH, 512, D_MODEL], F32)
    wpb_in = din("w_pb", [DEPTH, D_MODEL, D_MODEL], F32)
    wo_in = din("w_o", [DEPTH, D_MODEL, D_MODEL], F32)
    consts_in = din("consts", [128, 128 + 256 + 256 + 256], F32)
    idx_in = din("idx", [128, 16], I32)
    flags_in = din("flags", [128, 8], F32)
    out = nc.dram_tensor("out", [T, D_MODEL], F32, kind="ExternalOutput").ap()

    def dscr(name, shape, dt):
        if debug and not name.startswith(("pkg", "gath")):
            return nc.dram_tensor(name, shape, dt, kind="ExternalOutput").ap()
        return nc.dram_tensor(name, shape, dt).ap()

    xT_s = dscr("xT_s", [8, 128, T], F32)
    QT_s = dscr("QT_s", [12, 128, T], BF16)
    KT_s = dscr("KT_s", [12, 128, T], BF16)
    Vs = dscr("Vs", [3, 16, 128, 512], BF16)
    GA_s = dscr("GA_s", [4, 128, T], BF16)
    UT_s = dscr("UT_s", [8, 128, T], F32)
    GL_s = dscr("GL_s", [8, 128, T], BF16)
    MG_s = dscr("MG_s", [16, 128, T], BF16)
    A_s = dscr("A_s", [8, 128, T], F32)
    X_s = dscr("X_s", [8, 128, T], F32)
    pkg = dscr("pkg", [12 * 128, 2048], BF16)
    gaths = [dscr(f"gathkv{i}", [4 * 256, 2048], BF16) for i in range(6)]
    pkg2 = dscr("pkg2", [128, 24], F32)
    gath2 = dscr("gath2", [4 * 128, 24], F32)
    pkg3 = dscr("pkg3", [128, 16], F32)
    gath3 = dscr("gath3", [4 * 128, 16], F32)
    RG = [[0, 1, 2, 3], [4, 5, 6, 7]]
    if debug:
        AT_d = dscr("AT_d", [128, 4, T], BF16)
        BT_d = dscr("BT_d", [128, 8, T], BF16)
        hT_d = dscr("hT_d", [128, 8, T], BF16)
        sm_d = dscr("sm_d", [128, 64 + 40 + 24], F32)

    es = contextlib.ExitStack()
    with es:
        def sb(name, shape, dt):
            return es.enter_context(nc.sbuf_tensor(name, shape, dt))

        arena_t = sb("arena", [128, ARENA_BYTES // 2], BF16)
        AR = Arena(arena_t[:, :])
        ident = sb("ident_sb", [128, 128], F32)
        consts_bf = sb("consts_bf", [128, 256 + 512], BF16)
        ones_bf = consts_bf[:, 0:128]
        ident_bf = consts_bf[:, 128:256]
        maskPC = consts_bf[:, 256:512]
        maskF = consts_bf[:, 512:768]
        vecs = sb("vecs_sb", [128, DEPTH * NV], F32)
        cT = sb("cTs", [128, 8], F32)
        siluc = sb("siluc", [128, 8], BF16)
        idx = sb("idxs", [128, 16], I32)
        flags = sb("flagss", [128, 8], F32)
        modv_all = sb("modv", [128, 48], F32)
        derived = sb("derived", [128, 96], F32)
        wg_bf = sb("wg_bf", [128, DEPTH * 16 * 128], BF16)
        lru_sm = sb("lru_sm", [128, 64], F32)
        g3sb = sb("g3sb", [128, 4 * 16], F32)
        onep = sb("onep", [128, 1], F32)
        psum_all = es.enter_context(nc.psum_tensor("psall", [128, 8, 512], F32))
        psums = [psum_all[:, i, :] for i in range(8)]
        sems = {e: es.enter_context(nc.semaphore(f"sem_{e}")) for e in ENGS}
        dma_sems = {e: [es.enter_context(nc.semaphore(f"dsem_{e}{i}")) for i in range(12)]
                    for e in ("sp", "pool", "act")}
        dma_sems["cc"] = [es.enter_context(nc.semaphore(f"ccsem{i}")) for i in range(4)]
        dma_sems["pe"] = dma_sems["sp"]
        dma_sems["dve"] = dma_sems["sp"]
        block = es.enter_context(nc.Block())

        def dma(q, out_ap, in_ap, r, w):
            return P.op(q, lambda e, o=out_ap, i=in_ap: e.dma_start(out=o, in_=i), r=r, w=w, dma=True)

        def mm(out_ap, lhsT, rhs, start, stop, r, w):
            return P.op("pe", lambda e, o=out_ap, l=lhsT, rr=rhs, s=start, t=stop:
                        e.matmul(o, lhsT=l, rhs=rr, start=s, stop=t), r=r, w=w)

        def act(out_ap, in_ap, func, r, w, bias=None, scale=None):
            kw = {}
            if bias is not None:
                kw["bias"] = bias
            if scale is not None:
                kw["scale"] = scale
            return P.op("act", lambda e, o=out_ap, i=in_ap, f=func, kw=kw:
                        e.activation(out=o, in_=i, func=f, **kw), r=r, w=w)

        def tt(eng, out_ap, in0, in1, op, r, w):
            return P.op(eng, lambda e, o=out_ap, a=in0, b=in1, p=op:
                        e.tensor_tensor(out=o, in0=a, in1=b, op=p), r=r, w=w)

        def ts(eng, out_ap, in0, s1, s2, op0, op1, r, w):
            if op1 is None:
                return P.op(eng, lambda e, o=out_ap, a=in0, x=s1, p0=op0:
                            e.tensor_scalar(out=o, in0=a, scalar1=x, scalar2=None, op0=p0), r=r, w=w)
            return P.op(eng, lambda e, o=out_ap, a=in0, x=s1, y=s2, p0=op0, p1=op1:
                        e.tensor_scalar(out=o, in0=a, scalar1=x, scalar2=y, op0=p0, op1=p1), r=r, w=w)

        def stt(out_ap, in0, scalar, in1, op0, op1, r, w):
            return P.op("dve", lambda e, o=out_ap, a=in0, s=scalar, b=in1, p0=op0, p1=op1:
                        e.scalar_tensor_tensor(out=o, in0=a, scalar=s, in1=b, op0=p0, op1=p1), r=r, w=w)

        def cp(eng, out_ap, in_ap, r, w):
            return P.op(eng, lambda e, o=out_ap, i=in_ap: e.tensor_copy(out=o, in_=i), r=r, w=w)

        def gather(out_ap, src, col, r, w):
            return P.op("pool", lambda e, o=out_ap, s=src, c=col: e.indirect_dma_start(
                out=o, out_offset=None, in_=s,
                in_offset=bass.IndirectOffsetOnAxis(ap=idx[:, c:c + 1], axis=0)), r=r, w=w, dma=True)

        def allgather(src, dst, r, w):
            return P.op("pool", lambda e, s=src, d=dst: e.collective_compute(
                "AllGather", ALU.bypass, replica_groups=RG, ins=[s], outs=[d]), r=r, w=w, dma=True, inc=1)

        try:
            dma("sp", ident[:, :], consts_in[:, 0:128], r=[], w=["ident"])
            dma("pool", consts_bf[:, :], consts_in[:, 128:896], r=[], w=["cbf"])
            dma("sp", vecs[:, :], vecs_in, r=[], w=["vecs"])
            dma("sp", cT[:, :], cT_in, r=[], w=["cT"])
            dma("sp", idx[:, :], idx_in, r=[], w=["idx"])
            dma("sp", flags[:, :], flags_in, r=[], w=["flags"])
            dma("pool", wg_bf[:, :], wg_in, r=[], w=["wg"])
            act(siluc[:, :], cT[:, :], AF.Silu, r=["cT"], w=["siluc"])
            P.op("dve", lambda e: e.memset(onep[:, :], 1.0000001), r=[], w=["onep"])

            AR.reset(128 * 1024)
            zpad = AR.alloc([128, 1536], BF16)
            P.op("dve", lambda e, o=zpad: e.memset(o, 0.0), r=[], w=["zpad"])
            dma("sp", pkg[9 * 128:10 * 128, 512:2048], zpad, r=["zpad"], w=["pkgpad"])
            dma("sp", pkg[11 * 128:12 * 128, 512:2048], zpad, r=["zpad"], w=["pkgpad"])
            AR.reset(0)
            xt_b = [AR.alloc([128, 4, 1024], F32) for _ in range(2)]
            xst_b = [AR.alloc([128, 8, 512], F32) for _ in range(2)]
            for s in range(4):
                xt = xt_b[s % 2]
                xst = xst_b[s % 2]
                dma("sp", xt, x_in[s * 512:(s + 1) * 512, :].rearrange("(t p) f -> p t f", p=128),
                    r=[], w=[f"xt{s % 2}"])
                for k in range(8):
                    ps = psums[k % 4]
                    for t in range(4):
                        P.op("pe", lambda e, o=ps[:, t * 128:(t + 1) * 128], i=xt[:, t, k * 128:(k + 1) * 128]:
                             e.transpose(o, i, ident[:, :]),
                             r=[f"xt{s % 2}", "ident"], w=[f"ps{k % 4}"])
                    if k % 2 == 0:
                        cp("dve", xst[:, k, :], ps[:, :], r=[f"ps{k % 4}"], w=[f"xst{s % 2}"])
                    else:
                        act(xst[:, k, :], ps[:, :], AF.Copy, r=[f"ps{k % 4}"], w=[f"xst{s % 2}"])
                dma("sp", xT_s.rearrange("k p t -> p k t")[:, :, s * 512:(s + 1) * 512], xst,
                    r=[f"xst{s % 2}"], w=[f"xT_s{s}"])
            for l in range(DEPTH):
                vo = l * NV
                dv_ = derived[:, l * 48:(l + 1) * 48]
                Apre = dv_[:, 0:8]
                shiftv = dv_[:, 8:16]
                Gp = dv_[:, 16:24]
                Cv = dv_[:, 24:32]
                dtmp = dv_[:, 32:40]
                C2v = dv_[:, 40:48]
                modv = modv_all[:, l * 24:(l + 1) * 24]
                AR.reset(64 * 1024 + l * 32 * 1024)
                wm_b = [AR.alloc([128, 8, 1024], BF16) for _ in range(2)]
                for m in range(3):
                    wm = wm_b[m % 2]
                    dma("pool", wm, wmod_in[l].rearrange("(k p) c -> p k c", p=128)[:, :, m * 1024:(m + 1) * 1024],
                        r=[], w=[f"wm{l}_{m % 2}"])
                    ps = psums[4 + m % 2]
                    for oc in range(8):
                        for k in range(8):
                            mm(ps[:, oc:oc + 1], wm[:, k, oc * 128:(oc + 1) * 128], siluc[:, k:k + 1],
                               k == 0, k == 7, r=[f"wm{l}_{m % 2}", "siluc"], w=[f"ps{4 + m % 2}"])
                    tt("dve", modv[:, m * 8:(m + 1) * 8], ps[:, 0:8], vecs[:, vo + V_BMOD + m * 8: vo + V_BMOD + (m + 1) * 8],
                       ALU.add, r=[f"ps{4 + m % 2}", "vecs"], w=["modv"])
                cp("dve", shiftv, modv[:, 0:8], r=["modv"], w=["derived"])
                ts("dve", dtmp, modv[:, 8:16], 1.0, None, ALU.add, None, r=["modv"], w=["derived"])
                tt("dve", Apre, dtmp, vecs[:, vo + V_GPRE: vo + V_GPRE + 8], ALU.mult, r=["vecs", "derived"], w=["derived"])
                tt("dve", Gp, modv[:, 16:24], vecs[:, vo + V_GPOST: vo + V_GPOST + 8], ALU.mult,
                   r=["vecs", "modv"], w=["derived"])
                zz = lru_sm[:, 56:64]
                act(zz, vecs[:, vo + V_LAM: vo + V_LAM + 8], AF.Exp, r=["vecs"], w=["zz"], scale=-1.0)
                ts("dve", dtmp, zz, -0.25, 1.0 / 3.0, ALU.mult, ALU.add, r=["zz"], w=["derived"])
                tt("dve", dtmp, dtmp, zz, ALU.mult, r=["zz", "derived"], w=["derived"])
                ts("dve", dtmp, dtmp, -1.0, 0.5, ALU.mult, ALU.add, r=["derived"], w=["derived"])
                tt("dve", dtmp, dtmp, zz, ALU.mult, r=["zz", "derived"], w=["derived"])
                ts("dve", dtmp, dtmp, -1.0, 1.0, ALU.mult, ALU.add, r=["derived"], w=["derived"])
                tt("dve", dtmp, dtmp, zz, ALU.mult, r=["zz", "derived"], w=["derived"])
                ts("dve", Cv, dtmp, -8.0, None, ALU.mult, None, r=["derived"], w=["derived"])
                ts("dve", C2v, dtmp, -16.0, None, ALU.mult, None, r=["derived"], w=["derived"])
            P.barrier()
            if stop == 1:
                raise _Stop()

            for l in range(DEPTH):
                vo = l * NV
                last = (l == DEPTH - 1)
                dv_ = derived[:, l * 48:(l + 1) * 48]
                Apre = dv_[:, 0:8]
                shiftv = dv_[:, 8:16]
                Gp = dv_[:, 16:24]
                Cv = dv_[:, 24:32]
                dtmp = dv_[:, 32:40]
                C2v = dv_[:, 40:48]
                modv = modv_all[:, l * 24:(l + 1) * 24]

                AR.reset(0)
                hT = AR.alloc([128, 8, T], BF16)
                offA = AR.off
                xs_b = [AR.alloc([128, 8, 512], F32) for _ in range(2)]
                sq_b = [AR.alloc([128, 8, 512], BF16) for _ in range(2)]
                rstd_b = [AR.alloc([128, 512], F32) for _ in range(2)]
                tmp_b = [AR.alloc([128, 512], F32) for _ in range(2)]
                for s in range(4):
                    b = s % 2
                    xs, sq, rstd = xs_b[b], sq_b[b], rstd_b[b]
                    dma("sp", xs, xT_s.rearrange("k p t -> p k t")[:, :, s * 512:(s + 1) * 512],
                        r=[f"xT_s{s}"], w=[f"xs{b}"])
                    act(sq, xs, AF.Square, r=[f"xs{b}"], w=[f"sq{b}"])
                    ps = psums[b]
                    for k in range(8):
                        mm(ps[:, :], ones_bf, sq[:, k, :], k == 0, k == 7, r=[f"sq{b}", "cbf"], w=[f"ps{b}"])
                    act(rstd, ps[:, :], AF.Sqrt, r=[f"ps{b}"], w=[f"rstd{b}"], bias=EPS, scale=1.0 / D_MODEL)
                    P.op("dve", lambda e, o=rstd: e.reciprocal(out=o, in_=o), r=[f"rstd{b}"], w=[f"rstd{b}"])
                    for k in range(8):
                        tb = k % 2
                        stt(tmp_b[tb], xs[:, k, :], Apre[:, k:k + 1], rstd, ALU.mult, ALU.mult,
                            r=[f"xs{b}", f"rstd{b}", "derived"], w=[f"tmpA{tb}"])
                        act(hT[:, k, s * 512:(s + 1) * 512], tmp_b[tb], AF.Identity,
                            r=[f"tmpA{tb}", "derived"], w=["hT"], bias=shiftv[:, k:k + 1])
                P.barrier()
                if stop == 3:
                    raise _Stop()

                if debug and l == 0:
                    dma("sp", hT_d, hT, r=["hT"], w=["hT_d"])
                AR.reset(offA)
                wb_b = [AR.alloc([128, 8, 512], BF16) for _ in range(3)]
                st_b = [AR.alloc([128, T], F32) for _ in range(3)]
                vst_b = [AR.alloc([128, 4, 512], BF16) for _ in range(2)]
                ubuf = AR.alloc([128, T + 64], F32)
                uc_b = [AR.alloc([128, T], F32) for _ in range(2)]
                ucbf_b = [AR.alloc([128, T], BF16) for _ in range(2)]
                rr = AR.alloc([128, T], F32)
                ii = AR.alloc([128, T], F32)
                aa = AR.alloc([128, T], F32)
                t1 = AR.alloc([128, T], F32)
                t2 = AR.alloc([128, T], F32)
                xin = AR.alloc([128, T], F32)
                hl = AR.alloc([128, T], F32)
                pk3 = lru_sm[:, 0:16]
                hin = lru_sm[:, 16:24]
                rsum = lru_sm[:, 24:32]
                ut = lru_sm[:, 32:56]
                ut3 = ut.rearrange("p (k t) -> p k t", k=8)
                win_v = win_in[l].rearrange("(k p) c -> p k c", p=128)
                groups = [("u", 0, 5120), ("u", 1, 5632)]
                for g in range(3):
                    groups.append(("k", g, 1536 + g * 512))
                for g in range(3):
                    groups.append(("v", g, 3072 + g * 512))
                for g in range(3):
                    groups.append(("q", g, g * 512))
                groups.append(("ga", 0, 4608))
                groups.append(("gl", 0, 6144))
                groups.append(("gl", 1, 6656))
                for i in range(4):
                    groups.append(("mg", i, 7168 + i * 512))

                def utail_exchange():
                    dma("sp", pkg2.rearrange("p (k t) -> p k t", k=8), UT_s.rearrange("k p t -> p k t")[:, :, T - 3:T],
                        r=[f"UT{k}" for k in range(8)], w=["pkg2"])
                    allgather(pkg2, gath2, r=["pkg2"], w=["gath2"])
                    gather(ut, gath2, 12, r=["gath2", "idx"], w=["ut"])
                    ts("dve", ut, ut, flags[:, 0:1], None, ALU.mult, None, r=["ut", "flags"], w=["ut"])

                def halo_pack(ci):
                    if ci in (0, 1):
                        for h in (2 * ci, 2 * ci + 1):
                            dma("pool", pkg[h * 128:(h + 1) * 128, :], KT_s[8 + h], r=[f"KT{8 + h}"], w=[f"pkg{ci}"])
                    elif ci in (2, 3):
                        for h in (2 * (ci - 2), 2 * (ci - 2) + 1):
                            dma("pool", pkg[(4 + h) * 128:(5 + h) * 128, :].rearrange("p (t c) -> p t c", t=4),
                                Vs[2, 4 * h:4 * h + 4].rearrange("t p c -> p t c"), r=["Vs2"], w=[f"pkg{ci}"])
                    elif ci == 4:
                        for h in range(4):
                            dma("pool", pkg[8 * 128:9 * 128, h * 512:(h + 1) * 512].rearrange("p (r i) -> p r i", r=4),
                                KT_s[4 + h].rearrange("p (r i) -> p r i", r=4)[:, :, 384:512],
                                r=[f"KT{4 + h}"], w=[f"pkg{ci}"])
                            dma("pool", pkg[9 * 128:10 * 128, h * 128:(h + 1) * 128], KT_s[h][:, 1920:2048],
                                r=[f"KT{h}"], w=[f"pkg{ci}"])
                    else:
                        for r_ in range(4):
                            dma("pool", pkg[10 * 128:11 * 128, r_ * 512:(r_ + 1) * 512], Vs[1, r_ * 4 + 3],
                                r=["Vs1"], w=[f"pkg{ci}"])
                        dma("pool", pkg[11 * 128:12 * 128, 0:512], Vs[0, 15], r=["Vs0"], w=[f"pkg{ci}"])
                    allgather(pkg[ci * 256:(ci + 1) * 256, :], gaths[ci], r=[f"pkg{ci}"], w=[f"gath{ci}"])

                def lru_stage1(k):
                    b = k % 2
                    uc, ucbf = uc_b[b], ucbf_b[b]
                    cp("dve", ubuf[:, 61:64], ut3[:, k, :], r=["ut"], w=["ubuf"])
                    dma("pool", ubuf[:, 64:64 + T], UT_s[k], r=[f"UT{k}"], w=["ubuf"])
                    cw = lambda j: vecs[:, vo + V_CONVW + j * 8 + k: vo + V_CONVW + j * 8 + k + 1]
                    ts("dve", uc, ubuf[:, 64:64 + T], cw(0), vecs[:, vo + V_CONVB + k: vo + V_CONVB + k + 1],
                       ALU.mult, ALU.add, r=["ubuf", "vecs"], w=[f"uc{b}"])
                    for j in range(1, 4):
                        stt(uc, ubuf[:, 64 - j:64 - j + T], cw(j), uc, ALU.mult, ALU.add,
                            r=["ubuf", "vecs", f"uc{b}"], w=[f"uc{b}"])
                    cp("dve", ucbf, uc, r=[f"uc{b}"], w=[f"ucbf{b}"])

                def lru_stage2(k):
                    b = k % 2
                    uc, ucbf = uc_b[b], ucbf_b[b]
                    for gate in range(2):
                        wgt = wg_bf[:, ((l * 2 + gate) * 8 + k) * 128:((l * 2 + gate) * 8 + k + 1) * 128]
                        bcol = (V_BRG if gate == 0 else V_BIG) + k
                        dstt = rr if gate == 0 else ii
                        for hf in range(2):
                            for j in range(2):
                                s = hf * 2 + j
                                mm(psums[6 + j], wgt, ucbf[:, s * 512:(s + 1) * 512], True, True,
                                   r=[f"ucbf{b}", "wg"], w=[f"ps{6 + j}"])
                            act(dstt[:, hf * 1024:(hf + 1) * 1024].rearrange("p (a c) -> p a c", a=2),
                                psum_all[:, 6:8, :], AF.Sigmoid, r=["ps6", "ps7", "vecs"],
                                w=["rr" if gate == 0 else "ii"], bias=vecs[:, vo + bcol: vo + bcol + 1])
                    P.op("dve", lambda e, o=rsum[:, k:k + 1], i=rr: e.reduce_sum(out=o, in_=i, axis=mybir.AxisListType.X),
                         r=["rr"], w=["rsum"])
                    act(aa, rr, AF.Exp, r=["rr", "derived"], w=["aa"], scale=Cv[:, k:k + 1])
                    act(t1, rr, AF.Exp, r=["rr", "derived"], w=["t1"], scale=C2v[:, k:k + 1])
                    act(t1, t1, AF.Sqrt, r=["t1"], w=["t1"], scale=-1.0, bias=onep[:, 0:1])
                    tt("dve", t2, ii, uc, ALU.mult, r=["ii", f"uc{b}"], w=["t2"])
                    tt("dve", xin, t1, t2, ALU.mult, r=["t1", "t2"], w=["xin"])
                    P.op("dve", lambda e, o=hl, a=aa, x=xin: e.tensor_tensor_scan(
                        out=o, data0=a, data1=x, initial=0.0, op0=ALU.mult, op1=ALU.add), r=["aa", "xin"], w=["hl"])
                    cp("dve", pk3[:, k:k + 1], hl[:, T - 1:T], r=["hl"], w=["pk3"])
                    dma("pool", A_s[k], aa, r=["aa"], w=[f"A_s{k}"])
                    dma("pool", X_s[k], xin, r=["xin"], w=[f"X_s{k}"])

                S1_AT = {2 + (3 * c_) // 2: c_ for c_ in range(8)}
                S2_AT = {3 + (3 * c_) // 2: c_ for c_ in range(8)}

                def lru_exchange():
                    tt("dve", rsum, rsum, Cv, ALU.mult, r=["rsum", "derived"], w=["rsum"])
                    act(pk3[:, 8:16], rsum, AF.Exp, r=["rsum"], w=["pk3"])
                    dma("pool", pkg3, pk3, r=["pk3"], w=["pkg3"])
                    allgather(pkg3, gath3, r=["pkg3"], w=["gath3"])
                    g3 = g3sb[:, :].rearrange("p (s c) -> p s c", s=4)
                    dma("pool", g3, gath3.rearrange("(s p) c -> p s c", p=128), r=["gath3"], w=["g3"])
                    Fa = lru_sm[:, 56:64]
                    cp("dve", Fa, g3[:, 0, 0:8], r=["g3"], w=["Fa"])
                    ts("dve", hin, Fa, flags[:, 1:2], None, ALU.mult, None, r=["Fa", "flags"], w=["hin"])
                    for j in (1, 2):
                        tt("dve", Fa, Fa, g3[:, j, 8:16], ALU.mult, r=["Fa", "g3"], w=["Fa"])
                        tt("dve", Fa, Fa, g3[:, j, 0:8], ALU.add, r=["Fa", "g3"], w=["Fa"])
                        stt(hin, Fa, flags[:, 1 + j:2 + j], hin, ALU.mult, ALU.add, r=["Fa", "flags", "hin"], w=["hin"])

                stcount = 0
                pscount = 0
                vcount = 0
                def wload(gx):
                    if gx < len(groups):
                        dma("pool", wb_b[gx % 3], win_v[:, :, groups[gx][2]:groups[gx][2] + 512], r=[], w=[f"wb{gx % 3}"])
                wload(0)
                wload(1)
                for gidx, (kind, gi, c0) in enumerate(groups):
                    wbi = gidx % 3
                    wb = wb_b[wbi]
                    wload(gidx + 2)
                    if kind == "v":
                        D = GROUP_DIL[gi]
                        nb = 16 // D
                        for tq in range(4):
                            vb = vcount % 2
                            vcount += 1
                            vst = vst_b[vb]
                            for ti in range(4):
                                tidx = tq * 4 + ti
                                r_, n_ = tidx // nb, tidx % nb
                                pi = pscount % 6
                                pscount += 1
                                ps = psums[pi]
                                t0 = D * 128 * n_ + r_
                                if os.environ.get("V_NOSTRIDE"):
                                    D = 1
                                    t0 = 128 * tidx
                                for k in range(8):
                                    mm(ps[:, :], hT[:, k, t0:t0 + D * 127 + 1:D], wb[:, k, :], k == 0, k == 7,
                                       r=["hT", f"wb{wbi}"], w=[f"ps{pi}"])
                                act(vst[:, ti, :], ps[:, :], AF.Copy, r=[f"ps{pi}"], w=[f"vst{vb}"])
                            dma("sp", Vs[gi, tq * 4:(tq + 1) * 4].rearrange("t p c -> p t c"), vst,
                                r=[f"vst{vb}"], w=[f"Vs{gi}"])
                    else:
                        for cc in range(4):
                            sbi = stcount % 3
                            stcount += 1
                            if kind in ("q", "k", "ga", "gl", "mg"):
                                st = st_b[sbi][:, 0:T // 2].bitcast(BF16)
                            else:
                                st = st_b[sbi]
                            for half in range(2):
                                banks = ((pscount % 6), ((pscount + 1) % 6))
                                pscount += 2
                                merged = not (kind in ("q", "k") and GROUP_DIL[gi] != 1)
                                for k in range(8):
                                    for j in range(2):
                                        s = half * 2 + j
                                        mm(psums[banks[j]][:, :], wb[:, k, cc * 128:(cc + 1) * 128],
                                           hT[:, k, s * 512:(s + 1) * 512], k == 0, k == 7,
                                           r=["hT", f"wb{wbi}"], w=[f"ps{banks[j]}"])
                                if merged:
                                    fn_ = {"q": AF.Copy, "k": AF.Copy, "u": AF.Copy, "ga": AF.Silu, "gl": AF.Silu,
                                           "mg": AF.Sigmoid}[kind]
                                    act(st[:, half * 1024:(half + 1) * 1024].rearrange("p (a c) -> p a c", a=2),
                                        psum_all[:, banks[0]:banks[0] + 2, :], fn_,
                                        r=[f"ps{banks[0]}", f"ps{banks[1]}"], w=[f"st{sbi}"])
                                for j in range(2):
                                    if merged:
                                        break
                                    s = half * 2 + j
                                    pi = banks[j]
                                    ps = psums[pi]
                                    if kind in ("q", "k"):
                                        D = GROUP_DIL[gi]
                                        if D == 1:
                                            o_ap = st[:, s * 512:(s + 1) * 512]
                                            i_ap = ps[:, :]
                                        else:
                                            w_ = 512 // D
                                            o_ap = st.rearrange("p (r i) -> p r i", r=D)[:, :, s * w_:(s + 1) * w_]
                                            i_ap = ps[:, :].rearrange("p (i r) -> p r i", r=D)
                                        act(o_ap, i_ap, AF.Copy, r=[f"ps{pi}"], w=[f"st{sbi}"])
                                    elif kind == "u":
                                        act(st[:, s * 512:(s + 1) * 512], ps[:, :], AF.Copy, r=[f"ps{pi}"], w=[f"st{sbi}"])
                                    elif kind in ("ga", "gl"):
                                        act(st[:, s * 512:(s + 1) * 512], ps[:, :], AF.Silu, r=[f"ps{pi}"], w=[f"st{sbi}"])
                                    else:
                                        act(st[:, s * 512:(s + 1) * 512], ps[:, :], AF.Sigmoid, r=[f"ps{pi}"],
                                            w=[f"st{sbi}"])
                            if kind == "q":
                                dst, key = QT_s[gi * 4 + cc], f"QT{gi * 4 + cc}"
                            elif kind == "k":
                                dst, key = KT_s[gi * 4 + cc], f"KT{gi * 4 + cc}"
                            elif kind == "ga":
                                dst, key = GA_s[cc], f"GA{cc}"
                            elif kind == "u":
                                dst, key = UT_s[gi * 4 + cc], f"UT{gi * 4 + cc}"
                            elif kind == "gl":
                                dst, key = GL_s[gi * 4 + cc], f"GL{gi * 4 + cc}"
                            else:
                                dst, key = MG_s[gi * 4 + cc], f"MG{gi * 4 + cc}"
                            dma("sp", dst, st, r=[f"st{sbi}"], w=[key])
                    nl_ = os.environ.get("NO_LRU", "")
                    if nl_ == "1":
                        continue
                    if gidx == 1:
                        utail_exchange()
                    if 7 <= gidx < 13 and nl_ != "halo":
                        halo_pack(gidx - 7)
                    if nl_ == "stages":
                        continue
                    if gidx in S1_AT:
                        lru_stage1(S1_AT[gidx])
                    if gidx in S2_AT:
                        lru_stage2(S2_AT[gidx])
                    if gidx == 14:
                        lru_exchange()
                P.barrier()
                if stop == 6:
                    raise _Stop()

                TOFF = 48 * 1024
                AR.reset(0)
                AT = AR.alloc([128, 4, T], BF16)
                BT = AR.alloc([128, 8, T], BF16)
                assert AR.off <= TOFF
                AR.reset(TOFF)
                hp = [AR.alloc([128, 2048], BF16) for _ in range(4)]
                hv3 = [AR.alloc([128, 2048], BF16) for _ in range(4)]
                k3h = AR.alloc([128, 2048], BF16)
                q_b = [AR.alloc([128, T], BF16) for _ in range(2)]
                k_b = [AR.alloc([128, T], BF16) for _ in range(2)]
                v_b = [AR.alloc([128, 16, 128], BF16) for _ in range(2)]
                acc = AR.alloc([128, 2, T], F32)
                pT_b = [AR.alloc([128, 256], BF16) for _ in range(2)]
                pm_b = [AR.alloc([128, 256], BF16) for _ in range(3)]
                gab = AR.alloc([128, T], BF16)
                rden = AR.alloc([128, T], F32)
                assert AR.off <= ARENA_BYTES - 40 * 1024, AR.off
                AR.reset(ARENA_BYTES - 40 * 1024)
                wpa = AR.alloc([128, 4, 1024], BF16)
                wpb = AR.alloc([128, 8, 1024], BF16)
                wo = AR.alloc([128, 8, 1024], BF16)
                dma("pool", wpa, wpa_in[l].rearrange("(k p) c -> p k c", p=128), r=[], w=["wpa"])
                dma("pool", wpb, wpb_in[l].rearrange("(k p) c -> p k c", p=128), r=[], w=["wpb"])
                dma("pool", wo, wo_in[l].rearrange("(k p) c -> p k c", p=128), r=[], w=["wo"])
                for i in range(4):
                    gather(hp[i], gaths[(8 + i) // 2], 8 + i, r=[f"gath{(8 + i) // 2}", "idx"], w=[f"hp{i}"])
                    gather(hv3[i], gaths[(4 + i) // 2], 4 + i, r=[f"gath{(4 + i) // 2}", "idx"], w=[f"hv3{i}"])
                blk = 0
                hg = 0
                SC = 128.0 ** -0.5
                for h in range(4):
                    gather(k3h, gaths[h // 2], h, r=[f"gath{h // 2}", "idx"], w=["k3h"])
                    dma("sp", gab, GA_s[h], r=[f"GA{h}"], w=["gab"])
                    for g in range(3):
                        D = GROUP_DIL[g]
                        nb = 16 // D
                        Lr = T // D
                        b = hg % 2
                        hg += 1
                        qT, kT, vv = q_b[b], k_b[b], v_b[b]
                        dma("sp", qT, QT_s[g * 4 + h], r=[f"QT{g * 4 + h}"], w=[f"qT{b}"])
                        dma("sp", kT, KT_s[g * 4 + h], r=[f"KT{g * 4 + h}"], w=[f"kT{b}"])
                        dma("sp", vv, Vs[g].rearrange("t p c -> p t c")[:, :, h * 128:(h + 1) * 128],
                            r=[f"Vs{g}"], w=[f"vv{b}"])
                        blocks = []
                        for r_ in range(D):
                            for n_ in range(nb):
                                pos = r_ * Lr + n_ * 128
                                tidx = r_ * nb + n_
                                if n_ > 0:
                                    kprev, vprev, msk = kT[:, pos - 128:pos], vv[:, tidx - 1, :], maskPC
                                    rk, rv = [f"kT{b}"], [f"vv{b}"]
                                else:
                                    msk = maskF
                                    if g == 0:
                                        kprev, vprev = hp[1][:, h * 128:(h + 1) * 128], hp[3][:, h * 128:(h + 1) * 128]
                                        rk, rv = ["hp1"], ["hp3"]
                                    elif g == 1:
                                        kprev = hp[0][:, h * 512 + r_ * 128: h * 512 + (r_ + 1) * 128]
                                        vprev = hp[2][:, r_ * 512 + h * 128: r_ * 512 + (h + 1) * 128]
                                        rk, rv = ["hp0"], ["hp2"]
                                    else:
                                        kprev = k3h[:, r_ * 128:(r_ + 1) * 128]
                                        vprev = hv3[r_ // 4][:, (r_ % 4) * 512 + h * 128:(r_ % 4) * 512 + (h + 1) * 128]
                                        rk, rv = ["k3h"], [f"hv3{r_ // 4}"]
                                blocks.append((pos, tidx, kprev, vprev, msk, rk, rv, D * 128 * n_ + r_, blk))
                                blk += 1

                        def emit_S(bd):
                            pos, tidx, kprev, vprev, msk, rk, rv, t0, bn = bd
                            pb = bn % 3
                            pss, pm = psums[pb], pm_b[pb]
                            qblk = qT[:, pos:pos + 128]
                            mm(pss[:, 0:256], ident_bf, msk, True, False, r=["cbf"], w=[f"ps{pb}"])
                            mm(pss[:, 0:128], kprev, qblk, False, True, r=rk + [f"qT{b}"], w=[f"ps{pb}"])
                            mm(pss[:, 128:256], kT[:, pos:pos + 128], qblk, False, True,
                               r=[f"kT{b}", f"qT{b}"], w=[f"ps{pb}"])
                            act(pm, pss[:, 0:256], AF.Exp, r=[f"ps{pb}"], w=[f"pm{pb}"], scale=SC)

                        def emit_PV(bd):
                            pos, tidx, kprev, vprev, msk, rk, rv, t0, bn = bd
                            pb = bn % 3
                            po = 3 + bn % 2
                            pso, pm = psums[po], pm_b[pb]
                            mm(pso[:, 0:128], vprev, pm[:, 0:128], True, False, r=rv + [f"pm{pb}"], w=[f"ps{po}"])
                            mm(pso[:, 0:128], vv[:, tidx, :], pm[:, 128:256], False, True,
                               r=[f"vv{b}", f"pm{pb}"], w=[f"ps{po}"])
                            mm(pso[:, 128:256], ones_bf, pm[:, 0:128], True, False, r=["cbf", f"pm{pb}"], w=[f"ps{po}"])
                            mm(pso[:, 128:256], ones_bf, pm[:, 128:256], False, True,
                               r=["cbf", f"pm{pb}"], w=[f"ps{po}"])
                            av = acc[:, :, t0:t0 + D * 127 + 1:D]
                            pv = pso[:, 0:256].rearrange("p (a q) -> p a q", a=2)
                            if g == 0:
                                cp("dve", av, pv, r=[f"ps{po}"], w=["acc"])
                            else:
                                tt("dve", av, av, pv, ALU.add, r=[f"ps{po}", "acc"], w=["acc"])

                        emit_S(blocks[0])
                        emit_S(blocks[1])
                        for bi_ in range(len(blocks)):
                            if bi_ + 2 < len(blocks):
                                emit_S(blocks[bi_ + 2])
                            emit_PV(blocks[bi_])
                    P.op("dve", lambda e, o=rden, i=acc[:, 1, :]: e.reciprocal(out=o, in_=i), r=["acc"], w=["rden"])
                    tt("dve", rden, rden, acc[:, 0, :], ALU.mult, r=["acc", "rden"], w=["rden"])
                    tt("pool", AT[:, h, :], rden, gab, ALU.mult, r=["rden", "gab"], w=["AT"])
                P.barrier()
                if stop == 7:
                    raise _Stop()

                AR.reset(TOFF)
                a_b = [AR.alloc([128, T], F32) for _ in range(2)]
                x_b = [AR.alloc([128, T], F32) for _ in range(2)]
                gl_b = [AR.alloc([128, T], BF16) for _ in range(2)]
                h_b = [AR.alloc([128, T], F32) for _ in range(2)]
                for k in range(8):
                    b = k % 2
                    dma("sp", a_b[b], A_s[k], r=[f"A_s{k}"], w=[f"a2{b}"])
                    dma("sp", x_b[b], X_s[k], r=[f"X_s{k}"], w=[f"x2{b}"])
                    dma("sp", gl_b[b], GL_s[k], r=[f"GL{k}"], w=[f"gl{b}"])
                    P.op("dve", lambda e, o=h_b[b], a=a_b[b], x=x_b[b], hi=hin[:, k:k + 1]: e.tensor_tensor_scan(
                        out=o, data0=a, data1=x, initial=hi, op0=ALU.mult, op1=ALU.add),
                        r=[f"a2{b}", f"x2{b}", "hin"], w=[f"h2{b}"])
                    tt("pool", BT[:, k, :], h_b[b], gl_b[b], ALU.mult, r=[f"h2{b}", f"gl{b}"], w=["BT"])
                P.barrier()
                if stop == 8:
                    raise _Stop()

                if debug and l == 0:
                    dma("sp", AT_d, AT, r=["AT"], w=["AT_d"])
                    dma("sp", BT_d, BT, r=["BT"], w=["BT_d"])
                    dma("sp", sm_d[:, 0:64], lru_sm[:, :], r=["hin", "pk3", "ut"], w=["sm_d"])
                    dma("sp", sm_d[:, 64:104], derived[:, 0:40], r=["derived"], w=["sm_d"])
                    dma("sp", sm_d[:, 104:128], modv_all[:, 0:24], r=["modv"], w=["sm_d"])
                    P.barrier()
                AR.reset(TOFF)
                mga = AR.alloc([128, 16, 512], BF16)
                xs = AR.alloc([128, 8, 512], F32)
                ta = AR.alloc([128, 512], F32)
                tb_ = AR.alloc([128, 512], F32)
                zT = AR.alloc([128, 8, 512], BF16)
                ysq = AR.alloc([128, 8, 512], BF16)
                yT = AR.alloc([128, 8, 512], F32)
                rstd2 = AR.alloc([128, 512], F32)
                tmpf = AR.alloc([128, 512], F32)
                xn = AR.alloc([128, 8, 512], F32)
                otile = AR.alloc([128, 1024], F32)
                assert AR.off <= ARENA_BYTES - 40 * 1024, AR.off
                if stop == 8.1:
                    P.barrier()
                    raise _Stop()
                for s in range(4):
                    sl = slice(s * 512, (s + 1) * 512)
                    dma("sp", mga, MG_s.rearrange("o p t -> p o t")[:, :, sl], r=[f"MG{o}" for o in range(16)], w=["mga"])
                    dma("sp", xs, xT_s.rearrange("k p t -> p k t")[:, :, sl], r=[f"xT_s{s}"], w=["xsF"])
                    for o in range(8):
                        pa, pb_ = psums[(2 * o) % 4], psums[(2 * o + 1) % 4]
                        ka, kb = f"ps{(2 * o) % 4}", f"ps{(2 * o + 1) % 4}"
                        for h in range(4):
                            mm(pa[:, :], wpa[:, h, o * 128:(o + 1) * 128], AT[:, h, sl], h == 0, h == 3,
                               r=["wpa", "AT"], w=[ka])
                        for k in range(8):
                            mm(pb_[:, :], wpb[:, k, o * 128:(o + 1) * 128], BT[:, k, sl], k == 0, k == 7,
                               r=["wpb", "BT"], w=[kb])
                        tt("dve", ta, pa[:, :], mga[:, o, :], ALU.mult, r=[ka, "mga"], w=["ta"])
                        tt("dve", tb_, pb_[:, :], mga[:, 8 + o, :], ALU.mult, r=[kb, "mga"], w=["tb"])
                        tt("dve", zT[:, o, :], ta, tb_, ALU.add, r=["ta", "tb"], w=["zT"])
                    if stop == 8.2:
                        P.barrier()
                        raise _Stop()
                    if stop == 8.25:
                        P.barrier()
                        raise _Stop()
                    for o2 in range(8):
                        pi = 4 + o2 % 2
                        py = psums[pi]
                        for o in range(8):
                            mm(py[:, :], wo[:, o, o2 * 128:(o2 + 1) * 128], zT[:, o, :], o == 0, o == 7,
                               r=["wo", "zT"], w=[f"ps{pi}"])
                        if stop == 8.26:
                            P.barrier()
                            raise _Stop()
                        act(yT[:, o2, :], py[:, :], AF.Copy, r=[f"ps{pi}"], w=["yT"])
                        if stop == 8.27:
                            P.barrier()
                            raise _Stop()
                        act(ysq[:, o2, :], yT[:, o2, :], AF.Square, r=["yT"], w=["ysq"])
                    if stop == 8.3:
                        P.barrier()
                        raise _Stop()
                    pss_ = psums[6]
                    for o2 in range(8):
                        mm(pss_[:, :], ones_bf, ysq[:, o2, :], o2 == 0, o2 == 7, r=["ysq", "cbf"], w=["ps6"])
                    act(rstd2, pss_[:, :], AF.Sqrt, r=["ps6"], w=["rstd2"], bias=EPS, scale=1.0 / D_MODEL)
                    P.op("dve", lambda e, o=rstd2: e.reciprocal(out=o, in_=o), r=["rstd2"], w=["rstd2"])
                    for o2 in range(8):
                        stt(tmpf, yT[:, o2, :], Gp[:, o2:o2 + 1], rstd2, ALU.mult, ALU.mult,
                            r=["yT", "rstd2", "derived"], w=["tmpf"])
                        tt("dve", xn[:, o2, :], tmpf, xs[:, o2, :], ALU.add, r=["tmpf", "xsF"], w=["xn"])
                    if stop == 8.4:
                        P.barrier()
                        raise _Stop()
                    if not last:
                        dma("sp", xT_s.rearrange("k p t -> p k t")[:, :, sl], xn, r=["xn"], w=[f"xT_s{s}"])
                    else:
                        for t in range(4):
                            for half in range(2):
                                pi = 6 + half
                                pt = psums[pi]
                                for q in range(4):
                                    o2 = half * 4 + q
                                    P.op("pe", lambda e, o=pt[:, q * 128:(q + 1) * 128], i=xn[:, o2, t * 128:(t + 1) * 128]:
                                         e.transpose(o, i, ident[:, :]), r=["xn", "ident"], w=[f"ps{pi}"])
                                if half == 0:
                                    cp("dve", otile[:, 0:512], pt[:, :], r=["ps6"], w=["otile"])
                                else:
                                    act(otile[:, 512:1024], pt[:, :], AF.Copy, r=["ps7"], w=["otile"])
                            tok0 = s * 512 + t * 128
                            dma("sp", out[tok0:tok0 + 128, :], otile, r=["otile"], w=[f"out{tok0}"])
                P.barrier()
                if stop == 9:
                    raise _Stop()

        except _Stop:
            if debug:
                dma("sp", sm_d[:, 0:64], lru_sm[:, :], r=["hin", "pk3", "ut"], w=["sm_d"])
                dma("sp", sm_d[:, 64:104], derived[:, 0:40], r=["derived"], w=["sm_d"])
                dma("sp", sm_d[:, 104:128], modv_all[:, 0:24], r=["modv"], w=["sm_d"])
                P.barrier()
        P.emit(nc, block, sems, dma_sems)
    return nc


_NC_CACHE = {}


def _host_layouts(inputs):
    f = np.float32
    x = np.asarray(inputs["x"], f)
    c = np.asarray(inputs["c"], f)

    def colvec(v):
        v = np.asarray(v, f)
        return np.ascontiguousarray(v.reshape(-1, 128).T)

    vecs = np.zeros((128, DEPTH * NV), f)
    for l in range(DEPTH):
        o = l * NV
        vecs[:, o + V_BMOD:o + V_BMOD + 24] = colvec(inputs["b_mod"][l])
        vecs[:, o + V_GPRE:o + V_GPRE + 8] = colvec(inputs["g_pre"][l])
        for j in range(4):
            vecs[:, o + V_CONVW + j * 8:o + V_CONVW + (j + 1) * 8] = colvec(inputs["conv_w"][l][j])
        vecs[:, o + V_CONVB:o + V_CONVB + 8] = colvec(inputs["conv_b"][l])
        vecs[:, o + V_BRG:o + V_BRG + 8] = colvec(inputs["b_rg"][l])
        vecs[:, o + V_BIG:o + V_BIG + 8] = colvec(inputs["b_ig"][l])
        vecs[:, o + V_LAM:o + V_LAM + 8] = colvec(inputs["lru_lambda"][l])
        vecs[:, o + V_GPOST:o + V_GPOST + 8] = colvec(inputs["g_post"][l])
    wg = np.zeros((128, DEPTH, 2, 8, 128), f)
    for l in range(DEPTH):
        for gi, name in enumerate(("w_rg", "w_ig")):
            w = np.asarray(inputs[name][l], f)
            for k in range(8):
                wg[0:64, l, gi, k, 0:64] = w[2 * k]
                wg[64:128, l, gi, k, 64:128] = w[2 * k + 1]
    wg = np.ascontiguousarray(wg.reshape(128, -1))
    kk = np.arange(128)[:, None]
    qq = np.arange(128)[None, :]
    mprev = (kk >= qq).astype(f)
    mcur = (kk <= qq).astype(f)
    common = {
        "vecs": vecs, "wg": wg,
        "w_mod": np.ascontiguousarray(inputs["w_mod"], f), "w_in": np.ascontiguousarray(inputs["w_in"], f),
        "w_pa": np.ascontiguousarray(inputs["w_pa"], f), "w_pb": np.ascontiguousarray(inputs["w_pb"], f),
        "w_o": np.ascontiguousarray(inputs["w_o"], f),
    }
    in_maps = []
    for core in range(8):
        b, j = core // 4, core % 4
        hp = 1.0 if j > 0 else 0.0
        NEG = np.float32(-30000.0)
        consts = np.concatenate([np.eye(128, dtype=f), np.ones((128, 128), f), np.eye(128, dtype=f),
                                 (1 - mprev) * NEG, (1 - mcur) * NEG, (1 - mprev * hp) * NEG, (1 - mcur) * NEG], axis=1)
        slot = max(j - 1, 0)
        idx = np.zeros((128, 16), np.int32)
        for pce in range(12):
            idx[:, pce] = slot * 256 + (pce % 2) * 128 + np.arange(128)
        idx[:, 12] = slot * 128 + np.arange(128)
        flags = np.zeros((128, 8), f)
        flags[:, 0] = hp
        if j > 0:
            flags[:, j] = 1.0
        m = dict(common)
        m["x"] = np.ascontiguousarray(x[b, j * T:(j + 1) * T, :])
        m["cT"] = np.ascontiguousarray(c[b].reshape(8, 128).T)
        m["consts"] = np.ascontiguousarray(consts)
        m["idx"] = idx
        m["flags"] = flags
        in_maps.append(m)
    return in_maps


def kernel(**inputs):
    if "nc" not in _NC_CACHE:
        _NC_CACHE["nc"] = build_nc()
    nc = _NC_CACHE["nc"]
    in_maps = _host_layouts(inputs)
    res = run_bass_kernel_spmd(nc, in_maps, core_ids=list(range(8)))
    outf = np.zeros((2, 4 * T, D_MODEL), np.float32)
    for core in range(8):
        b, j = core // 4, core % 4
        outf[b, j * T:(j + 1) * T, :] = res.results[core]["out"]
    return outf
```

```python
import contextlib
import os
import numpy as np
import concourse.bass as bass
import concourse.mybir as mybir
from concourse.bass_utils import run_bass_kernel_spmd

F32 = mybir.dt.float32
BF16 = mybir.dt.bfloat16
I32 = mybir.dt.int32
ALU = mybir.AluOpType
AF = mybir.ActivationFunctionType

D_MODEL = 1024
T = 2048
DEPTH = 2
NV = 104
EPS = 1e-6
ARENA_BYTES = 186 * 1024
GROUP_DIL = (1, 4, 16)

V_BMOD, V_GPRE, V_CONVW, V_CONVB, V_BRG, V_BIG, V_LAM, V_GPOST = 0, 24, 32, 64, 72, 80, 88, 96


class Op:
    __slots__ = ("eng", "fn", "deps", "dma", "idx", "signal", "sigval", "sem", "semval", "inc")

    def __init__(self, eng, fn, dma):
        self.eng, self.fn, self.dma = eng, fn, dma
        self.deps = []
        self.signal = False
        self.sigval = None
        self.sem = None
        self.semval = None
        self.inc = 16


ENGS = ("pe", "act", "dve", "pool", "sp")


class Prog:
    def __init__(self):
        self.ops = {e: [] for e in ENGS}
        self.lastw = {}
        self.readers = {}
        self.dmas_since_barrier = []

    def op(self, eng, fn, r=(), w=(), dma=False, inc=16):
        o = Op(eng, fn, dma)
        o.inc = inc
        deps = []
        for k in r:
            lw = self.lastw.get(k)
            if lw is not None:
                deps.append(lw)
        for k in w:
            lw = self.lastw.get(k)
            if lw is not None:
                deps.append(lw)
            deps.extend(self.readers.get(k, ()))
        seen = set()
        for d in deps:
            if id(d) in seen or d is o:
                continue
            seen.add(id(d))
            if (not d.dma) and (not dma) and d.eng == eng and eng == "pe":
                continue
            o.deps.append(d)
            d.signal = True
        for k in r:
            lst = self.readers.setdefault(k, [])
            if not dma:
                lst[:] = [x_ for x_ in lst if x_.dma or x_.eng != eng]
            lst.append(o)
        for k in w:
            self.lastw[k] = o
            self.readers[k] = []
        self.ops[eng].append(o)
        if dma:
            o.signal = True
            self.dmas_since_barrier.append(o)
        return o

    def barrier(self):
        lasts = []
        for e in ENGS:
            for o in reversed(self.ops[e]):
                if o.fn is not None and not o.dma:
                    lasts.append(o)
                    break
        deps = lasts + self.dmas_since_barrier
        self.dmas_since_barrier = []
        for e in ENGS:
            b = Op(e, None, False)
            for d in deps:
                if (not d.dma) and d.eng == e:
                    continue
                b.deps.append(d)
                d.signal = True
            self.ops[e].append(b)

    def emit(self, nc, block, sems, dma_sems):
        tot = {}
        ncc = 0
        for e in ENGS:
            cnt = 0
            nd = 0
            for o in self.ops[e]:
                if o.fn is None:
                    continue
                if o.dma:
                    if o.inc == 16:
                        pool = dma_sems[e]
                        o.sem = pool[nd % len(pool)]
                        nd += 1
                    else:
                        pool = dma_sems["cc"]
                        o.sem = pool[ncc % len(pool)]
                        ncc += 1
                    prev = tot.get(id(o.sem), 0)
                    o.sigval = prev
                    o.semval = prev + o.inc
                    tot[id(o.sem)] = o.semval
                elif o.signal:
                    cnt += 1
                    o.sigval = cnt
        prog = self

        def run_engine(e, engobj):
            known = {x: 0 for x in ENGS}
            known_dma = {}
            nd = 0
            for o in prog.ops[e]:
                for d in o.deps:
                    if d.dma:
                        key = id(d.sem)
                        if known_dma.get(key, 0) >= d.semval:
                            continue
                        engobj.wait_ge(d.sem, d.semval)
                        known_dma[key] = d.semval
                    else:
                        if known[d.eng] >= d.sigval:
                            continue
                        engobj.wait_ge(sems[d.eng], d.sigval)
                        known[d.eng] = d.sigval
                if o.fn is None:
                    continue
                if o.dma:
                    pool = dma_sems[e]
                    prev = o.sigval
                    key = id(o.sem)
                    if prev > 0 and known_dma.get(key, 0) < prev:
                        engobj.wait_ge(o.sem, prev)
                        known_dma[key] = prev
                    nd += 1
                    ins = o.fn(engobj)
                    ins.then_inc(o.sem, o.inc)
                else:
                    ins = o.fn(engobj)
                    if o.signal:
                        ins.then_inc(sems[e], 1)

        @block.tensor
        def _(eng):
            run_engine("pe", eng)

        @block.scalar
        def _(eng):
            run_engine("act", eng)

        @block.vector
        def _(eng):
            run_engine("dve", eng)

        @block.gpsimd
        def _(eng):
            run_engine("pool", eng)

        @block.sync
        def _(eng):
            run_engine("sp", eng)


class Arena:
    def __init__(self, ap_bf16):
        self.ap = ap_bf16
        self.off = 0

    def reset(self, off):
        self.off = off

    def alloc(self, shape, dtype):
        esz = 4 if dtype in (F32, I32) else 2
        n = int(np.prod(shape[1:]))
        nbytes = n * esz
        self.off = (self.off + 63) // 64 * 64
        start = self.off
        self.off += nbytes
        assert self.off <= ARENA_BYTES, f"arena overflow {self.off}"
        v = self.ap[:, start // 2:(start + nbytes) // 2]
        if esz == 4:
            v = v.bitcast(dtype)
        if len(shape) == 3:
            v = v.rearrange("p (a b) -> p a b", a=shape[1])
        return v


class _Stop(Exception):
    pass


def build_nc(stop=None, debug=False):
    nc = bass.Bass("TRN2", target_bir_lowering=False)
    P = Prog()

    def din(name, shape, dt):
        return nc.dram_tensor(name, shape, dt, kind="ExternalInput").ap()

    x_in = din("x", [T, D_MODEL], F32)
    cT_in = din("cT", [128, 8], F32)
    vecs_in = din("vecs", [128, DEPTH * NV], F32)
    wmod_in = din("w_mod", [DEPTH, D_MODEL, 3 * D_MODEL], F32)
    win_in = din("w_in", [DEPTH, D_MODEL, 9216], F32)
    wg_in = din("wg", [128, DEPTH * 16 * 128], F32)
    wpa_in = din("w_pa", [DEPTH, 512, D_MODEL], F32)
    wpb_in = din("w_pb", [DEPTH, D_MODEL, D_MODEL], F32)
    wo_in = din("w_o", [DEPTH, D_MODEL, D_MODEL], F32)
    consts_in = din("consts", [128, 128 + 256 + 256 + 256], F32)
    idx_in = din("idx", [128, 16], I32)
    flags_in = din("flags", [128, 8], F32)
    out = nc.dram_tensor("out", [T, D_MODEL], F32, kind="ExternalOutput").ap()

    def dscr(name, shape, dt):
        if debug and not name.startswith(("pkg", "gath")):
            return nc.dram_tensor(name, shape, dt, kind="ExternalOutput").ap()
        return nc.dram_tensor(name, shape, dt).ap()

    xT_s = dscr("xT_s", [8, 128, T], F32)
    QT_s = dscr("QT_s", [12, 128, T], BF16)
    KT_s = dscr("KT_s", [12, 128, T], BF16)
    Vs = dscr("Vs", [3, 16, 128, 512], BF16)
    GA_s = dscr("GA_s", [4, 128, T], BF16)
    UT_s = dscr("UT_s", [8, 128, T], F32)
    GL_s = dscr("GL_s", [8, 128, T], BF16)
    MG_s = dscr("MG_s", [16, 128, T], BF16)
    A_s = dscr("A_s", [8, 128, T], F32)
    X_s = dscr("X_s", [8, 128, T], F32)
    pkg = dscr("pkg", [12 * 128, 2048], BF16)
    gaths = [dscr(f"gathkv{i}", [4 * 256, 2048], BF16) for i in range(6)]
    pkg2 = dscr("pkg2", [128, 24], F32)
    gath2 = dscr("gath2", [4 * 128, 24], F32)
    pkg3 = dscr("pkg3", [128, 16], F32)
    gath3 = dscr("gath3", [4 * 128, 16], F32)
    RG = [[0, 1, 2, 3], [4, 5, 6, 7]]
    if debug:
        AT_d = dscr("AT_d", [128, 4, T], BF16)
        BT_d = dscr("BT_d", [128, 8, T], BF16)
        hT_d = dscr("hT_d", [128, 8, T], BF16)
        sm_d = dscr("sm_d", [128, 64 + 40 + 24], F32)

    es = contextlib.ExitStack()
    with es:
        def sb(name, shape, dt):
            return es.enter_context(nc.sbuf_tensor(name, shape, dt))

        arena_t = sb("arena", [128, ARENA_BYTES // 2], BF16)
        AR = Arena(arena_t[:, :])
        ident = sb("ident_sb", [128, 128], F32)
        consts_bf = sb("consts_bf", [128, 256 + 512], BF16)
        ones_bf = consts_bf[:, 0:128]
        ident_bf = consts_bf[:, 128:256]
        maskPC = consts_bf[:, 256:512]
        maskF = consts_bf[:, 512:768]
        vecs = sb("vecs_sb", [128, DEPTH * NV], F32)
        cT = sb("cTs", [128, 8], F32)
        siluc = sb("siluc", [128, 8], BF16)
        idx = sb("idxs", [128, 16], I32)
        flags = sb("flagss", [128, 8], F32)
        modv_all = sb("modv", [128, 48], F32)
        derived = sb("derived", [128, 96], F32)
        wg_bf = sb("wg_bf", [128, DEPTH * 16 * 128], BF16)
        lru_sm = sb("lru_sm", [128, 64], F32)
        g3sb = sb("g3sb", [128, 4 * 16], F32)
        onep = sb("onep", [128, 1], F32)
        psum_all = es.enter_context(nc.psum_tensor("psall", [128, 8, 512], F32))
        psums = [psum_all[:, i, :] for i in range(8)]
        sems = {e: es.enter_context(nc.semaphore(f"sem_{e}")) for e in ENGS}
        dma_sems = {e: [es.enter_context(nc.semaphore(f"dsem_{e}{i}")) for i in range(12)]
                    for e in ("sp", "pool", "act")}
        dma_sems["cc"] = [es.enter_context(nc.semaphore(f"ccsem{i}")) for i in range(4)]
        dma_sems["pe"] = dma_sems["sp"]
        dma_sems["dve"] = dma_sems["sp"]
        block = es.enter_context(nc.Block())

        def dma(q, out_ap, in_ap, r, w):
            return P.op(q, lambda e, o=out_ap, i=in_ap: e.dma_start(out=o, in_=i), r=r, w=w, dma=True)

        def mm(out_ap, lhsT, rhs, start, stop, r, w):
            return P.op("pe", lambda e, o=out_ap, l=lhsT, rr=rhs, s=start, t=stop:
                        e.matmul(o, lhsT=l, rhs=rr, start=s, stop=t), r=r, w=w)

        def act(out_ap, in_ap, func, r, w, bias=None, scale=None):
            kw = {}
            if bias is not None:
                kw["bias"] = bias
            if scale is not None:
                kw["scale"] = scale
            return P.op("act", lambda e, o=out_ap, i=in_ap, f=func, kw=kw:
                        e.activation(out=o, in_=i, func=f, **kw), r=r, w=w)

        def tt(eng, out_ap, in0, in1, op, r, w):
            return P.op(eng, lambda e, o=out_ap, a=in0, b=in1, p=op:
                        e.tensor_tensor(out=o, in0=a, in1=b, op=p), r=r, w=w)

        def ts(eng, out_ap, in0, s1, s2, op0, op1, r, w):
            if op1 is None:
                return P.op(eng, lambda e, o=out_ap, a=in0, x=s1, p0=op0:
                            e.tensor_scalar(out=o, in0=a, scalar1=x, scalar2=None, op0=p0), r=r, w=w)
            return P.op(eng, lambda e, o=out_ap, a=in0, x=s1, y=s2, p0=op0, p1=op1:
                        e.tensor_scalar(out=o, in0=a, scalar1=x, scalar2=y, op0=p0, op1=p1), r=r, w=w)

        def stt(out_ap, in0, scalar, in1, op0, op1, r, w):
            return P.op("dve", lambda e, o=out_ap, a=in0, s=scalar, b=in1, p0=op0, p1=op1:
                        e.scalar_tensor_tensor(out=o, in0=a, scalar=s, in1=b, op0=p0, op1=p1), r=r, w=w)

        def cp(eng, out_ap, in_ap, r, w):
            return P.op(eng, lambda e, o=out_ap, i=in_ap: e.tensor_copy(out=o, in_=i), r=r, w=w)

        def gather(out_ap, src, col, r, w):
            return P.op("pool", lambda e, o=out_ap, s=src, c=col: e.indirect_dma_start(
                out=o, out_offset=None, in_=s,
                in_offset=bass.IndirectOffsetOnAxis(ap=idx[:, c:c + 1], axis=0)), r=r, w=w, dma=True)

        def allgather(src, dst, r, w):
            return P.op("pool", lambda e, s=src, d=dst: e.collective_compute(
                "AllGather", ALU.bypass, replica_groups=RG, ins=[s], outs=[d]), r=r, w=w, dma=True, inc=1)

        try:
            dma("sp", ident[:, :], consts_in[:, 0:128], r=[], w=["ident"])
            dma("pool", consts_bf[:, :], consts_in[:, 128:896], r=[], w=["cbf"])
            dma("sp", vecs[:, :], vecs_in, r=[], w=["vecs"])
            dma("sp", cT[:, :], cT_in, r=[], w=["cT"])
            dma("sp", idx[:, :], idx_in, r=[], w=["idx"])
            dma("sp", flags[:, :], flags_in, r=[], w=["flags"])
            dma("pool", wg_bf[:, :], wg_in, r=[], w=["wg"])
            act(siluc[:, :], cT[:, :], AF.Silu, r=["cT"], w=["siluc"])
            P.op("dve", lambda e: e.memset(onep[:, :], 1.0000001), r=[], w=["onep"])

            AR.reset(128 * 1024)
            zpad = AR.alloc([128, 1536], BF16)
            P.op("dve", lambda e, o=zpad: e.memset(o, 0.0), r=[], w=["zpad"])
            dma("sp", pkg[9 * 128:10 * 128, 512:2048], zpad, r=["zpad"], w=["pkgpad"])
            dma("sp", pkg[11 * 128:12 * 128, 512:2048], zpad, r=["zpad"], w=["pkgpad"])
            AR.reset(0)
            xt_b = [AR.alloc([128, 4, 1024], F32) for _ in range(2)]
            xst_b = [AR.alloc([128, 8, 512], F32) for _ in range(2)]
            for s in range(4):
                xt = xt_b[s % 2]
                xst = xst_b[s % 2]
                dma("sp", xt, x_in[s * 512:(s + 1) * 512, :].rearrange("(t p) f -> p t f", p=128),
                    r=[], w=[f"xt{s % 2}"])
                for k in range(8):
                    ps = psums[k % 4]
                    for t in range(4):
                        P.op("pe", lambda e, o=ps[:, t * 128:(t + 1) * 128], i=xt[:, t, k * 128:(k + 1) * 128]:
                             e.transpose(o, i, ident[:, :]),
                             r=[f"xt{s % 2}", "ident"], w=[f"ps{k % 4}"])
                    if k % 2 == 0:
                        cp("dve", xst[:, k, :], ps[:, :], r=[f"ps{k % 4}"], w=[f"xst{s % 2}"])
                    else:
                        act(xst[:, k, :], ps[:, :], AF.Copy, r=[f"ps{k % 4}"], w=[f"xst{s % 2}"])
                dma("sp", xT_s.rearrange("k p t -> p k t")[:, :, s * 512:(s + 1) * 512], xst,
                    r=[f"xst{s % 2}"], w=[f"xT_s{s}"])
            for l in range(DEPTH):
                vo = l * NV
                dv_ = derived[:, l * 48:(l + 1) * 48]
                Apre = dv_[:, 0:8]
                shiftv = dv_[:, 8:16]
                Gp = dv_[:, 16:24]
                Cv = dv_[:, 24:32]
                dtmp = dv_[:, 32:40]
                C2v = dv_[:, 40:48]
                modv = modv_all[:, l * 24:(l + 1) * 24]
                AR.reset(64 * 1024 + l * 32 * 1024)
                wm_b = [AR.alloc([128, 8, 1024], BF16) for _ in range(2)]
                for m in range(3):
                    wm = wm_b[m % 2]
                    dma("pool", wm, wmod_in[l].rearrange("(k p) c -> p k c", p=128)[:, :, m * 1024:(m + 1) * 1024],
                        r=[], w=[f"wm{l}_{m % 2}"])
                    ps = psums[4 + m % 2]
                    for oc in range(8):
                        for k in range(8):
                            mm(ps[:, oc:oc + 1], wm[:, k, oc * 128:(oc + 1) * 128], siluc[:, k:k + 1],
                               k == 0, k == 7, r=[f"wm{l}_{m % 2}", "siluc"], w=[f"ps{4 + m % 2}"])
                    tt("dve", modv[:, m * 8:(m + 1) * 8], ps[:, 0:8], vecs[:, vo + V_BMOD + m * 8: vo + V_BMOD + (m + 1) * 8],
                       ALU.add, r=[f"ps{4 + m % 2}", "vecs"], w=["modv"])
                cp("dve", shiftv, modv[:, 0:8], r=["modv"], w=["derived"])
                ts("dve", dtmp, modv[:, 8:16], 1.0, None, ALU.add, None, r=["modv"], w=["derived"])
                tt("dve", Apre, dtmp, vecs[:, vo + V_GPRE: vo + V_GPRE + 8], ALU.mult, r=["vecs", "derived"], w=["derived"])
                tt("dve", Gp, modv[:, 16:24], vecs[:, vo + V_GPOST: vo + V_GPOST + 8], ALU.mult,
                   r=["vecs", "modv"], w=["derived"])
                zz = lru_sm[:, 56:64]
                act(zz, vecs[:, vo + V_LAM: vo + V_LAM + 8], AF.Exp, r=["vecs"], w=["zz"], scale=-1.0)
                ts("dve", dtmp, zz, -0.25, 1.0 / 3.0, ALU.mult, ALU.add, r=["zz"], w=["derived"])
                tt("dve", dtmp, dtmp, zz, ALU.mult, r=["zz", "derived"], w=["derived"])
                ts("dve", dtmp, dtmp, -1.0, 0.5, ALU.mult, ALU.add, r=["derived"], w=["derived"])
                tt("dve", dtmp, dtmp, zz, ALU.mult, r=["zz", "derived"], w=["derived"])
                ts("dve", dtmp, dtmp, -1.0, 1.0, ALU.mult, ALU.add, r=["derived"], w=["derived"])
                tt("dve", dtmp, dtmp, zz, ALU.mult, r=["zz", "derived"], w=["derived"])
                ts("dve", Cv, dtmp, -8.0, None, ALU.mult, None, r=["derived"], w=["derived"])
                ts("dve", C2v, dtmp, -16.0, None, ALU.mult, None, r=["derived"], w=["derived"])
            P.barrier()
            if stop == 1:
                raise _Stop()

            for l in range(DEPTH):
                vo = l * NV
                last = (l == DEPTH - 1)
                dv_ = derived[:, l * 48:(l + 1) * 48]
                Apre = dv_[:, 0:8]
                shiftv = dv_[:, 8:16]
                Gp = dv_[:, 16:24]
                Cv = dv_[:, 24:32]
                dtmp = dv_[:, 32:40]
                C2v = dv_[:, 40:48]
                modv = modv_all[:, l * 24:(l + 1) * 24]

                AR.reset(0)
                hT = AR.alloc([128, 8, T], BF16)
                offA = AR.off
                xs_b = [AR.alloc([128, 8, 512], F32) for _ in range(2)]
                sq_b = [AR.alloc([128, 8, 512], BF16) for _ in range(2)]
                rstd_b = [AR.alloc([128, 512], F32) for _ in range(2)]
                tmp_b = [AR.alloc([128, 512], F32) for _ in range(2)]
                for s in range(4):
                    b = s % 2
                    xs, sq, rstd = xs_b[b], sq_b[b], rstd_b[b]
                    dma("sp", xs, xT_s.rearrange("k p t -> p k t")[:, :, s * 512:(s + 1) * 512],
                        r=[f"xT_s{s}"], w=[f"xs{b}"])
                    act(sq, xs, AF.Square, r=[f"xs{b}"], w=[f"sq{b}"])
                    ps = psums[b]
                    for k in range(8):
                        mm(ps[:, :], ones_bf, sq[:, k, :], k == 0, k == 7, r=[f"sq{b}", "cbf"], w=[f"ps{b}"])
                    act(rstd, ps[:, :], AF.Sqrt, r=[f"ps{b}"], w=[f"rstd{b}"], bias=EPS, scale=1.0 / D_MODEL)
                    P.op("dve", lambda e, o=rstd: e.reciprocal(out=o, in_=o), r=[f"rstd{b}"], w=[f"rstd{b}"])
                    for k in range(8):
                        tb = k % 2
                        stt(tmp_b[tb], xs[:, k, :], Apre[:, k:k + 1], rstd, ALU.mult, ALU.mult,
                            r=[f"xs{b}", f"rstd{b}", "derived"], w=[f"tmpA{tb}"])
                        act(hT[:, k, s * 512:(s + 1) * 512], tmp_b[tb], AF.Identity,
                            r=[f"tmpA{tb}", "derived"], w=["hT"], bias=shiftv[:, k:k + 1])
                P.barrier()
                if stop == 3:
                    raise _Stop()

                if debug and l == 0:
                    dma("sp", hT_d, hT, r=["hT"], w=["hT_d"])
                AR.reset(offA)
                wb_b = [AR.alloc([128, 8, 512], BF16) for _ in range(3)]
                st_b = [AR.alloc([128, T], F32) for _ in range(3)]
                vst_b = [AR.alloc([128, 4, 512], BF16) for _ in range(2)]
                ubuf = AR.alloc([128, T + 64], F32)
                uc_b = [AR.alloc([128, T], F32) for _ in range(2)]
                ucbf_b = [AR.alloc([128, T], BF16) for _ in range(2)]
                rr = AR.alloc([128, T], F32)
                ii = AR.alloc([128, T], F32)
                aa = AR.alloc([128, T], F32)
                t1 = AR.alloc([128, T], F32)
                t2 = AR.alloc([128, T], F32)
                xin = AR.alloc([128, T], F32)
                hl = AR.alloc([128, T], F32)
                pk3 = lru_sm[:, 0:16]
                hin = lru_sm[:, 16:24]
                rsum = lru_sm[:, 24:32]
                ut = lru_sm[:, 32:56]
                ut3 = ut.rearrange("p (k t) -> p k t", k=8)
                win_v = win_in[l].rearrange("(k p) c -> p k c", p=128)
                groups = [("u", 0, 5120), ("u", 1, 5632)]
                for g in range(3):
                    groups.append(("k", g, 1536 + g * 512))
                for g in range(3):
                    groups.append(("v", g, 3072 + g * 512))
                for g in range(3):
                    groups.append(("q", g, g * 512))
                groups.append(("ga", 0, 4608))
                groups.append(("gl", 0, 6144))
                groups.append(("gl", 1, 6656))
                for i in range(4):
                    groups.append(("mg", i, 7168 + i * 512))

                def utail_exchange():
                    dma("sp", pkg2.rearrange("p (k t) -> p k t", k=8), UT_s.rearrange("k p t -> p k t")[:, :, T - 3:T],
                        r=[f"UT{k}" for k in range(8)], w=["pkg2"])
                    allgather(pkg2, gath2, r=["pkg2"], w=["gath2"])
                    gather(ut, gath2, 12, r=["gath2", "idx"], w=["ut"])
                    ts("dve", ut, ut, flags[:, 0:1], None, ALU.mult, None, r=["ut", "flags"], w=["ut"])

                def halo_pack(ci):
                    if ci in (0, 1):
                        for h in (2 * ci, 2 * ci + 1):
                            dma("pool", pkg[h * 128:(h + 1) * 128, :], KT_s[8 + h], r=[f"KT{8 + h}"], w=[f"pkg{ci}"])
                    elif ci in (2, 3):
                        for h in (2 * (ci - 2), 2 * (ci - 2) + 1):
                            dma("pool", pkg[(4 + h) * 128:(5 + h) * 128, :].rearrange("p (t c) -> p t c", t=4),
                                Vs[2, 4 * h:4 * h + 4].rearrange("t p c -> p t c"), r=["Vs2"], w=[f"pkg{ci}"])
                    elif ci == 4:
                        for h in range(4):
                            dma("pool", pkg[8 * 128:9 * 128, h * 512:(h + 1) * 512].rearrange("p (r i) -> p r i", r=4),
                                KT_s[4 + h].rearrange("p (r i) -> p r i", r=4)[:, :, 384:512],
                                r=[f"KT{4 + h}"], w=[f"pkg{ci}"])
                            dma("pool", pkg[9 * 128:10 * 128, h * 128:(h + 1) * 128], KT_s[h][:, 1920:2048],
                                r=[f"KT{h}"], w=[f"pkg{ci}"])
                    else:
                        for r_ in range(4):
                            dma("pool", pkg[10 * 128:11 * 128, r_ * 512:(r_ + 1) * 512], Vs[1, r_ * 4 + 3],
                                r=["Vs1"], w=[f"pkg{ci}"])
                        dma("pool", pkg[11 * 128:12 * 128, 0:512], Vs[0, 15], r=["Vs0"], w=[f"pkg{ci}"])
                    allgather(pkg[ci * 256:(ci + 1) * 256, :], gaths[ci], r=[f"pkg{ci}"], w=[f"gath{ci}"])

                def lru_stage1(k):
                    b = k % 2
                    uc, ucbf = uc_b[b], ucbf_b[b]
                    cp("dve", ubuf[:, 61:64], ut3[:, k, :], r=["ut"], w=["ubuf"])
                    dma("pool", ubuf[:, 64:64 + T], UT_s[k], r=[f"UT{k}"], w=["ubuf"])
                    cw = lambda j: vecs[:, vo + V_CONVW + j * 8 + k: vo + V_CONVW + j * 8 + k + 1]
                    ts("dve", uc, ubuf[:, 64:64 + T], cw(0), vecs[:, vo + V_CONVB + k: vo + V_CONVB + k + 1],
                       ALU.mult, ALU.add, r=["ubuf", "vecs"], w=[f"uc{b}"])
                    for j in range(1, 4):
                        stt(uc, ubuf[:, 64 - j:64 - j + T], cw(j), uc, ALU.mult, ALU.add,
                            r=["ubuf", "vecs", f"uc{b}"], w=[f"uc{b}"])
                    cp("dve", ucbf, uc, r=[f"uc{b}"], w=[f"ucbf{b}"])

                def lru_stage2(k):
                    b = k % 2
                    uc, ucbf = uc_b[b], ucbf_b[b]
                    for gate in range(2):
                        wgt = wg_bf[:, ((l * 2 + gate) * 8 + k) * 128:((l * 2 + gate) * 8 + k + 1) * 128]
                        bcol = (V_BRG if gate == 0 else V_BIG) + k
                        dstt = rr if gate == 0 else ii
                        for hf in range(2):
                            for j in range(2):
                                s = hf * 2 + j
                                mm(psums[6 + j], wgt, ucbf[:, s * 512:(s + 1) * 512], True, True,
                                   r=[f"ucbf{b}", "wg"], w=[f"ps{6 + j}"])
                            act(dstt[:, hf * 1024:(hf + 1) * 1024].rearrange("p (a c) -> p a c", a=2),
                                psum_all[:, 6:8, :], AF.Sigmoid, r=["ps6", "ps7", "vecs"],
                                w=["rr" if gate == 0 else "ii"], bias=vecs[:, vo + bcol: vo + bcol + 1])
                    P.op("dve", lambda e, o=rsum[:, k:k + 1], i=rr: e.reduce_sum(out=o, in_=i, axis=mybir.AxisListType.X),
                         r=["rr"], w=["rsum"])
                    act(aa, rr, AF.Exp, r=["rr", "derived"], w=["aa"], scale=Cv[:, k:k + 1])
                    act(t1, rr, AF.Exp, r=["rr", "derived"], w=["t1"], scale=C2v[:, k:k + 1])
                    act(t1, t1, AF.Sqrt, r=["t1"], w=["t1"], scale=-1.0, bias=onep[:, 0:1])
                    tt("dve", t2, ii, uc, ALU.mult, r=["ii", f"uc{b}"], w=["t2"])
                    tt("dve", xin, t1, t2, ALU.mult, r=["t1", "t2"], w=["xin"])
                    P.op("dve", lambda e, o=hl, a=aa, x=xin: e.tensor_tensor_scan(
                        out=o, data0=a, data1=x, initial=0.0, op0=ALU.mult, op1=ALU.add), r=["aa", "xin"], w=["hl"])
                    cp("dve", pk3[:, k:k + 1], hl[:, T - 1:T], r=["hl"], w=["pk3"])
                    dma("pool", A_s[k], aa, r=["aa"], w=[f"A_s{k}"])
                    dma("pool", X_s[k], xin, r=["xin"], w=[f"X_s{k}"])

                S1_AT = {2 + (3 * c_) // 2: c_ for c_ in range(8)}
                S2_AT = {3 + (3 * c_) // 2: c_ for c_ in range(8)}

                def lru_exchange():
                    tt("dve", rsum, rsum, Cv, ALU.mult, r=["rsum", "derived"], w=["rsum"])
                    act(pk3[:, 8:16], rsum, AF.Exp, r=["rsum"], w=["pk3"])
                    dma("pool", pkg3, pk3, r=["pk3"], w=["pkg3"])
                    allgather(pkg3, gath3, r=["pkg3"], w=["gath3"])
                    g3 = g3sb[:, :].rearrange("p (s c) -> p s c", s=4)
                    dma("pool", g3, gath3.rearrange("(s p) c -> p s c", p=128), r=["gath3"], w=["g3"])
                    Fa = lru_sm[:, 56:64]
                    cp("dve", Fa, g3[:, 0, 0:8], r=["g3"], w=["Fa"])
                    ts("dve", hin, Fa, flags[:, 1:2], None, ALU.mult, None, r=["Fa", "flags"], w=["hin"])
                    for j in (1, 2):
                        tt("dve", Fa, Fa, g3[:, j, 8:16], ALU.mult, r=["Fa", "g3"], w=["Fa"])
                        tt("dve", Fa, Fa, g3[:, j, 0:8], ALU.add, r=["Fa", "g3"], w=["Fa"])
                        stt(hin, Fa, flags[:, 1 + j:2 + j], hin, ALU.mult, ALU.add, r=["Fa", "flags", "hin"], w=["hin"])

                stcount = 0
                pscount = 0
                vcount = 0
                def wload(gx):
                    if gx < len(groups):
                        dma("pool", wb_b[gx % 3], win_v[:, :, groups[gx][2]:groups[gx][2] + 512], r=[], w=[f"wb{gx % 3}"])
                wload(0)
                wload(1)
                for gidx, (kind, gi, c0) in enumerate(groups):
                    wbi = gidx % 3
                    wb = wb_b[wbi]
                    wload(gidx + 2)
                    if kind == "v":
                        D = GROUP_DIL[gi]
                        nb = 16 // D
                        for tq in range(4):
                            vb = vcount % 2
                            vcount += 1
                            vst = vst_b[vb]
                            for ti in range(4):
                                tidx = tq * 4 + ti
                                r_, n_ = tidx // nb, tidx % nb
                                pi = pscount % 6
                                pscount += 1
                                ps = psums[pi]
                                t0 = D * 128 * n_ + r_
                                if os.environ.get("V_NOSTRIDE"):
                                    D = 1
                                    t0 = 128 * tidx
                                for k in range(8):
                                    mm(ps[:, :], hT[:, k, t0:t0 + D * 127 + 1:D], wb[:, k, :], k == 0, k == 7,
                                       r=["hT", f"wb{wbi}"], w=[f"ps{pi}"])
                                act(vst[:, ti, :], ps[:, :], AF.Copy, r=[f"ps{pi}"], w=[f"vst{vb}"])
                            dma("sp", Vs[gi, tq * 4:(tq + 1) * 4].rearrange("t p c -> p t c"), vst,
                                r=[f"vst{vb}"], w=[f"Vs{gi}"])
                    else:
                        for cc in range(4):
                            sbi = stcount % 3
                            stcount += 1
                            if kind in ("q", "k", "ga", "gl", "mg"):
                                st = st_b[sbi][:, 0:T // 2].bitcast(BF16)
                            else:
                                st = st_b[sbi]
                            for half in range(2):
                                banks = ((pscount % 6), ((pscount + 1) % 6))
                                pscount += 2
                                merged = not (kind in ("q", "k") and GROUP_DIL[gi] != 1)
                                for k in range(8):
                                    for j in range(2):
                                        s = half * 2 + j
                                        mm(psums[banks[j]][:, :], wb[:, k, cc * 128:(cc + 1) * 128],
                                           hT[:, k, s * 512:(s + 1) * 512], k == 0, k == 7,
                                           r=["hT", f"wb{wbi}"], w=[f"ps{banks[j]}"])
                                if merged:
                                    fn_ = {"q": AF.Copy, "k": AF.Copy, "u": AF.Copy, "ga": AF.Silu, "gl": AF.Silu,
                                           "mg": AF.Sigmoid}[kind]
                                    act(st[:, half * 1024:(half + 1) * 1024].rearrange("p (a c) -> p a c", a=2),
                                        psum_all[:, banks[0]:banks[0] + 2, :], fn_,
                                        r=[f"ps{banks[0]}", f"ps{banks[1]}"], w=[f"st{sbi}"])
                                for j in range(2):
                                    if merged:
                                        break
                                    s = half * 2 + j
                                    pi = banks[j]
                                    ps = psums[pi]
                                    if kind in ("q", "k"):
                                        D = GROUP_DIL[gi]
                                        if D == 1:
                                            o_ap = st[:, s * 512:(s + 1) * 512]
                                            i_ap = ps[:, :]
                                        else:
                                            w_ = 512 // D
                                            o_ap = st.rearrange("p (r i) -> p r i", r=D)[:, :, s * w_:(s + 1) * w_]
                                            i_ap = ps[:, :].rearrange("p (i r) -> p r i", r=D)
                                        act(o_ap, i_ap, AF.Copy, r=[f"ps{pi}"], w=[f"st{sbi}"])
                                    elif kind == "u":
                                        act(st[:, s * 512:(s + 1) * 512], ps[:, :], AF.Copy, r=[f"ps{pi}"], w=[f"st{sbi}"])
                                    elif kind in ("ga", "gl"):
                                        act(st[:, s * 512:(s + 1) * 512], ps[:, :], AF.Silu, r=[f"ps{pi}"], w=[f"st{sbi}"])
                                    else:
                                        act(st[:, s * 512:(s + 1) * 512], ps[:, :], AF.Sigmoid, r=[f"ps{pi}"],
                                            w=[f"st{sbi}"])
                            if kind == "q":
                                dst, key = QT_s[gi * 4 + cc], f"QT{gi * 4 + cc}"
                            elif kind == "k":
                                dst, key = KT_s[gi * 4 + cc], f"KT{gi * 4 + cc}"
                            elif kind == "ga":
                                dst, key = GA_s[cc], f"GA{cc}"
                            elif kind == "u":
                                dst, key = UT_s[gi * 4 + cc], f"UT{gi * 4 + cc}"
                            elif kind == "gl":
                                dst, key = GL_s[gi * 4 + cc], f"GL{gi * 4 + cc}"
                            else:
                                dst, key = MG_s[gi * 4 + cc], f"MG{gi * 4 + cc}"
                            dma("sp", dst, st, r=[f"st{sbi}"], w=[key])
                    nl_ = os.environ.get("NO_LRU", "")
                    if nl_ == "1":
                        continue
                    if gidx == 1:
                        utail_exchange()
                    if 7 <= gidx < 13 and nl_ != "halo":
                        halo_pack(gidx - 7)
                    if nl_ == "stages":
                        continue
                    if gidx in S1_AT:
                        lru_stage1(S1_AT[gidx])
                    if gidx in S2_AT:
                        lru_stage2(S2_AT[gidx])
                    if gidx == 14:
                        lru_exchange()
                P.barrier()
                if stop == 6:
                    raise _Stop()

                TOFF = 48 * 1024
                AR.reset(0)
                AT = AR.alloc([128, 4, T], BF16)
                BT = AR.alloc([128, 8, T], BF16)
                assert AR.off <= TOFF
                AR.reset(TOFF)
                hp = [AR.alloc([128, 2048], BF16) for _ in range(4)]
                hv3 = [AR.alloc([128, 2048], BF16) for _ in range(4)]
                k3h = AR.alloc([128, 2048], BF16)
                q_b = [AR.alloc([128, T], BF16) for _ in range(2)]
                k_b = [AR.alloc([128, T], BF16) for _ in range(2)]
                v_b = [AR.alloc([128, 16, 128], BF16) for _ in range(2)]
                acc = AR.alloc([128, 2, T], F32)
                pT_b = [AR.alloc([128, 256], BF16) for _ in range(2)]
                pm_b = [AR.alloc([128, 256], BF16) for _ in range(3)]
                gab = AR.alloc([128, T], BF16)
                rden = AR.alloc([128, T], F32)
                assert AR.off <= ARENA_BYTES - 40 * 1024, AR.off
                AR.reset(ARENA_BYTES - 40 * 1024)
                wpa = AR.alloc([128, 4, 1024], BF16)
                wpb = AR.alloc([128, 8, 1024], BF16)
                wo = AR.alloc([128, 8, 1024], BF16)
                dma("pool", wpa, wpa_in[l].rearrange("(k p) c -> p k c", p=128), r=[], w=["wpa"])
                dma("pool", wpb, wpb_in[l].rearrange("(k p) c -> p k c", p=128), r=[], w=["wpb"])
                dma("pool", wo, wo_in[l].rearrange("(k p) c -> p k c", p=128), r=[], w=["wo"])
                for i in range(4):
                    gather(hp[i], gaths[(8 + i) // 2], 8 + i, r=[f"gath{(8 + i) // 2}", "idx"], w=[f"hp{i}"])
                    gather(hv3[i], gaths[(4 + i) // 2], 4 + i, r=[f"gath{(4 + i) // 2}", "idx"], w=[f"hv3{i}"])
                blk = 0
                hg = 0
                SC = 128.0 ** -0.5
                for h in range(4):
                    gather(k3h, gaths[h // 2], h, r=[f"gath{h // 2}", "idx"], w=["k3h"])
                    dma("sp", gab, GA_s[h], r=[f"GA{h}"], w=["gab"])
                    for g in range(3):
                        D = GROUP_DIL[g]
                        nb = 16 // D
                        Lr = T // D
                        b = hg % 2
                        hg += 1
                        qT, kT, vv = q_b[b], k_b[b], v_b[b]
                        dma("sp", qT, QT_s[g * 4 + h], r=[f"QT{g * 4 + h}"], w=[f"qT{b}"])
                        dma("sp", kT, KT_s[g * 4 + h], r=[f"KT{g * 4 + h}"], w=[f"kT{b}"])
                        dma("sp", vv, Vs[g].rearrange("t p c -> p t c")[:, :, h * 128:(h + 1) * 128],
                            r=[f"Vs{g}"], w=[f"vv{b}"])
                        blocks = []
                        for r_ in range(D):
                            for n_ in range(nb):
                                pos = r_ * Lr + n_ * 128
                                tidx = r_ * nb + n_
                                if n_ > 0:
                                    kprev, vprev, msk = kT[:, pos - 128:pos], vv[:, tidx - 1, :], maskPC
                                    rk, rv = [f"kT{b}"], [f"vv{b}"]
                                else:
                                    msk = maskF
                                    if g == 0:
                                        kprev, vprev = hp[1][:, h * 128:(h + 1) * 128], hp[3][:, h * 128:(h + 1) * 128]
                                        rk, rv = ["hp1"], ["hp3"]
                                    elif g == 1:
                                        kprev = hp[0][:, h * 512 + r_ * 128: h * 512 + (r_ + 1) * 128]
                                        vprev = hp[2][:, r_ * 512 + h * 128: r_ * 512 + (h + 1) * 128]
                                        rk, rv = ["hp0"], ["hp2"]
                                    else:
                                        kprev = k3h[:, r_ * 128:(r_ + 1) * 128]
                                        vprev = hv3[r_ // 4][:, (r_ % 4) * 512 + h * 128:(r_ % 4) * 512 + (h + 1) * 128]
                                        rk, rv = ["k3h"], [f"hv3{r_ // 4}"]
                                blocks.append((pos, tidx, kprev, vprev, msk, rk, rv, D * 128 * n_ + r_, blk))
                                blk += 1

                        def emit_S(bd):
                            pos, tidx, kprev, vprev, msk, rk, rv, t0, bn = bd
                            pb = bn % 3
                            pss, pm = psums[pb], pm_b[pb]
                            qblk = qT[:, pos:pos + 128]
                            mm(pss[:, 0:128], ident_bf, msk[:, 0:128], True, False, r=["cbf"], w=[f"ps{pb}"])
                            mm(pss[:, 0:128], kprev, qblk, False, True, r=rk + [f"qT{b}"], w=[f"ps{pb}"])
                            mm(pss[:, 128:256], ident_bf, msk[:, 128:256], True, False, r=["cbf"], w=[f"ps{pb}"])
                            mm(pss[:, 128:256], kT[:, pos:pos + 128], qblk, False, True,
                               r=[f"kT{b}", f"qT{b}"], w=[f"ps{pb}"])
                            act(pm, pss[:, 0:256], AF.Exp, r=[f"ps{pb}"], w=[f"pm{pb}"], scale=SC)

                        def emit_PV(bd):
                            pos, tidx, kprev, vprev, msk, rk, rv, t0, bn = bd
                            pb = bn % 3
                            po = 3 + bn % 2
                            pso, pm = psums[po], pm_b[pb]
                            mm(pso[:, 0:128], vprev, pm[:, 0:128], True, False, r=rv + [f"pm{pb}"], w=[f"ps{po}"])
                            mm(pso[:, 0:128], vv[:, tidx, :], pm[:, 128:256], False, True,
                               r=[f"vv{b}", f"pm{pb}"], w=[f"ps{po}"])
                            mm(pso[:, 128:256], ones_bf, pm[:, 0:128], True, False, r=["cbf", f"pm{pb}"], w=[f"ps{po}"])
                            mm(pso[:, 128:256], ones_bf, pm[:, 128:256], False, True,
                               r=["cbf", f"pm{pb}"], w=[f"ps{po}"])
                            av = acc[:, :, t0:t0 + D * 127 + 1:D]
                            pv = pso[:, 0:256].rearrange("p (a q) -> p a q", a=2)
                            if g == 0:
                                cp("dve", av, pv, r=[f"ps{po}"], w=["acc"])
                            else:
                                tt("dve", av, av, pv, ALU.add, r=[f"ps{po}", "acc"], w=["acc"])

                        emit_S(blocks[0])
                        emit_S(blocks[1])
                        for bi_ in range(len(blocks)):
                            if bi_ + 2 < len(blocks):
                                emit_S(blocks[bi_ + 2])
                            emit_PV(blocks[bi_])
                    P.op("dve", lambda e, o=rden, i=acc[:, 1, :]: e.reciprocal(out=o, in_=i), r=["acc"], w=["rden"])
                    tt("dve", rden, rden, acc[:, 0, :], ALU.mult, r=["acc", "rden"], w=["rden"])
                    tt("pool", AT[:, h, :], rden, gab, ALU.mult, r=["rden", "gab"], w=["AT"])
                P.barrier()
                if stop == 7:
                    raise _Stop()

                AR.reset(TOFF)
                a_b = [AR.alloc([128, T], F32) for _ in range(2)]
                x_b = [AR.alloc([128, T], F32) for _ in range(2)]
                gl_b = [AR.alloc([128, T], BF16) for _ in range(2)]
                h_b = [AR.alloc([128, T], F32) for _ in range(2)]
                for k in range(8):
                    b = k % 2
                    dma("sp", a_b[b], A_s[k], r=[f"A_s{k}"], w=[f"a2{b}"])
                    dma("sp", x_b[b], X_s[k], r=[f"X_s{k}"], w=[f"x2{b}"])
                    dma("sp", gl_b[b], GL_s[k], r=[f"GL{k}"], w=[f"gl{b}"])
                    P.op("dve", lambda e, o=h_b[b], a=a_b[b], x=x_b[b], hi=hin[:, k:k + 1]: e.tensor_tensor_scan(
                        out=o, data0=a, data1=x, initial=hi, op0=ALU.mult, op1=ALU.add),
                        r=[f"a2{b}", f"x2{b}", "hin"], w=[f"h2{b}"])
                    tt("pool", BT[:, k, :], h_b[b], gl_b[b], ALU.mult, r=[f"h2{b}", f"gl{b}"], w=["BT"])
                P.barrier()
                if stop == 8:
                    raise _Stop()

                if debug and l == 0:
                    dma("sp", AT_d, AT, r=["AT"], w=["AT_d"])
                    dma("sp", BT_d, BT, r=["BT"], w=["BT_d"])
                    dma("sp", sm_d[:, 0:64], lru_sm[:, :], r=["hin", "pk3", "ut"], w=["sm_d"])
                    dma("sp", sm_d[:, 64:104], derived[:, 0:40], r=["derived"], w=["sm_d"])
                    dma("sp", sm_d[:, 104:128], modv_all[:, 0:24], r=["modv"], w=["sm_d"])
                    P.barrier()
                AR.reset(TOFF)
                mga = AR.alloc([128, 16, 512], BF16)
                xs = AR.alloc([128, 8, 512], F32)
                ta = AR.alloc([128, 512], F32)
                tb_ = AR.alloc([128, 512], F32)
                zT = AR.alloc([128, 8, 512], BF16)
                ysq = AR.alloc([128, 8, 512], BF16)
                yT = AR.alloc([128, 8, 512], F32)
                rstd2 = AR.alloc([128, 512], F32)
                tmpf = AR.alloc([128, 512], F32)
                xn = AR.alloc([128, 8, 512], F32)
                otile = AR.alloc([128, 1024], F32)
                assert AR.off <= ARENA_BYTES - 40 * 1024, AR.off
                if stop == 8.1:
                    P.barrier()
                    raise _Stop()
                for s in range(4):
                    sl = slice(s * 512, (s + 1) * 512)
                    dma("sp", mga, MG_s.rearrange("o p t -> p o t")[:, :, sl], r=[f"MG{o}" for o in range(16)], w=["mga"])
                    dma("sp", xs, xT_s.rearrange("k p t -> p k t")[:, :, sl], r=[f"xT_s{s}"], w=["xsF"])
                    for o in range(8):
                        pa, pb_ = psums[(2 * o) % 4], psums[(2 * o + 1) % 4]
                        ka, kb = f"ps{(2 * o) % 4}", f"ps{(2 * o + 1) % 4}"
                        for h in range(4):
                            mm(pa[:, :], wpa[:, h, o * 128:(o + 1) * 128], AT[:, h, sl], h == 0, h == 3,
                               r=["wpa", "AT"], w=[ka])
                        for k in range(8):
                            mm(pb_[:, :], wpb[:, k, o * 128:(o + 1) * 128], BT[:, k, sl], k == 0, k == 7,
                               r=["wpb", "BT"], w=[kb])
                        tt("dve", ta, pa[:, :], mga[:, o, :], ALU.mult, r=[ka, "mga"], w=["ta"])
                        tt("dve", tb_, pb_[:, :], mga[:, 8 + o, :], ALU.mult, r=[kb, "mga"], w=["tb"])
                        tt("dve", zT[:, o, :], ta, tb_, ALU.add, r=["ta", "tb"], w=["zT"])
                    if stop == 8.2:
                        P.barrier()
                        raise _Stop()
                    if stop == 8.25:
                        P.barrier()
                        raise _Stop()
                    for o2 in range(8):
                        pi = 4 + o2 % 2
                        py = psums[pi]
                        for o in range(8):
                            mm(py[:, :], wo[:, o, o2 * 128:(o2 + 1) * 128], zT[:, o, :], o == 0, o == 7,
                               r=["wo", "zT"], w=[f"ps{pi}"])
                        if stop == 8.26:
                            P.barrier()
                            raise _Stop()
                        act(yT[:, o2, :], py[:, :], AF.Copy, r=[f"ps{pi}"], w=["yT"])
                        if stop == 8.27:
                            P.barrier()
                            raise _Stop()
                        act(ysq[:, o2, :], yT[:, o2, :], AF.Square, r=["yT"], w=["ysq"])
                    if stop == 8.3:
                        P.barrier()
                        raise _Stop()
                    pss_ = psums[6]
                    for o2 in range(8):
                        mm(pss_[:, :], ones_bf, ysq[:, o2, :], o2 == 0, o2 == 7, r=["ysq", "cbf"], w=["ps6"])
                    act(rstd2, pss_[:, :], AF.Sqrt, r=["ps6"], w=["rstd2"], bias=EPS, scale=1.0 / D_MODEL)
                    P.op("dve", lambda e, o=rstd2: e.reciprocal(out=o, in_=o), r=["rstd2"], w=["rstd2"])
                    for o2 in range(8):
                        stt(tmpf, yT[:, o2, :], Gp[:, o2:o2 + 1], rstd2, ALU.mult, ALU.mult,
                            r=["yT", "rstd2", "derived"], w=["tmpf"])
                        tt("dve", xn[:, o2, :], tmpf, xs[:, o2, :], ALU.add, r=["tmpf", "xsF"], w=["xn"])
                    if stop == 8.4:
                        P.barrier()
                        raise _Stop()
                    if not last:
                        dma("sp", xT_s.rearrange("k p t -> p k t")[:, :, sl], xn, r=["xn"], w=[f"xT_s{s}"])
                    else:
                        for t in range(4):
                            for half in range(2):
                                pi = 6 + half
                                pt = psums[pi]
                                for q in range(4):
                                    o2 = half * 4 + q
                                    P.op("pe", lambda e, o=pt[:, q * 128:(q + 1) * 128], i=xn[:, o2, t * 128:(t + 1) * 128]:
                                         e.transpose(o, i, ident[:, :]), r=["xn", "ident"], w=[f"ps{pi}"])
                                if half == 0:
                                    cp("dve", otile[:, 0:512], pt[:, :], r=["ps6"], w=["otile"])
                                else:
                                    act(otile[:, 512:1024], pt[:, :], AF.Copy, r=["ps7"], w=["otile"])
                            tok0 = s * 512 + t * 128
                            dma("sp", out[tok0:tok0 + 128, :], otile, r=["otile"], w=[f"out{tok0}"])
                P.barrier()
                if stop == 9:
                    raise _Stop()

        except _Stop:
            if debug:
                dma("sp", sm_d[:, 0:64], lru_sm[:, :], r=["hin", "pk3", "ut"], w=["sm_d"])
                dma("sp", sm_d[:, 64:104], derived[:, 0:40], r=["derived"], w=["sm_d"])
                dma("sp", sm_d[:, 104:128], modv_all[:, 0:24], r=["modv"], w=["sm_d"])
                P.barrier()
        P.emit(nc, block, sems, dma_sems)
    return nc


_NC_CACHE = {}


def _host_layouts(inputs):
    f = np.float32
    x = np.asarray(inputs["x"], f)
    c = np.asarray(inputs["c"], f)

    def colvec(v):
        v = np.asarray(v, f)
        return np.ascontiguousarray(v.reshape(-1, 128).T)

    vecs = np.zeros((128, DEPTH * NV), f)
    for l in range(DEPTH):
        o = l * NV
        vecs[:, o + V_BMOD:o + V_BMOD + 24] = colvec(inputs["b_mod"][l])
        vecs[:, o + V_GPRE:o + V_GPRE + 8] = colvec(inputs["g_pre"][l])
        for j in range(4):
            vecs[:, o + V_CONVW + j * 8:o + V_CONVW + (j + 1) * 8] = colvec(inputs["conv_w"][l][j])
        vecs[:, o + V_CONVB:o + V_CONVB + 8] = colvec(inputs["conv_b"][l])
        vecs[:, o + V_BRG:o + V_BRG + 8] = colvec(inputs["b_rg"][l])
        vecs[:, o + V_BIG:o + V_BIG + 8] = colvec(inputs["b_ig"][l])
        vecs[:, o + V_LAM:o + V_LAM + 8] = colvec(inputs["lru_lambda"][l])
        vecs[:, o + V_GPOST:o + V_GPOST + 8] = colvec(inputs["g_post"][l])
    wg = np.zeros((128, DEPTH, 2, 8, 128), f)
    for l in range(DEPTH):
        for gi, name in enumerate(("w_rg", "w_ig")):
            w = np.asarray(inputs[name][l], f)
            for k in range(8):
                wg[0:64, l, gi, k, 0:64] = w[2 * k]
                wg[64:128, l, gi, k, 64:128] = w[2 * k + 1]
    wg = np.ascontiguousarray(wg.reshape(128, -1))
    kk = np.arange(128)[:, None]
    qq = np.arange(128)[None, :]
    mprev = (kk >= qq).astype(f)
    mcur = (kk <= qq).astype(f)
    common = {
        "vecs": vecs, "wg": wg,
        "w_mod": np.ascontiguousarray(inputs["w_mod"], f), "w_in": np.ascontiguousarray(inputs["w_in"], f),
        "w_pa": np.ascontiguousarray(inputs["w_pa"], f), "w_pb": np.ascontiguousarray(inputs["w_pb"], f),
        "w_o": np.ascontiguousarray(inputs["w_o"], f),
    }
    in_maps = []
    for core in range(8):
        b, j = core // 4, core % 4
        hp = 1.0 if j > 0 else 0.0
        NEG = np.float32(-30000.0)
        consts = np.concatenate([np.eye(128, dtype=f), np.ones((128, 128), f), np.eye(128, dtype=f),
                                 (1 - mprev) * NEG, (1 - mcur) * NEG, (1 - mprev * hp) * NEG, (1 - mcur) * NEG], axis=1)
        slot = max(j - 1, 0)
        idx = np.zeros((128, 16), np.int32)
        for pce in range(12):
            idx[:, pce] = slot * 256 + (pce % 2) * 128 + np.arange(128)
        idx[:, 12] = slot * 128 + np.arange(128)
        flags = np.zeros((128, 8), f)
        flags[:, 0] = hp
        if j > 0:
            flags[:, j] = 1.0
        m = dict(common)
        m["x"] = np.ascontiguousarray(x[b, j * T:(j + 1) * T, :])
        m["cT"] = np.ascontiguousarray(c[b].reshape(8, 128).T)
        m["consts"] = np.ascontiguousarray(consts)
        m["idx"] = idx
        m["flags"] = flags
        in_maps.append(m)
    return in_maps


def kernel(**inputs):
    if "nc" not in _NC_CACHE:
        _NC_CACHE["nc"] = build_nc()
    nc = _NC_CACHE["nc"]
    in_maps = _host_layouts(inputs)
    res = run_bass_kernel_spmd(nc, in_maps, core_ids=list(range(8)))
    outf = np.zeros((2, 4 * T, D_MODEL), np.float32)
    for core in range(8):
        b, j = core // 4, core % 4
        outf[b, j * T:(j + 1) * T, :] = res.results[core]["out"]
    return outf
```
